# Optimizing a Trainium2 kernel written in Bass

```python
import jax, jax.numpy as jnp
from jax import lax
import numpy as np

D_MODEL = 2048
BATCH = 2
SEQ = 4096
DEPTH = 1

D_MIX = D_MODEL
RWKV_W = D_MIX // 2
HGRN_W = D_MIX - RWKV_W
RWKV_HEAD = 64
RWKV_HEADS = RWKV_W // RWKV_HEAD
HGRN_EXPAND = 128
HGRN_HEADS = HGRN_W // HGRN_EXPAND
HGRN_DV = HGRN_W // HGRN_HEADS
DECAY_LORA = 96
A_LORA = 96
CHUNK = 64
NORM_EPS = 1e-6
LNX_EPS = 64e-5

N_RWKV_COLS = 4 * RWKV_W + DECAY_LORA + A_LORA
N_HGRN_COLS = 4 * HGRN_W
N_IN = N_RWKV_COLS + N_HGRN_COLS

kernel_name = "hymba_rwkv7_hgrn2_layer"


def rmsnorm(x, g, eps=NORM_EPS):
    xf = x.astype(jnp.float32)
    y = xf * lax.rsqrt(jnp.mean(xf * xf, axis=-1, keepdims=True) + eps)
    return (y * g.astype(jnp.float32)).astype(x.dtype)


def token_shift(p):
    return jnp.pad(p[:, :-1], ((0, 0), (1, 0), (0, 0)))


def rwkv7_mix(pr, mu, w0, w2, a0, a2, k_k, k_a, r_k, lnx_w, lnx_b):
    B, T, _ = pr.shape
    H, N = RWKV_HEADS, RWKV_HEAD
    pm = pr + mu * (token_shift(pr) - pr)
    idx = np.cumsum([RWKV_W, RWKV_W, RWKV_W, RWKV_W, DECAY_LORA])
    r, k, v, gate, wd, ad = jnp.split(pm, idx, axis=-1)
    w_log = -jax.nn.softplus(-(w0 + jnp.tanh(wd) @ w2)) - 0.5
    decay = jnp.exp(-jnp.exp(w_log))
    a = jax.nn.sigmoid(a0 + ad @ a2)
    kk = (k * k_k).reshape(B, T, H, N)
    kk = kk / jnp.maximum(jnp.linalg.norm(kk, axis=-1, keepdims=True), 1e-12)
    k = k * (1.0 + (a - 1.0) * k_a)
    hs = lambda z: z.reshape(B, T, H, N)
    r4, k4, v4, d4, a4 = hs(r), hs(k), hs(v), hs(decay), hs(a)
    avec = -kk
    bvec = kk * a4

    def step(S, inp):
        rt, wt, kt, vt, at, bt = inp
        Sa = jnp.einsum('bhvk,bhk->bhv', S, at)
        S = S * wt[:, :, None, :] + Sa[..., None] * bt[:, :, None, :] + vt[..., None] * kt[:, :, None, :]
        y = jnp.einsum('bhvk,bhk->bhv', S, rt)
        return S, y

    tm = lambda z: jnp.transpose(z, (1, 0, 2, 3))
    S0 = jnp.zeros((B, H, N, N), jnp.float32)
    _, y = lax.scan(step, S0, (tm(r4), tm(d4), tm(k4), tm(v4), tm(avec), tm(bvec)))
    y = jnp.transpose(y, (1, 0, 2, 3))
    mean = jnp.mean(y, axis=-1, keepdims=True)
    var = jnp.mean(jnp.square(y - mean), axis=-1, keepdims=True)
    y = (y - mean) * lax.rsqrt(var + LNX_EPS)
    y = y.reshape(B, T, RWKV_W) * lnx_w + lnx_b
    bonus = jnp.sum(r4 * k4 * r_k, axis=-1, keepdims=True) * v4
    y = y + bonus.reshape(B, T, RWKV_W)
    return y * jax.nn.silu(gate)


def hgrn2_mix(ph, lb, norm_g):
    B, T, _ = ph.shape
    H, DK, DV, C = HGRN_HEADS, HGRN_EXPAND, HGRN_DV, CHUNK
    nC = T // C
    q, f_raw, i, gate = jnp.split(ph, 4, axis=-1)
    f = lb + (1.0 - lb) * jax.nn.sigmoid(f_raw)
    k = 1.0 - f
    logf = jnp.log(f)

    def chunked(z, d):
        return jnp.transpose(z.reshape(B, nC, C, H, d), (1, 0, 3, 2, 4))

    qc, kc, vc = chunked(q, DK), chunked(k, DK), chunked(i, DV)
    Gc = jnp.cumsum(chunked(logf, DK), axis=-2)
    causal = (jnp.arange(C)[:, None] >= jnp.arange(C)[None, :])[None, None, :, :, None]

    def step(S, inp):
        qt, kt, vt, Gt = inp
        G_last = Gt[:, :, -1, :]
        inter = jnp.einsum('bhtk,bhkv->bhtv', qt * jnp.exp(Gt), S)
        Dlt = Gt[:, :, :, None, :] - Gt[:, :, None, :, :]
        E = jnp.where(causal, jnp.exp(jnp.where(causal, Dlt, 0.0)), 0.0)
        A = jnp.einsum('bhtk,bhsk,bhtsk->bhts', qt, kt, E)
        intra = jnp.einsum('bhts,bhsv->bhtv', A, vt)
        kdec = kt * jnp.exp(G_last[:, :, None, :] - Gt)
        S = S * jnp.exp(G_last)[..., None] + jnp.einsum('bhsk,bhsv->bhkv', kdec, vt)
        return S, inter + intra

    S0 = jnp.zeros((B, H, DK, DV), jnp.float32)
    _, o = lax.scan(step, S0, (qc, kc, vc, Gc))
    o = jnp.transpose(o, (1, 0, 3, 2, 4)).reshape(B, T, H, DV)
    o = o * lax.rsqrt(jnp.mean(o * o, axis=-1, keepdims=True) + NORM_EPS)
    o = o.reshape(B, T, HGRN_W) * norm_g
    return o * jax.nn.silu(gate)


def setup_inputs(seed: int = 0) -> dict:
    key = jax.random.key(seed)
    ks = jax.random.split(key, 20)
    f32 = jnp.float32
    nrm = lambda k, s, sc: (jax.random.normal(k, s, f32) * sc)
    return {
        "x": nrm(ks[0], (BATCH, SEQ, D_MODEL), 1.0),
        "norm_g": 1.0 + nrm(ks[1], (DEPTH, D_MODEL), 0.02),
        "w_in": nrm(ks[2], (DEPTH, D_MODEL, N_IN), D_MODEL ** -0.5),
        "mu": jax.random.uniform(ks[3], (DEPTH, N_RWKV_COLS), f32, 0.0, 1.0),
        "w0": jax.random.uniform(ks[4], (DEPTH, RWKV_W), f32, -4.0, 0.0),
        "w2": nrm(ks[5], (DEPTH, DECAY_LORA, RWKV_W), 0.1),
        "a0": nrm(ks[6], (DEPTH, RWKV_W), 0.1),
        "a2": nrm(ks[7], (DEPTH, A_LORA, RWKV_W), 0.1),
        "k_k": 0.85 + nrm(ks[8], (DEPTH, RWKV_W), 0.02),
        "k_a": 1.0 + nrm(ks[9], (DEPTH, RWKV_W), 0.02),
        "r_k": nrm(ks[10], (DEPTH, RWKV_HEADS, RWKV_HEAD), 0.1),
        "lnx_w": 1.0 + nrm(ks[11], (DEPTH, RWKV_W), 0.02),
        "lnx_b": nrm(ks[12], (DEPTH, RWKV_W), 0.01),
        "hgrn_norm_g": 1.0 + nrm(ks[13], (DEPTH, HGRN_W), 0.02),
        "lb_param": nrm(ks[14], (DEPTH + 1, HGRN_W), 0.1),
        "w_out": nrm(ks[15], (DEPTH, D_MIX, D_MODEL), D_MIX ** -0.5),
        "final_g": 1.0 + nrm(ks[16], (D_MODEL,), 0.02),
    }


def reference(x, norm_g, w_in, mu, w0, w2, a0, a2, k_k, k_a, r_k, lnx_w, lnx_b,
              hgrn_norm_g, lb_param, w_out, final_g):
    f32 = jnp.float32
    lbs = jnp.cumsum(jax.nn.softmax(lb_param.astype(f32), axis=0), axis=0)
    h = x
    for l in range(DEPTH):
        hn = rmsnorm(h, norm_g[l])
        p = (hn @ w_in[l]).astype(f32)
        pr, ph = p[..., :N_RWKV_COLS], p[..., N_RWKV_COLS:]
        y_r = rwkv7_mix(pr, mu[l].astype(f32), w0[l].astype(f32), w2[l].astype(f32),
                        a0[l].astype(f32), a2[l].astype(f32), k_k[l].astype(f32),
                        k_a[l].astype(f32), r_k[l].astype(f32), lnx_w[l].astype(f32),
                        lnx_b[l].astype(f32))
        y_h = hgrn2_mix(ph, lbs[l], hgrn_norm_g[l].astype(f32))
        y = jnp.concatenate([y_r, y_h], axis=-1).astype(h.dtype)
        h = h + y @ w_out[l]
    return rmsnorm(h, final_g)
```

```python
import math
import numpy as np
import ml_dtypes
import concourse.bass as bass
import concourse.mybir as mybir
from concourse.bass_utils import run_bass_kernel_spmd

F32 = mybir.dt.float32
BF16 = mybir.dt.bfloat16
ALU = mybir.AluOpType
AF = mybir.ActivationFunctionType

T_SEQ = 4096
D = 2048
KD = 16
TB = 256
FILL_OPS = 5.0
CPB = TB // 64
NT = TB // 128
NB = T_SEQ // TB
NCOL = 2240
NPRM = 46
CEXP = math.exp(-0.5)
NORM_EPS = 1e-6
LNX_EPS = 64e-5


class Buf:
    __slots__ = ("name", "last_write", "reads")

    def __init__(self, name=""):
        self.name = name
        self.last_write = None
        self.reads = []


class Sched:
    ENG = ("pe", "act", "dve", "pool", "sp")

    def __init__(self, nc, n_dma_sems=32, same_eng_sync=True):
        self.nc = nc
        self.same_eng_sync = same_eng_sync
        self.ops = {e: [] for e in self.ENG}
        self.count = {e: 0 for e in self.ENG}
        self.waited = {e: {} for e in self.ENG}
        self.sems = {}
        self._stack = []
        self.dma_sems = []
        for i in range(n_dma_sems):
            cm = nc.semaphore("dma_%d" % i)
            self.dma_sems.append([cm.__enter__(), 0])
            self._stack.append(cm)
        self.dma_rr = 0
        self.n_rot = n_dma_sems

    def close(self):
        for cm in reversed(self._stack):
            cm.__exit__(None, None, None)

    def _collect(self, eng, reads, writes):
        deps = []
        for b in reads:
            if b.last_write is not None:
                deps.append(b.last_write)
        for b in writes:
            if b.last_write is not None:
                deps.append(b.last_write)
            deps.extend(b.reads)
        wd = self.waited[eng]
        best = {}
        for (key, sem, val, src) in deps:
            if src == eng and (eng in ("pe", "sp") or not self.same_eng_sync):
                continue
            if src == eng and key == "e_%s_%d" % (eng, self.count[eng] // self.EPOCH) \
                    and (self.count[eng] % self.EPOCH) - val >= self.SAME_ENG_GAP:
                continue
            if wd.get(key, 0) >= val:
                continue
            wd[key] = val
            best[key] = (sem, val)
        return list(best.values())

    EPOCH = 2000
    SAME_ENG_GAP = 3

    def _esem(self, eng, epoch):
        key = (eng, epoch)
        if key not in self.sems:
            cm = self.nc.semaphore("prog_%s_%d" % (eng, epoch))
            self.sems[key] = cm.__enter__()
            self._stack.append(cm)
        return self.sems[key]

    def op(self, eng, fn, reads=(), writes=()):
        waits = self._collect(eng, reads, writes)
        epoch, pos = divmod(self.count[eng], self.EPOCH)
        self.count[eng] += 1
        sem = self._esem(eng, epoch)
        tok = ("e_%s_%d" % (eng, epoch), sem, pos + 1, eng)
        self.ops[eng].append((waits, fn, sem, 1))
        for b in reads:
            b.reads.append(tok)
        for b in writes:
            b.last_write = tok
            b.reads = []
        return tok

    def dma(self, eng, fn, reads=(), writes=(), inc=16):
        waits = self._collect(eng, reads, writes)
        if eng == "pool":
            cm = self.nc.semaphore("swdma_%d" % len(self.dma_sems))
            self.dma_sems.append([cm.__enter__(), 0])
            self._stack.append(cm)
            idx = len(self.dma_sems) - 1
        else:
            idx = self.dma_rr % self.n_rot
            self.dma_rr += 1
        slot = self.dma_sems[idx]
        key = "d_%d" % idx
        sem, cur = slot
        wd = self.waited[eng]
        if cur > 0 and wd.get(key, 0) < cur:
            wd[key] = cur
            waits.append((sem, cur))
        slot[1] = cur + inc
        tok = (key, sem, slot[1], None)
        self.ops[eng].append((waits, fn, sem, inc))
        for b in reads:
            b.reads.append(tok)
        for b in writes:
            b.last_write = tok
            b.reads = []
        return tok

    def wait_all(self, eng, bufs):
        waits = self._collect(eng, bufs, ())
        if waits:
            self.ops[eng].append((waits, None, None, 0))

    def emit(self):
        nc = self.nc
        ops = self.ops

        def run(e, lst):
            for waits, fn, sem, inc in lst:
                for s, v in waits:
                    e.wait_ge(s, v)
                if fn is not None:
                    fn(e).then_inc(sem, inc)

        with nc.Block() as block:
            @block.tensor
            def _(e):
                run(e, ops["pe"])

            @block.scalar
            def _(e):
                run(e, ops["act"])

            @block.vector
            def _(e):
                run(e, ops["dve"])

            @block.gpsimd
            def _(e):
                run(e, ops["pool"])

            @block.sync
            def _(e):
                run(e, ops["sp"])


class Tl:
    def __init__(self, t, name=""):
        self.t = t
        self.b = Buf(name)


def build_program(dbg=False, stop=None, phaseB=True, rlim=None):
    nc = bass.Bass("TRN2", target_bir_lowering=False)
    S = Sched(nc)

    def din(name, shape, dt=F32):
        return nc.dram_tensor(name, list(shape), dt, kind="ExternalInput").ap()

    x_d = din("x", [T_SEQ, D])
    w_d = din("w", [D, NCOL])
    prm_d = din("prm", [128, NPRM])
    w2_d = din("w2s", [96, 256])
    a2_d = din("a2s", [96, 256])
    wout_d = din("wout", [D, D])
    fg_d = din("fg", [128, D])
    ident_d = din("ident", [128, 128], BF16)
    cst_d = din("cst", [128, 5, TB])
    ones_d = din("ones", [128, 2, 128])
    out_d = nc.dram_tensor("out", [1024, D], F32, kind="ExternalOutput").ap()
    xres_d = din("xres", [1024, D])
    ybuf_d = [nc.dram_tensor("ybuf%d" % j_, [512, 1024], BF16) for j_ in range(4)]
    yall_d = nc.dram_tensor("yall", [8192, 1024], BF16)
    b_ybuf = [Buf("ybuf%d" % j_) for j_ in range(4)]
    b_yall = Buf("yall")
    b_out = Buf("out")
    if dbg:
        dbg_d = nc.dram_tensor("dbg", [16, 128, TB], F32, kind="ExternalOutput").ap()
        b_dbg = Buf("dbg")

    def sb(name, shape, dt=F32):
        return Tl(nc.alloc_sbuf_tensor("s_" + name, list(shape), dt), name)

    def ps(name, shape, dt=F32):
        return Tl(nc.alloc_psum_tensor("p_" + name, list(shape), dt), name)

    LIM = {"on": False, "n": None, "c": 0}
    BANK_LOCK = {}

    def OP(eng, fn, r, w):
        if LIM["on"] and LIM["n"] is not None:
            LIM["c"] += 1
            if LIM["c"] > LIM["n"]:
                return None
        rr = [t.b if isinstance(t, Tl) else t for t in r]
        ww = [t.b if isinstance(t, Tl) else t for t in w]
        if True:
            for b_ in rr + ww:
                lk = BANK_LOCK.get(id(b_))
                if lk is not None and lk not in ww:
                    ww.append(lk)
        return S.op(eng, fn, reads=rr, writes=ww)

    def DMA(fn, r, w, eng="sp", inc=16):
        return S.dma(eng, fn, reads=[t.b if isinstance(t, Tl) else t for t in r],
                     writes=[t.b if isinstance(t, Tl) else t for t in w], inc=inc)

    def TT(eng, out, in0, in1, op, r, w):
        OP(eng, lambda e: e.tensor_tensor(out=out, in0=in0, in1=in1, op=op), r, w)

    def TS(eng, out, in0, s1, s2, op0, op1, r, w):
        if op1 is None:
            OP(eng, lambda e: e.tensor_scalar(out=out, in0=in0, scalar1=s1, scalar2=None, op0=op0), r, w)
        else:
            OP(eng, lambda e: e.tensor_scalar(out=out, in0=in0, scalar1=s1, scalar2=s2, op0=op0, op1=op1), r, w)

    def STT(out, in0, scalar, in1, op0, op1, r, w):
        OP("dve", lambda e: e.scalar_tensor_tensor(out=out, in0=in0, scalar=scalar, in1=in1, op0=op0, op1=op1), r, w)

    def ACT(out, in_, func, r, w, bias=None, scale=None, accum=None, eng="act"):
        kw = {}
        if bias is not None:
            kw["bias"] = bias
        if scale is not None:
            kw["scale"] = scale
        if accum is not None:
            kw["accum_out"] = accum
        OP(eng, lambda e: e.activation(out=out, in_=in_, func=func, **kw), r, w)

    def SIG(out, in_, r, w, nbias=None, scale=1.0):
        ACT(out, in_, AF.Exp, r, w, bias=nbias, scale=-scale)
        ACT(out, out, AF.Ln, w + [epsn], w, bias=epsn.t[0:out.shape[0], 2:3])
        ACT(out, out, AF.Exp, w, w, scale=-1.0)

    def CP(eng, out, in_, r, w):
        if eng == "act":
            OP(eng, lambda e: e.activation(out=out, in_=in_, func=AF.Copy), r, w)
        else:
            OP(eng, lambda e: e.tensor_copy(out=out, in_=in_), r, w)

    def MM(out, lhsT, rhs, start, stop, r, w):
        OP("pe", lambda e: e.matmul(out, lhsT=lhsT, rhs=rhs, start=start, stop=stop), r, w)

    def TR(out, in_, r, w):
        OP("pe", lambda e: e.transpose(out, in_, ident.t[:, :]), list(r) + [ident], w)

    wbf = nc.alloc_sbuf_tensor("wbf", [128, KD * NCOL], BF16)
    b_wbf = [Buf("wbf%d" % i) for i in range(KD)]
    wbf3 = wbf[:, :].rearrange("p (k c) -> p k c", c=NCOL)
    wob3 = wbf[:, 0:KD * D].rearrange("p (k c) -> p k c", c=D)
    stage = [sb("stage%d" % i, [128, NCOL]) for i in range(2)]
    stage_rr = [0]

    def next_stage():
        s = stage[stage_rr[0] % 2]
        stage_rr[0] += 1
        return s

    xs = [sb("xs%d" % i, [128, D], BF16) for i in range(2)]
    hn_raw = nc.alloc_sbuf_tensor("hnT", [128, 2 * KD * TB], BF16)
    b_hn = [Buf("hn0"), Buf("hn1")]
    hnT = [hn_raw[:, i * KD * TB:(i + 1) * KD * TB].rearrange("p (k t) -> p k t", t=TB) for i in range(2)]
    yT3 = hn_raw[:, :].rearrange("p (k t) -> p k t", t=512)
    prm = sb("prm", [128, NPRM])
    drv = sb("drv", [128, 12])
    epsn = sb("epsn", [128, 3])
    ident = sb("ident", [128, 128], BF16)
    cst = sb("cst", [128, 5, TB])
    ones = sb("ones", [128, 2, 128])
    w2b = sb("w2b", [96, 512], BF16)
    MS = cst.t[:, 0, :]
    MI = cst.t[:, 1, :]
    MST = cst.t[:, 2, :]
    RST = cst.t[:, 3, :]
    EYE = cst.t[:, 4, :]
    BONES = ones.t[:, 0, :]
    AONES = ones.t[:, 1, :]

    lastcol = sb("lastcol", [128, 10])
    PR = [sb("PR%d" % i, [128, TB + 1]) for i in range(3)]
    pm_arena = nc.alloc_sbuf_tensor("pm_arena", [128, 32 * TB], F32)
    pm2 = [[Tl(pm_arena[:, (k * 16 + i) * TB:(k * 16 + i + 1) * TB], "pm%d_%d" % (k, i)) for i in range(16)]
           for k in range(2)]
    pm = pm2[0]
    twb2 = [sb("twb%d" % k, [96, TB], BF16) for k in range(2)]
    adb2 = [sb("adb%d" % k, [96, TB], BF16) for k in range(2)]
    tP = sb("tP", [128, TB])
    ssx = sb("ssx", [128, 2])
    rstd_x = sb("rstdx", [128, 2])

    def mk(name, dt=F32, n=2, shape=None):
        return [sb("%s%d" % (name, i), shape or [128, TB], dt) for i in range(n)]

    rh, kh, bh, ah, kt, bt, vb = [mk(n_, BF16) for n_ in ("rh", "kh", "bh", "ah", "kt", "bt", "vb")]
    tokall = mk("tokall", BF16, shape=[128, 3 * CPB, 64])
    ktok = [Tl(tokall[i].t[:, 0:CPB, :]) for i in range(2)]
    btok = [Tl(tokall[i].t[:, CPB:2 * CPB, :]) for i in range(2)]
    vtok = [Tl(tokall[i].t[:, 2 * CPB:3 * CPB, :]) for i in range(2)]
    for i in range(2):
        ktok[i].b = btok[i].b = vtok[i].b = tokall[i].b
    gmC = mk("gmC", shape=[128, CPB])
    bonus = mk("bonus")
    ysb = mk("ysb")
    Lab, LabT, LakT, LrbT, LrkT = [mk(n_, BF16) for n_ in ("Lab", "LabT", "LakT", "LrbT", "LrkT")]
    Pp = [mk("Pa", BF16), mk("Pb", BF16)]
    PTp = [mk("PTa", BF16), mk("PTb", BF16)]
    TTp = [mk("TTa", BF16), mk("TTb", BF16)]
    tmp_arena = nc.alloc_sbuf_tensor("tmp_arena", [128, 8 * TB], F32)
    tA, tB, tC, tD, tE, tF, tG, tH = [Tl(tmp_arena[:, i * TB:(i + 1) * TB], "tmp%d" % i) for i in range(8)]
    tmps = [tA, tB, tC, tD, tE, tF, tG, tH]
    fgt_ap = tmp_arena[:, 0:D]
    Hf = sb("Hf", [128, 128])
    Hb = sb("Hb", [128, 128], BF16)
    Wb = sb("Wb", [128, 256], BF16)
    Ub = sb("Ub", [128, 256], BF16)
    yout = mk("yout", BF16, n=4)
    qh, gkh, gkd, gvb = [mk(n_, BF16) for n_ in ("qh", "gkh", "gkd", "gvb")]
    gpad = [[sb("gpad%d_%d" % (j_, par_), [128, 2 * NT, 128], BF16) for par_ in range(2)] for j_ in range(2)]
    ggC = mk("ggC", shape=[128, CPB])
    ATb = sb("ATb", [128, TB], BF16)
    osb = mk("osb")
    Sf = sb("Sf", [128, 256])
    Sb_ = sb("Sb", [128, 256], BF16)
    hsb_ap = [pm_arena[:, i * D:(i + 1) * D] for i in range(2)]
    hsb_b = [[pm[k].b for k in range(i * (D // TB), (i + 1) * (D // TB))] for i in range(2)]

    Bp = [ps("Bp%d" % i, [128, 512]) for i in range(2)]
    Bt = ps("Bt", [128, 1024], BF16)
    Bg = [ps("Bg%d" % i, [128, 512]) for i in range(2)]
    B6 = nc.alloc_psum_tensor("B6", [128, 512], F32)
    b_H = Buf("psH")
    b_Y = [Buf("psY%d" % i) for i in range(3)]
    B7 = nc.alloc_psum_tensor("B7", [128, 512], F32)
    b_S = Buf("psS")
    b_O = [Buf("psO%d" % i) for i in range(2)]
    B5 = nc.alloc_psum_tensor("B5", [128, 512], F32)
    b_W = Buf("psW")
    b_U = Buf("psU")
    for grp in ([Bp[0].b], [Bp[1].b], [Bt.b], [Bg[0].b], [Bg[1].b], [b_W, b_U], [b_H] + b_Y, [b_S] + b_O):
        lk_ = Buf("lock")
        for b_ in grp:
            BANK_LOCK[id(b_)] = lk_
    bg_rr = [0]

    def next_bg():
        b = Bg[bg_rr[0] % 2]
        bg_rr[0] += 1
        return b

    DMA(lambda e: e.dma_start(out=prm.t[:, :], in_=prm_d), [], [prm])
    DMA(lambda e: e.dma_start(out=ident.t[:, :], in_=ident_d), [], [ident])
    DMA(lambda e: e.dma_start(out=cst.t[:, :, :], in_=cst_d), [], [cst])
    DMA(lambda e: e.dma_start(out=ones.t[:, :, :], in_=ones_d), [], [ones])
    st0 = next_stage()
    DMA(lambda e: e.dma_start(out=st0.t[0:96, 0:256], in_=w2_d), [], [st0])
    DMA(lambda e: e.dma_start(out=st0.t[0:96, 256:512], in_=a2_d), [], [st0])
    CP("pool", w2b.t[:, :], st0.t[0:96, 0:512], [st0], [w2b])
    OP("pool", lambda e: e.memset(lastcol.t[:, :], 0.0), [], [lastcol])
    OP("pool", lambda e: e.memset(Hf.t[:, :], 0.0), [], [Hf])
    OP("pool", lambda e: e.memset(Hb.t[:, :], 0.0), [], [Hb])
    OP("pool", lambda e: e.memset(Sf.t[:, :], 0.0), [], [Sf])
    OP("pool", lambda e: e.memset(Sb_.t[:, :], 0.0), [], [Sb_])
    for j_ in range(2):
        for par_ in range(2):
            g_ = gpad[j_][par_]
            OP("pool", lambda e, g_=g_: e.memset(g_.t[:, :, :], 0.0), [], [g_])
    OP("pool", lambda e: e.memset(epsn.t[:, 0:1], NORM_EPS), [], [epsn])
    OP("pool", lambda e: e.memset(epsn.t[:, 1:2], LNX_EPS), [], [epsn])
    OP("pool", lambda e: e.memset(epsn.t[:, 2:3], 1.0), [], [epsn])
    TS("dve", drv.t[:, 0:2], prm.t[:, 32:34], -1.0, 1.0, ALU.mult, ALU.add, [prm], [drv])
    TT("dve", drv.t[:, 6:8], prm.t[:, 42:44], prm.t[:, 44:46], ALU.subtract, [prm], [drv])
    SIG(drv.t[:, 2:4], drv.t[:, 6:8], [drv], [drv])
    TS("dve", drv.t[:, 8:12], prm.t[:, 26:30], -1.0, None, ALU.mult, None, [prm], [drv])
    TS("dve", drv.t[:, 4:6], drv.t[:, 2:4], -1.0, 1.0, ALU.mult, ALU.add, [drv], [drv])

    for dc in range(KD):
        st = next_stage()
        DMA(lambda e, st=st, dc=dc: e.dma_start(out=st.t[:, :], in_=w_d[dc * 128:(dc + 1) * 128, :]), [], [st])
        if dc % 2 == 0:
            TS("dve", wbf3[:, dc, :], st.t[:, :], prm.t[:, dc:dc + 1], None, ALU.mult, None, [st, prm], [b_wbf[dc]])
        else:
            ACT(wbf3[:, dc, :], st.t[:, :], AF.Copy, [st, prm], [b_wbf[dc]], scale=prm.t[:, dc:dc + 1])

    NXT = T_SEQ // 128

    def x_dma(t):
        st = stage[t % 2]
        DMA(lambda e: e.dma_start(out=st.t[:, 0:D], in_=x_d[t * 128:(t + 1) * 128, :]), [], [st])

    def stage_X(blk):
        hb = blk % 2
        for tt in range(NT):
            t = blk * NT + tt
            if t == 0:
                x_dma(0)
            if t + 1 < NXT:
                x_dma(t + 1)
            st = stage[t % 2]
            xsb = xs[tt % 2]
            OP("dve", lambda e, st=st, xsb=xsb: e.scalar_tensor_tensor(
                out=xsb.t[:, :], in0=st.t[:, 0:D], scalar=1.0, in1=st.t[:, 0:D], op0=ALU.mult, op1=ALU.mult,
                accum_out=ssx.t[:, 0:1]), [st], [xsb, ssx])
            yield
            ACT(ssx.t[:, 1:2], ssx.t[:, 0:1], AF.Ln, [ssx, epsn], [ssx], bias=epsn.t[:, 0:1], scale=1.0 / D)
            ACT(rstd_x.t[:, 0:1], ssx.t[:, 1:2], AF.Exp, [ssx], [rstd_x], scale=-0.5)
            ACT(xsb.t[:, :], st.t[:, 0:D], AF.Copy, [st, rstd_x], [xsb], scale=rstd_x.t[:, 0:1])
            yield
            for half in range(2):
                for j in range(8):
                    dc = half * 8 + j
                    TR(Bt.t[:, j * 128:(j + 1) * 128], xsb.t[:, dc * 128:(dc + 1) * 128], [xsb], [Bt])
                CP("act" if half == 0 else "dve",
                   hnT[hb][:, half * 8:(half + 1) * 8, tt * 128:(tt + 1) * 128],
                   Bt.t[:, :].rearrange("p (k t) -> p k t", t=128), [Bt], [b_hn[hb]])
                yield

    COLOFF = [i * 128 for i in range(16)] + [2048, 2144]
    CTM = [128] * 16 + [96, 96]

    def stage_P(blk):
        hb = blk % 2
        pmw, twb, adb = pm2[blk % 2], twb2[blk % 2], adb2[blk % 2]
        order = [16, 17, 0, 1, 2, 3, 4, 5, 6, 7, 10, 11, 8, 9, 12, 13, 14, 15]

        def evac(n_, ct):
            M = CTM[ct]
            bp = Bp[n_ % 2]
            if ct < 8 or ct >= 16:
                li = ct if ct < 8 else ct - 8
                mucol = 16 + li
                pr_ = PR[n_ % 3]
                CP("pool", pr_.t[0:M, 0:1], lastcol.t[0:M, li:li + 1], [lastcol], [pr_])
                CP("act" if n_ % 2 else "dve", pr_.t[0:M, 1:TB + 1], bp.t[0:M, 0:TB], [bp], [pr_])
                CP("pool", lastcol.t[0:M, li:li + 1], pr_.t[0:M, TB:TB + 1], [pr_], [lastcol])
                TT("pool", tP.t[0:M, :], pr_.t[0:M, 0:TB], pr_.t[0:M, 1:TB + 1], ALU.subtract, [pr_], [tP])
                if ct < 8:
                    STT(pmw[ct].t[:, :], tP.t[:, :], prm.t[:, mucol:mucol + 1], pr_.t[:, 1:TB + 1], ALU.mult, ALU.add,
                        [tP, prm, pr_], [pmw[ct]])
                elif ct == 16:
                    STT(tP.t[0:96, :], tP.t[0:96, :], prm.t[0:96, mucol:mucol + 1], pr_.t[0:96, 1:TB + 1],
                        ALU.mult, ALU.add, [tP, prm, pr_], [tP])
                    SIG(tP.t[0:96, :], tP.t[0:96, :], [tP], [tP], scale=2.0)
                    TS("dve", twb.t[:, :], tP.t[0:96, :], 2.0, -1.0, ALU.mult, ALU.add, [tP], [twb])
                else:
                    STT(adb.t[:, :], tP.t[0:96, :], prm.t[0:96, mucol:mucol + 1], pr_.t[0:96, 1:TB + 1],
                        ALU.mult, ALU.add, [tP, prm, pr_], [adb])
            else:
                CP("act" if n_ % 2 else "dve", pmw[ct].t[:, :], bp.t[:, 0:TB], [bp], [pmw[ct]])

        pending = None
        for n_, ct in enumerate(order):
            M = CTM[ct]
            bp = Bp[n_ % 2]
            for dc in range(KD):
                MM(bp.t[0:M, 0:TB], wbf3[:, dc, COLOFF[ct]:COLOFF[ct] + M], hnT[hb][:, dc, :], dc == 0, dc == KD - 1,
                   [b_wbf[dc], b_hn[hb]], [bp])
                if dc == KD // 2 - 1:
                    yield
            if pending is not None:
                evac(*pending)
            pending = (n_, ct)
            yield
        evac(*pending)
        yield

    c3 = lambda ap: ap.rearrange("p (c t) -> p c t", t=64)
    tok3 = lambda n: Bt.t[:, n * NT * 128:(n + 1) * NT * 128].rearrange("p (k t) -> p k t", t=128)

    def stage_R(hp, blk):
        pm, twb, adb = pm2[blk % 2], twb2[blk % 2], adb2[blk % 2]
        p_r, p_k, p_v = pm[hp], pm[2 + hp], pm[4 + hp]
        if stop == "R":
            LIM["on"] = True
            LIM["n"] = rlim
            LIM["c"] = 0
        bz = next_bg()
        MM(bz.t[:, 0:TB], w2b.t[:, hp * 128:(hp + 1) * 128], twb.t[:, :], True, True, [w2b, twb], [bz])
        SIG(tA.t[:, :], bz.t[:, 0:TB], [bz, drv], [tA], nbias=drv.t[:, 8 + hp:9 + hp])
        bz2 = next_bg()
        MM(bz2.t[:, 0:TB], w2b.t[:, 256 + hp * 128:256 + (hp + 1) * 128], adb.t[:, :], True, True, [w2b, adb], [bz2])
        SIG(tB.t[:, :], bz2.t[:, 0:TB], [bz2, drv], [tB], nbias=drv.t[:, 10 + hp:11 + hp])
        OP("dve", lambda e: e.tensor_tensor_scan(out=tC.t[:, :], data0=RST, data1=tA.t[:, :], initial=0.0,
                                                  op0=ALU.mult, op1=ALU.add), [cst, tA], [tC])
        ACT(tH.t[:, :], tC.t[:, :], AF.Exp, [tC], [tH], scale=-CEXP)
        CP("pool", gmC[hp].t[:, :], c3(tH.t[:, :])[:, :, 63], [tH], [gmC[hp]])
        ACT(tD.t[:, :], tC.t[:, :], AF.Exp, [tC], [tD], scale=CEXP)
        yield
        TT("pool", tE.t[:, :], tC.t[:, :], tA.t[:, :], ALU.subtract, [tC, tA], [tE])
        ACT(tE.t[:, :], tE.t[:, :], AF.Exp, [tE], [tE], scale=-CEXP)
        TT("dve", c3(tF.t[:, :]), c3(tC.t[:, :])[:, :, 63:64].to_broadcast([128, CPB, 64]), c3(tC.t[:, :]),
           ALU.subtract, [tC], [tF])
        ACT(tF.t[:, :], tF.t[:, :], AF.Exp, [tF], [tF], scale=-CEXP)
        yield
        TS("dve", tG.t[:, :], p_k.t[:, :], prm.t[:, 30 + hp:31 + hp], None, ALU.mult, None, [p_k, prm], [tG])
        ACT(tA.t[:, :], tG.t[:, :], AF.Square, [tG], [tA])
        bz3 = next_bg()
        MM(bz3.t[:, 0:TB], BONES, tA.t[:, :], True, True, [ones, tA], [bz3])
        ACT(tA.t[:, :], bz3.t[:, 0:TB], AF.Ln, [bz3], [tA], scale=64.0)
        TS("dve", tA.t[:, :], tA.t[:, :], 0.5, math.log(1e-12), ALU.mult, ALU.max, [tA], [tA])
        ACT(tA.t[:, :], tA.t[:, :], AF.Exp, [tA], [tA], scale=-1.0)
        TT("dve", tG.t[:, :], tG.t[:, :], tA.t[:, :], ALU.mult, [tG, tA], [tG])
        TS("dve", tA.t[:, :], tB.t[:, :], prm.t[:, 32 + hp:33 + hp], drv.t[:, hp:hp + 1], ALU.mult, ALU.add,
           [tB, prm, drv], [tA])
        TT("dve", tA.t[:, :], tA.t[:, :], p_k.t[:, :], ALU.mult, [tA, p_k], [tA])
        TT("pool", tB.t[:, :], tG.t[:, :], tB.t[:, :], ALU.mult, [tG, tB], [tB])
        yield
        TT("dve", rh[hp].t[:, :], p_r.t[:, :], tH.t[:, :], ALU.mult, [p_r, tH], [rh[hp]])
        TT("dve", kh[hp].t[:, :], tA.t[:, :], tD.t[:, :], ALU.mult, [tA, tD], [kh[hp]])
        TT("pool", bh[hp].t[:, :], tB.t[:, :], tD.t[:, :], ALU.mult, [tB, tD], [bh[hp]])
        STT(ah[hp].t[:, :], tG.t[:, :], -1.0, tE.t[:, :], ALU.mult, ALU.mult, [tG, tE], [ah[hp]])
        TT("dve", kt[hp].t[:, :], tA.t[:, :], tF.t[:, :], ALU.mult, [tA, tF], [kt[hp]])
        TT("pool", bt[hp].t[:, :], tB.t[:, :], tF.t[:, :], ALU.mult, [tB, tF], [bt[hp]])
        CP("pool", vb[hp].t[:, :], p_v.t[:, :], [p_v], [vb[hp]])
        STT(tD.t[:, :], p_r.t[:, :], prm.t[:, 34 + hp:35 + hp], tA.t[:, :], ALU.mult, ALU.mult, [p_r, prm, tA], [tD])
        bz4 = next_bg()
        MM(bz4.t[:, 0:TB], BONES, tD.t[:, :], True, True, [ones, tD], [bz4])
        STT(bonus[hp].t[:, :], bz4.t[:, 0:TB], 64.0, p_v.t[:, :], ALU.mult, ALU.mult, [bz4, p_v], [bonus[hp]])
        yield
        for n, src in enumerate((kt[hp], bt[hp], vb[hp])):
            for ck in range(CPB):
                for h in range(2):
                    ph = slice(h * 64, (h + 1) * 64)
                    c0 = (n * CPB + ck) * 64
                    OP("pe", lambda e, ph=ph, c0=c0, src=src, ck=ck: e.transpose(
                        Bt.t[ph, c0:c0 + 64], src.t[ph, ck * 64:(ck + 1) * 64], ident.t[ph, ph]), [src, ident], [Bt])
        CP("act", tokall[hp].t[:, :, :], Bt.t[:, 0:3 * CPB * 64].rearrange("p (k t) -> p k t", t=64), [Bt],
           [tokall[hp]])
        LIM["on"] = False
        yield

    NQ = CPB // 2
    LW = CPB * 64

    def lcol(ck):
        return slice(ck * 64, (ck + 1) * 64)

    def lmat(dst_bg, lhs_t, rhs_t):
        for h in range(2):
            ph = slice(h * 64, (h + 1) * 64)
            for ck in range(CPB):
                cs = slice(ck * 64, (ck + 1) * 64)
                MM(dst_bg.t[ph, lcol(ck)], lhs_t.t[ph, cs], rhs_t.t[ph, cs], True, True, [lhs_t, rhs_t], [dst_bg])

    def lsq(dst_bg, lhs_t, rhs_t):
        for h in range(2):
            ph = slice(h * 64, (h + 1) * 64)
            for ck in range(CPB):
                MM(dst_bg.t[ph, lcol(ck)], lhs_t.t[ph, lcol(ck)], rhs_t.t[ph, lcol(ck)], True, True,
                   [lhs_t, rhs_t], [dst_bg])

    Tfinal = [None, None]

    def stage_L(hp):
        if stop == "L":
            LIM["on"] = True
            LIM["n"] = rlim
            LIM["c"] = 0
        for (dst, lt, rt, msk) in ((LabT, bh, ah, MS), (Lab, ah, bh, MST), (LakT, kh, ah, MS),
                                   (LrbT, bh, rh, MI), (LrkT, kh, rh, MI)):
            bgx = next_bg()
            lmat(bgx, lt[hp], rt[hp])
            TT("dve", dst[hp].t[:, :], bgx.t[:, 0:LW], msk, ALU.mult, [bgx, cst], [dst[hp]])
            yield
        P, PT, Tt = Lab[hp], LabT[hp], TTp[0][hp]
        TT("pool", Tt.t[:, :], PT.t[:, :], EYE, ALU.add, [PT, cst], [Tt])
        for j in range(5):
            Pn = Pp[j % 2][hp]
            PTn = PTp[j % 2][hp]
            b1 = next_bg()
            lsq(b1, PT, P)
            CP("act", Pn.t[:, :], b1.t[:, 0:LW], [b1], [Pn])
            if j < 4:
                b2 = next_bg()
                lsq(b2, P, PT)
                CP("dve", PTn.t[:, :], b2.t[:, 0:LW], [b2], [PTn])
            yield
            b3 = next_bg()
            Tn = TTp[(j + 1) % 2][hp]
            lsq(b3, Pn, Tt)
            TT("dve", Tn.t[:, :], b3.t[:, 0:LW], Tt.t[:, :], ALU.add, [b3, Tt], [Tn])
            P, PT, Tt = Pn, PTn, Tn
            yield
        Tfinal[hp] = Tt
        LIM["on"] = False

    def stage_C():
        for ck in range(CPB):
            cs = slice(ck * 64, (ck + 1) * 64)
            ys = ck % 3
            ycol = 128 + ys * 128
            for hp in range(2):
                hc = slice(hp * 64, (hp + 1) * 64)
                for h in range(2):
                    ph = slice(h * 64, (h + 1) * 64)
                    MM(B5[ph, hc], ah[hp].t[ph, cs], Hb.t[ph, hc], True, False, [ah[hp], Hb], [b_W])
                    MM(B5[ph, hc], LakT[hp].t[ph, lcol(ck)], vtok[hp].t[ph, ck, :], False, True,
                       [LakT[hp], vtok[hp]], [b_W])
            CP("dve", Wb.t[:, 0:128], B5[:, 0:128], [b_W], [Wb])
            for hp in range(2):
                hc = slice(hp * 64, (hp + 1) * 64)
                Tf = Tfinal[hp]
                for h in range(2):
                    ph = slice(h * 64, (h + 1) * 64)
                    MM(B5[ph, 256 + hp * 64:256 + (hp + 1) * 64], Tf.t[ph, lcol(ck)], Wb.t[ph, hc], True, True,
                       [Tf, Wb], [b_U])
            CP("act", Ub.t[:, 0:128], B5[:, 256:384], [b_U], [Ub])
            yield
            for hp in range(2):
                hc = slice(hp * 64, (hp + 1) * 64)
                for h in range(2):
                    ph = slice(h * 64, (h + 1) * 64)
                    MM(B6[ph, hc], btok[hp].t[ph, ck, :], Ub.t[ph, hc], True, False, [btok[hp], Ub], [b_H])
                    MM(B6[ph, hc], ktok[hp].t[ph, ck, :], vtok[hp].t[ph, ck, :], False, True,
                       [ktok[hp], vtok[hp]], [b_H])
            for hp in range(2):
                hc = slice(hp * 64, (hp + 1) * 64)
                for h in range(2):
                    ph = slice(h * 64, (h + 1) * 64)
                    yo = B6[ph, ycol + hp * 64:ycol + (hp + 1) * 64]
                    MM(yo, Hb.t[ph, hc], rh[hp].t[ph, cs], True, False, [Hb, rh[hp]], [b_Y[ys]])
                    MM(yo, Ub.t[ph, hc], LrbT[hp].t[ph, lcol(ck)], False, False, [Ub, LrbT[hp]], [b_Y[ys]])
                    MM(yo, vtok[hp].t[ph, ck, :], LrkT[hp].t[ph, lcol(ck)], False, True, [vtok[hp], LrkT[hp]],
                       [b_Y[ys]])
            for hp in range(2):
                STT(Hf.t[:, hp * 64:(hp + 1) * 64], Hf.t[:, hp * 64:(hp + 1) * 64],
                    gmC[hp].t[:, ck:ck + 1], B6[:, hp * 64:(hp + 1) * 64], ALU.mult, ALU.add,
                    [Hf, gmC[hp], b_H], [Hf])
            CP("act", Hb.t[:, :], Hf.t[:, :], [Hf], [Hb])
            for hp in range(2):
                CP("act", ysb[hp].t[:, cs], B6[:, ycol + hp * 64:ycol + (hp + 1) * 64], [b_Y[ys]], [ysb[hp]])
            yield

    def ystore(m, blk, src):
        tok0 = blk * TB
        j_ = tok0 // 1024
        r0 = m * 128
        c0 = tok0 % 1024
        DMA(lambda e: e.dma_start(out=ybuf_d[j_][r0:r0 + 128, c0:c0 + TB], in_=src.t[:, :]), [src], [b_ybuf[j_]],
            eng="act")

    def stage_Y(hp, blk, tA=None, tB=None):
        tA = tA or tmps[0]
        tB = tB or tmps[1]
        p_g = pm2[blk % 2][6 + hp]
        bm = next_bg()
        MM(bm.t[:, 0:TB], BONES, ysb[hp].t[:, :], True, True, [ones, ysb[hp]], [bm])
        TT("dve", tA.t[:, :], ysb[hp].t[:, :], bm.t[:, 0:TB], ALU.subtract, [ysb[hp], bm], [tA])
        ACT(tB.t[:, :], tA.t[:, :], AF.Square, [tA], [tB])
        yield
        bv = next_bg()
        MM(bv.t[:, 0:TB], BONES, tB.t[:, :], True, True, [ones, tB], [bv])
        ACT(tB.t[:, :], bv.t[:, 0:TB], AF.Ln, [bv, epsn], [tB], bias=epsn.t[:, 1:2])
        ACT(tB.t[:, :], tB.t[:, :], AF.Exp, [tB], [tB], scale=-0.5)
        TT("dve", tA.t[:, :], tA.t[:, :], tB.t[:, :], ALU.mult, [tA, tB], [tA])
        yield
        TS("dve", tA.t[:, :], tA.t[:, :], prm.t[:, 36 + hp:37 + hp], prm.t[:, 38 + hp:39 + hp], ALU.mult, ALU.add,
           [tA, prm], [tA])
        TT("pool", tA.t[:, :], tA.t[:, :], bonus[hp].t[:, :], ALU.add, [tA, bonus[hp]], [tA])
        SIG(tB.t[:, :], p_g.t[:, :], [p_g], [tB])
        TT("pool", tB.t[:, :], tB.t[:, :], p_g.t[:, :], ALU.mult, [tB, p_g], [tB])
        TT("dve", yout[hp].t[:, :], tA.t[:, :], tB.t[:, :], ALU.mult, [tA, tB], [yout[hp]])
        ystore(hp, blk, yout[hp])
        yield

    def stage_G(j, blk):
        pm = pm2[blk % 2]
        p_q, p_f, p_i = pm[8 + j], pm[10 + j], pm[12 + j]
        SIG(tA.t[:, :], p_f.t[:, :], [p_f], [tA])
        TS("dve", tA.t[:, :], tA.t[:, :], drv.t[:, 4 + j:5 + j], drv.t[:, 2 + j:3 + j], ALU.mult, ALU.add,
           [tA, drv], [tA])
        TS("pool", tB.t[:, :], tA.t[:, :], -1.0, 1.0, ALU.mult, ALU.add, [tA], [tB])
        ACT(tA.t[:, :], tA.t[:, :], AF.Ln, [tA], [tA])
        yield
        OP("dve", lambda e: e.tensor_tensor_scan(out=tC.t[:, :], data0=RST, data1=tA.t[:, :], initial=0.0,
                                                  op0=ALU.mult, op1=ALU.add), [cst, tA], [tC])
        ACT(tH.t[:, :], tC.t[:, :], AF.Exp, [tC], [tH])
        CP("pool", ggC[j].t[:, :], c3(tH.t[:, :])[:, :, 63], [tH], [ggC[j]])
        ACT(tD.t[:, :], tC.t[:, :], AF.Exp, [tC], [tD], scale=-1.0)
        TT("dve", c3(tF.t[:, :]), c3(tC.t[:, :])[:, :, 63:64].to_broadcast([128, CPB, 64]), c3(tC.t[:, :]),
           ALU.subtract, [tC], [tF])
        ACT(tF.t[:, :], tF.t[:, :], AF.Exp, [tF], [tF])
        yield
        TT("dve", qh[j].t[:, :], p_q.t[:, :], tH.t[:, :], ALU.mult, [p_q, tH], [qh[j]])
        TT("pool", gkh[j].t[:, :], tB.t[:, :], tD.t[:, :], ALU.mult, [tB, tD], [gkh[j]])
        TT("dve", gkd[j].t[:, :], tB.t[:, :], tF.t[:, :], ALU.mult, [tB, tF], [gkd[j]])
        CP("pool", gvb[j].t[:, :], p_i.t[:, :], [p_i], [gvb[j]])
        yield
        for n, src in enumerate((gkd[j], gvb[j])):
            for jj in range(NT):
                TR(Bt.t[:, (n * NT + jj) * 128:(n * NT + jj + 1) * 128], src.t[:, jj * 128:(jj + 1) * 128], [src], [Bt])
        for par in range(2):
            pp = slice(par * 64, (par + 1) * 64)
            CP("act", gpad[j][par].t[pp, :, :], Bt.t[pp, 0:2 * NT * 128].rearrange("p (k t) -> p k t", t=128), [Bt],
               [gpad[j][par]])
        yield

    def gcol(j, ck):
        return slice((j * NQ + ck // 2) * 64, (j * NQ + ck // 2 + 1) * 64)

    def stage_GA():
        bga = next_bg()
        for j in range(2):
            for ck in range(CPB):
                par = ck % 2
                cs = slice(ck * 64, (ck + 1) * 64)
                MM(bga.t[par * 64:(par + 1) * 64, gcol(j, ck)], gkh[j].t[:, cs], qh[j].t[:, cs], True, True,
                   [gkh[j], qh[j]], [bga])
        TT("dve", ATb.t[:, 0:2 * NQ * 64], bga.t[:, 0:2 * NQ * 64], MI[:, 0:2 * NQ * 64], ALU.mult, [bga, cst], [ATb])
        yield

    def stage_H():
        if stop == "H":
            LIM["on"] = True
            LIM["n"] = rlim
            LIM["c"] = 0
        for ck in range(CPB):
            par, q = ck % 2, ck // 2
            cs = slice(ck * 64, (ck + 1) * 64)
            osl = ck % 2
            ocol = 256 + osl * 128
            for j in range(2):
                gp = gpad[j][par]
                MM(B7[:, ocol + j * 64:ocol + (j + 1) * 64], Sb_.t[:, j * 128:(j + 1) * 128], qh[j].t[:, cs], True, False,
                   [Sb_, qh[j]], [b_O[osl]])
                MM(B7[:, ocol + j * 64:ocol + (j + 1) * 64], gp.t[:, NT + q, :], ATb.t[:, gcol(j, ck)], False, True,
                   [gp, ATb], [b_O[osl]])
            for j in range(2):
                gp = gpad[j][par]
                MM(B7[:, j * 128:(j + 1) * 128], gp.t[:, q, :], gp.t[:, NT + q, :], True, True, [gp], [b_S])
            for j in range(2):
                STT(Sf.t[:, j * 128:(j + 1) * 128], Sf.t[:, j * 128:(j + 1) * 128],
                    ggC[j].t[:, ck:ck + 1], B7[:, j * 128:(j + 1) * 128], ALU.mult, ALU.add,
                    [Sf, ggC[j], b_S], [Sf])
            CP("pool", Sb_.t[:, :], Sf.t[:, :], [Sf], [Sb_])
            for j in range(2):
                CP("act" if j == 0 else "dve", osb[j].t[:, cs], B7[:, ocol + j * 64:ocol + (j + 1) * 64], [b_O[osl]],
                   [osb[j]])
            yield

    def stage_O(j, blk, tA=None, tB=None):
        tA = tA or tmps[0]
        tB = tB or tmps[1]
        p_g = pm2[blk % 2][14 + j]
        ACT(tA.t[:, :], osb[j].t[:, :], AF.Square, [osb[j]], [tA])
        bm = next_bg()
        MM(bm.t[:, 0:TB], AONES, tA.t[:, :], True, True, [ones, tA], [bm])
        ACT(tA.t[:, :], bm.t[:, 0:TB], AF.Ln, [bm, epsn], [tA], bias=epsn.t[:, 0:1])
        ACT(tA.t[:, :], tA.t[:, :], AF.Exp, [tA], [tA], scale=-0.5)
        TT("dve", tA.t[:, :], tA.t[:, :], osb[j].t[:, :], ALU.mult, [tA, osb[j]], [tA])
        yield
        SIG(tB.t[:, :], p_g.t[:, :], [p_g], [tB])
        TT("pool", tB.t[:, :], tB.t[:, :], p_g.t[:, :], ALU.mult, [tB, p_g], [tB])
        STT(yout[2 + j].t[:, :], tA.t[:, :], prm.t[:, 40 + j:41 + j], tB.t[:, :], ALU.mult, ALU.mult,
            [tA, prm, tB], [yout[2 + j]])
        ystore(2 + j, blk, yout[2 + j])
        yield

    def run(gen):
        for _ in gen:
            pass

    def gather(j_):
        S.wait_all("pool", [b_ybuf[j_]])
        DMA(lambda e: e.collective_compute("AllGather", ALU.bypass, replica_groups=[[0, 1, 2, 3], [4, 5, 6, 7]],
                                           ins=[ybuf_d[j_].ap().opt()],
                                           outs=[yall_d[j_ * 2048:(j_ + 1) * 2048, :].opt()]),
            [b_ybuf[j_]], [b_yall], eng="pool", inc=1)

    dbg_n = [0]

    if dbg:
        dbgt = sb("dbgt", [128, TB])

    def dbg_dump(ap, tl, np_=128, w=TB):
        if dbg:
            slot = dbg_n[0]
            dbg_n[0] += 1
            if ap.dtype != F32:
                CP("dve", dbgt.t[0:np_, 0:w], ap, [tl], [dbgt])
                DMA(lambda e: e.dma_start(out=dbg_d[slot, 0:np_, 0:w], in_=dbgt.t[0:np_, 0:w]), [dbgt], [b_dbg])
            else:
                DMA(lambda e: e.dma_start(out=dbg_d[slot, 0:np_, 0:w], in_=ap), [tl], [b_dbg])

    nblk = NB if not dbg else dbg

    def phaseA():
        run(stage_X(0))
        if stop == "X":
            return
        for blk in range(nblk):
            run(stage_P(blk))
            if blk + 1 < nblk:
                run(stage_X(blk + 1))
            if dbg and blk == nblk - 1:
                for i_ in (0, 2, 4, 6, 8, 10):
                    dbg_dump(pm2[blk % 2][i_].t[:, :], pm2[blk % 2][i_])
            if stop == "P":
                continue
            for hp in range(2):
                run(stage_R(hp, blk))
                if stop == "R":
                    continue
                run(stage_L(hp))
            if dbg and blk == nblk - 1:
                dbg_dump(rh[0].t[:, :], rh[0]); dbg_dump(kh[0].t[:, :], kh[0]); dbg_dump(ah[0].t[:, :], ah[0])
                dbg_dump(bh[0].t[:, :], bh[0])
            if stop in ("R", "L"):
                continue
            for j in range(2):
                run(stage_G(j, blk))
            run(stage_GA())
            if stop == "G":
                continue
            run(stage_C())
            if dbg and blk == nblk - 1 and stop == "C":
                dbg_dump(ysb[0].t[:, :], ysb[0])
            if stop == "C":
                continue
            run(stage_H())
            if dbg and blk == nblk - 1:
                dbg_dump(ysb[0].t[:, :], ysb[0])
                dbg_dump(osb[0].t[:, :], osb[0])
            if stop == "H":
                continue
            for hp in range(2):
                run(stage_Y(hp, blk))
            for j in range(2):
                run(stage_O(j, blk))
            if dbg and blk == nblk - 1:
                dbg_dump(yout[0].t[:, :], yout[0])
                dbg_dump(yout[2].t[:, :], yout[2])
            if phaseB and ((blk + 1) * TB) % 1024 == 0:
                gather(((blk + 1) * TB) // 1024 - 1)
    def chain_(*gens):
        for g_ in gens:
            for tag in g_:
                yield tag

    def nofill(gen):
        for _ in gen:
            yield "nofill"

    def rr_(*gens):
        gens = list(gens)
        while gens:
            for g_ in list(gens):
                try:
                    yield next(g_)
                except StopIteration:
                    gens.remove(g_)

    W_DONE = [False]

    def stage_W():
        W_DONE[0] = True
        for kt_ in range(KD):
            r_, m_ = kt_ // 4, kt_ % 4
            row0 = (r_ * 256 + m_ * 128) if m_ < 2 else (1024 + r_ * 256 + (m_ - 2) * 128)
            st = next_stage()
            DMA(lambda e, st=st, row0=row0: e.dma_start(out=st.t[:, 0:D], in_=wout_d[row0:row0 + 128, :]), [], [st])
            yield
            CP("dve" if kt_ % 2 == 0 else "act", wob3[:, kt_, :], st.t[:, 0:D], [st], b_wbf)
            yield

    def n_main_ops():
        return S.count["act"] + S.count["dve"] + S.count["pool"]

    def interleave(main, filler, ops_per_fill):
        fill_alive = True
        credit = 0.0
        last = n_main_ops()
        for tag in main:
            now = n_main_ops()
            if tag != "nofill":
                credit += (now - last) / ops_per_fill
            while fill_alive and credit >= 1.0:
                credit -= 1.0
                try:
                    next(filler)
                except StopIteration:
                    fill_alive = False
                now = n_main_ops()
            last = now
        if fill_alive:
            for _ in filler:
                pass

    def phaseA_pipelined():
        run(stage_X(0))
        run(stage_P(0))
        if NB > 1:
            run(stage_X(1))
        for blk in range(NB):
            fl = []
            if blk + 1 < NB:
                fl.append(stage_P(blk + 1))
            if blk + 2 < NB:
                fl.append(stage_X(blk + 2))
            if blk == NB - 1 and phaseB:
                fl.append(stage_W())
            main = chain_(stage_R(0, blk),
                          rr_(stage_L(0), stage_R(1, blk)),
                          rr_(stage_L(1), chain_(stage_G(0, blk), stage_G(1, blk), stage_GA())),
                          rr_(stage_C(),
                              chain_(stage_H(), stage_O(0, blk, tmps[4], tmps[5]), stage_O(1, blk, tmps[6], tmps[7]))),
                          rr_(stage_Y(0, blk, tmps[0], tmps[1]), stage_Y(1, blk, tmps[2], tmps[3])))
            interleave(main, chain_(*fl), FILL_OPS)
            if phaseB and ((blk + 1) * TB) % 1024 == 0:
                gather(((blk + 1) * TB) // 1024 - 1)

    if dbg or stop:
        phaseA()
    else:
        phaseA_pipelined()

    if not phaseB:
        fin = b_ybuf + ([b_dbg] if dbg else [])
        for e_ in ("sp", "pool", "act"):
            S.wait_all(e_, fin)
        S.emit()
        S.close()
        return nc
    if not W_DONE[0]:
        run(stage_W())
    tmp_b = [t.b for t in tmps]
    pid_cache = {}
    DMA(lambda e: e.dma_start(out=fgt_ap, in_=fg_d), [], tmp_b)
    yTq = [hn_raw[:, k_ * KD * 256:(k_ + 1) * KD * 256].rearrange("p (k t) -> p k t", t=256) for k_ in range(2)]
    for qt in range(4):
        kb = qt % 2
        for hh_ in range(2):
            def ld(e, qt=qt, hh_=hh_, kb=kb):
                if "g" not in pid_cache:
                    pid_cache["g"] = e.partition_id() % 4
                g_ = pid_cache["g"]
                src = yall_d[bass.ds(g_ * 2048 + hh_ * 1024, 1024), qt * 256:(qt + 1) * 256]
                return e.dma_start(out=yTq[kb][:, hh_ * 8:(hh_ + 1) * 8, :], in_=src.rearrange("(m p) c -> p m c", p=128))
            DMA(ld, [b_yall], [b_hn[kb]], eng="pool")
        for t2 in range(2):
            tt = qt * 2 + t2
            st = next_stage()
            DMA(lambda e, st=st, tt=tt: e.dma_start(out=st.t[:, 0:D], in_=xres_d[tt * 128:(tt + 1) * 128, :]), [], [st])
            h_ap = hsb_ap[tt % 2]
            h_b = hsb_b[tt % 2]
            for n_ in range(4):
                bp = Bp[n_ % 2]
                for kt_ in range(KD):
                    MM(bp.t[:, :], yTq[kb][:, kt_, t2 * 128:(t2 + 1) * 128], wob3[:, kt_, n_ * 512:(n_ + 1) * 512],
                       kt_ == 0, kt_ == KD - 1, [b_hn[kb]] + b_wbf, [bp])
                TT("dve", h_ap[:, n_ * 512:(n_ + 1) * 512], bp.t[:, :], st.t[:, n_ * 512:(n_ + 1) * 512], ALU.add,
                   [bp, st], h_b)
            ACT(xs[0].t[:, :], h_ap, AF.Square, h_b, [xs[0], ssx], accum=ssx.t[:, 0:1])
            ACT(ssx.t[:, 1:2], ssx.t[:, 0:1], AF.Ln, [ssx, epsn], [ssx], bias=epsn.t[:, 0:1], scale=1.0 / D)
            ACT(rstd_x.t[:, 0:1], ssx.t[:, 1:2], AF.Exp, [ssx], [rstd_x], scale=-0.5)
            STT(h_ap, h_ap, rstd_x.t[:, 0:1], fgt_ap, ALU.mult, ALU.mult, h_b + [rstd_x] + tmp_b, h_b)
            DMA(lambda e, h_ap=h_ap, tt=tt: e.dma_start(out=out_d[tt * 128:(tt + 1) * 128, :], in_=h_ap),
                h_b, [b_out])
    fin = [b_out] + ([b_dbg] if dbg else [])
    for e_ in ("sp", "pool", "act"):
        S.wait_all(e_, fin)
    S.emit()
    S.close()
    return nc


def _host_inputs(x, norm_g, w_in, mu, w0, w2, a0, a2, k_k, k_a, r_k, lnx_w, lnx_b, hgrn_norm_g, lb_param, w_out,
                 final_g):
    f = lambda a: np.ascontiguousarray(np.asarray(a), dtype=np.float32)
    x, norm_g, w_in, mu, w0, w2, a0, a2, k_k, k_a, r_k, lnx_w, lnx_b, hgrn_norm_g, lb_param, w_out, final_g = map(
        f, (x, norm_g, w_in, mu, w0, w2, a0, a2, k_k, k_a, r_k, lnx_w, lnx_b, hgrn_norm_g, lb_param, w_out, final_g))
    p = np.arange(128)
    j = np.arange(TB)
    cst = np.zeros((128, 5, TB), np.float32)
    pj, jj = (p % 64)[:, None], (j % 64)[None, :]
    cst[:, 0] = (jj > pj)
    cst[:, 1] = (jj >= pj)
    cst[:, 2] = (jj < pj)
    cst[:, 3] = (jj != 0)
    cst[:, 4] = (jj == pj)
    ones = np.zeros((128, 2, 128), np.float32)
    ones[:, 0] = ((p[:, None] // 64) == (p[None, :] // 64)) / 64.0
    ones[:, 1] = 1.0 / 128.0
    ident = np.eye(128, dtype=np.float32).astype(ml_dtypes.bfloat16)
    fg = np.ascontiguousarray(np.broadcast_to(final_g[None, :], (128, D)))
    wout = w_out[0]
    maps = []
    rk_flat = r_k[0].reshape(-1)
    for c in range(8):
        b, g = c // 4, c % 4
        cols = []
        for base in (0, 1024, 2048, 3072):
            cols.append(np.arange(base + g * 256, base + (g + 1) * 256))
        for base in (0, 1024, 2048, 3072):
            cols.append(np.arange(4288 + base + g * 256, 4288 + base + (g + 1) * 256))
        cols.append(np.arange(4096, 4288))
        cols = np.concatenate(cols)
        w = np.ascontiguousarray(w_in[0][:, cols])
        prm = np.zeros((128, NPRM), np.float32)
        prm[:, 0:16] = norm_g[0].reshape(16, 128).T
        for i, base in enumerate((0, 1024, 2048, 3072)):
            for hp in range(2):
                prm[:, 16 + 2 * i + hp] = mu[0][base + g * 256 + hp * 128 + p]
        prm[0:96, 24] = mu[0][4096:4192]
        prm[0:96, 25] = mu[0][4192:4288]
        for hp in range(2):
            ch = g * 256 + hp * 128 + p
            prm[:, 26 + hp] = w0[0][ch]
            prm[:, 28 + hp] = a0[0][ch]
            prm[:, 30 + hp] = k_k[0][ch]
            prm[:, 32 + hp] = k_a[0][ch]
            prm[:, 34 + hp] = rk_flat[ch]
            prm[:, 36 + hp] = lnx_w[0][ch]
            prm[:, 38 + hp] = lnx_b[0][ch]
            prm[:, 40 + hp] = hgrn_norm_g[0][ch]
            prm[:, 42 + hp] = lb_param[0][ch]
            prm[:, 44 + hp] = lb_param[1][ch]
        maps.append({
            "x": x[b], "xres": np.ascontiguousarray(x[b, g * 1024:(g + 1) * 1024]), "w": w, "prm": prm,
            "w2s": np.ascontiguousarray(w2[0][:, g * 256:(g + 1) * 256]),
            "a2s": np.ascontiguousarray(a2[0][:, g * 256:(g + 1) * 256]),
            "wout": wout, "fg": fg, "ident": ident, "cst": cst, "ones": ones,
        })
    return maps


_CACHE = {}


def kernel(**inputs):
    maps = _host_inputs(**inputs)
    nc = _get_program()
    res = run_bass_kernel_spmd(nc, maps, core_ids=list(range(8)))
    out = np.zeros((2, T_SEQ, D), np.float32)
    for c in range(8):
        b, g = c // 4, c % 4
        out[b, g * 1024:(g + 1) * 1024, :] = res.results[c]["out"]
    return out


def _get_program():
    if "nc" not in _CACHE:
        _CACHE["nc"] = build_program()
    return _CACHE["nc"]
```

```python
import math
import numpy as np
import ml_dtypes
import concourse.bass as bass
import concourse.mybir as mybir
from concourse.bass_utils import run_bass_kernel_spmd

F32 = mybir.dt.float32
BF16 = mybir.dt.bfloat16
ALU = mybir.AluOpType
AF = mybir.ActivationFunctionType

T_SEQ = 4096
D = 2048
KD = 16
TB = 256
FILL_OPS = 5.0
CPB = TB // 64
NT = TB // 128
NB = T_SEQ // TB
NCOL = 2240
NPRM = 46
CEXP = math.exp(-0.5)
NORM_EPS = 1e-6
LNX_EPS = 64e-5


class Buf:
    __slots__ = ("name", "last_write", "reads")

    def __init__(self, name=""):
        self.name = name
        self.last_write = None
        self.reads = []


class Sched:
    ENG = ("pe", "act", "dve", "pool", "sp")

    def __init__(self, nc, n_dma_sems=32, same_eng_sync=True):
        self.nc = nc
        self.same_eng_sync = same_eng_sync
        self.ops = {e: [] for e in self.ENG}
        self.count = {e: 0 for e in self.ENG}
        self.waited = {e: {} for e in self.ENG}
        self.sems = {}
        self._stack = []
        self.dma_sems = []
        for i in range(n_dma_sems):
            cm = nc.semaphore("dma_%d" % i)
            self.dma_sems.append([cm.__enter__(), 0])
            self._stack.append(cm)
        self.dma_rr = 0
        self.n_rot = n_dma_sems

    def close(self):
        for cm in reversed(self._stack):
            cm.__exit__(None, None, None)

    def _collect(self, eng, reads, writes):
        deps = []
        for b in reads:
            if b.last_write is not None:
                deps.append(b.last_write)
        for b in writes:
            if b.last_write is not None:
                deps.append(b.last_write)
            deps.extend(b.reads)
        wd = self.waited[eng]
        best = {}
        for (key, sem, val, src) in deps:
            if src == eng and (eng in ("pe", "sp") or not self.same_eng_sync):
                continue
            if src == eng and key == "e_%s_%d" % (eng, self.count[eng] // self.EPOCH) \
                    and (self.count[eng] % self.EPOCH) - val >= self.SAME_ENG_GAP:
                continue
            if wd.get(key, 0) >= val:
                continue
            wd[key] = val
            best[key] = (sem, val)
        return list(best.values())

    EPOCH = 2000
    SAME_ENG_GAP = 3

    def _esem(self, eng, epoch):
        key = (eng, epoch)
        if key not in self.sems:
            cm = self.nc.semaphore("prog_%s_%d" % (eng, epoch))
            self.sems[key] = cm.__enter__()
            self._stack.append(cm)
        return self.sems[key]

    def op(self, eng, fn, reads=(), writes=()):
        waits = self._collect(eng, reads, writes)
        epoch, pos = divmod(self.count[eng], self.EPOCH)
        self.count[eng] += 1
        sem = self._esem(eng, epoch)
        tok = ("e_%s_%d" % (eng, epoch), sem, pos + 1, eng)
        self.ops[eng].append((waits, fn, sem, 1))
        for b in reads:
            b.reads.append(tok)
        for b in writes:
            b.last_write = tok
            b.reads = []
        return tok

    def dma(self, eng, fn, reads=(), writes=(), inc=16):
        waits = self._collect(eng, reads, writes)
        if eng == "pool":
            cm = self.nc.semaphore("swdma_%d" % len(self.dma_sems))
            self.dma_sems.append([cm.__enter__(), 0])
            self._stack.append(cm)
            idx = len(self.dma_sems) - 1
        else:
            idx = self.dma_rr % self.n_rot
            self.dma_rr += 1
        slot = self.dma_sems[idx]
        key = "d_%d" % idx
        sem, cur = slot
        wd = self.waited[eng]
        if cur > 0 and wd.get(key, 0) < cur:
            wd[key] = cur
            waits.append((sem, cur))
        slot[1] = cur + inc
        tok = (key, sem, slot[1], None)
        self.ops[eng].append((waits, fn, sem, inc))
        for b in reads:
            b.reads.append(tok)
        for b in writes:
            b.last_write = tok
            b.reads = []
        return tok

    def wait_all(self, eng, bufs):
        waits = self._collect(eng, bufs, ())
        if waits:
            self.ops[eng].append((waits, None, None, 0))

    def emit(self):
        nc = self.nc
        ops = self.ops

        def run(e, lst):
            for waits, fn, sem, inc in lst:
                for s, v in waits:
                    e.wait_ge(s, v)
                if fn is not None:
                    fn(e).then_inc(sem, inc)

        with nc.Block() as block:
            @block.tensor
            def _(e):
                run(e, ops["pe"])

            @block.scalar
            def _(e):
                run(e, ops["act"])

            @block.vector
            def _(e):
                run(e, ops["dve"])

            @block.gpsimd
            def _(e):
                run(e, ops["pool"])

            @block.sync
            def _(e):
                run(e, ops["sp"])


class Tl:
    def __init__(self, t, name=""):
        self.t = t
        self.b = Buf(name)


def build_program(dbg=False, stop=None, phaseB=True, rlim=None):
    nc = bass.Bass("TRN2", target_bir_lowering=False)
    S = Sched(nc)

    def din(name, shape, dt=F32):
        return nc.dram_tensor(name, list(shape), dt, kind="ExternalInput").ap()

    x_d = din("x", [T_SEQ, D])
    w_d = din("w", [D, NCOL])
    prm_d = din("prm", [128, NPRM])
    w2_d = din("w2s", [96, 256])
    a2_d = din("a2s", [96, 256])
    wout_d = din("wout", [D, D])
    fg_d = din("fg", [128, D])
    ident_d = din("ident", [128, 128], BF16)
    cst_d = din("cst", [128, 5, TB])
    ones_d = din("ones", [128, 2, 128])
    out_d = nc.dram_tensor("out", [1024, D], F32, kind="ExternalOutput").ap()
    xres_d = din("xres", [1024, D])
    ybuf_d = [nc.dram_tensor("ybuf%d" % j_, [512, 1024], BF16) for j_ in range(4)]
    yall_d = nc.dram_tensor("yall", [8192, 1024], BF16)
    b_ybuf = [Buf("ybuf%d" % j_) for j_ in range(4)]
    b_yall = Buf("yall")
    b_out = Buf("out")
    if dbg:
        dbg_d = nc.dram_tensor("dbg", [16, 128, TB], F32, kind="ExternalOutput").ap()
        b_dbg = Buf("dbg")

    def sb(name, shape, dt=F32):
        return Tl(nc.alloc_sbuf_tensor("s_" + name, list(shape), dt), name)

    def ps(name, shape, dt=F32):
        return Tl(nc.alloc_psum_tensor("p_" + name, list(shape), dt), name)

    LIM = {"on": False, "n": None, "c": 0}
    BANK_LOCK = {}

    def OP(eng, fn, r, w):
        if LIM["on"] and LIM["n"] is not None:
            LIM["c"] += 1
            if LIM["c"] > LIM["n"]:
                return None
        rr = [t.b if isinstance(t, Tl) else t for t in r]
        ww = [t.b if isinstance(t, Tl) else t for t in w]
        if True:
            for b_ in rr + ww:
                lk = BANK_LOCK.get(id(b_))
                if lk is not None and lk not in ww:
                    ww.append(lk)
        return S.op(eng, fn, reads=rr, writes=ww)

    def DMA(fn, r, w, eng="sp", inc=16):
        return S.dma(eng, fn, reads=[t.b if isinstance(t, Tl) else t for t in r],
                     writes=[t.b if isinstance(t, Tl) else t for t in w], inc=inc)

    def TT(eng, out, in0, in1, op, r, w):
        OP(eng, lambda e: e.tensor_tensor(out=out, in0=in0, in1=in1, op=op), r, w)

    def TS(eng, out, in0, s1, s2, op0, op1, r, w):
        if op1 is None:
            OP(eng, lambda e: e.tensor_scalar(out=out, in0=in0, scalar1=s1, scalar2=None, op0=op0), r, w)
        else:
            OP(eng, lambda e: e.tensor_scalar(out=out, in0=in0, scalar1=s1, scalar2=s2, op0=op0, op1=op1), r, w)

    def STT(out, in0, scalar, in1, op0, op1, r, w):
        OP("dve", lambda e: e.scalar_tensor_tensor(out=out, in0=in0, scalar=scalar, in1=in1, op0=op0, op1=op1), r, w)

    def ACT(out, in_, func, r, w, bias=None, scale=None, accum=None, eng="act"):
        kw = {}
        if bias is not None:
            kw["bias"] = bias
        if scale is not None:
            kw["scale"] = scale
        if accum is not None:
            kw["accum_out"] = accum
        OP(eng, lambda e: e.activation(out=out, in_=in_, func=func, **kw), r, w)

    def SIG(out, in_, r, w, nbias=None, scale=1.0):
        ACT(out, in_, AF.Exp, r, w, bias=nbias, scale=-scale)
        ACT(out, out, AF.Ln, w + [epsn], w, bias=epsn.t[0:out.shape[0], 2:3])
        ACT(out, out, AF.Exp, w, w, scale=-1.0)

    def CP(eng, out, in_, r, w):
        if eng == "act":
            OP(eng, lambda e: e.activation(out=out, in_=in_, func=AF.Copy), r, w)
        else:
            OP(eng, lambda e: e.tensor_copy(out=out, in_=in_), r, w)

    def MM(out, lhsT, rhs, start, stop, r, w):
        OP("pe", lambda e: e.matmul(out, lhsT=lhsT, rhs=rhs, start=start, stop=stop), r, w)

    def TR(out, in_, r, w):
        OP("pe", lambda e: e.transpose(out, in_, ident.t[:, :]), list(r) + [ident], w)

    wbf = nc.alloc_sbuf_tensor("wbf", [128, KD * NCOL], BF16)
    b_wbf = [Buf("wbf%d" % i) for i in range(KD)]
    wbf3 = wbf[:, :].rearrange("p (k c) -> p k c", c=NCOL)
    wob3 = wbf[:, 0:KD * D].rearrange("p (k c) -> p k c", c=D)
    stage = [sb("stage%d" % i, [128, NCOL]) for i in range(2)]
    stage_rr = [0]

    def next_stage():
        s = stage[stage_rr[0] % 2]
        stage_rr[0] += 1
        return s

    xs = [sb("xs%d" % i, [128, D], BF16) for i in range(2)]
    hn_raw = nc.alloc_sbuf_tensor("hnT", [128, 2 * KD * TB], BF16)
    b_hn = [Buf("hn0"), Buf("hn1")]
    hnT = [hn_raw[:, i * KD * TB:(i + 1) * KD * TB].rearrange("p (k t) -> p k t", t=TB) for i in range(2)]
    yT3 = hn_raw[:, :].rearrange("p (k t) -> p k t", t=512)
    prm = sb("prm", [128, NPRM])
    drv = sb("drv", [128, 12])
    epsn = sb("epsn", [128, 3])
    ident = sb("ident", [128, 128], BF16)
    cst = sb("cst", [128, 5, TB])
    ones = sb("ones", [128, 2, 128])
    w2b = sb("w2b", [96, 512], BF16)
    MS = cst.t[:, 0, :]
    MI = cst.t[:, 1, :]
    MST = cst.t[:, 2, :]
    RST = cst.t[:, 3, :]
    EYE = cst.t[:, 4, :]
    BONES = ones.t[:, 0, :]
    AONES = ones.t[:, 1, :]

    lastcol = sb("lastcol", [128, 10])
    PR = [sb("PR%d" % i, [128, TB + 1]) for i in range(3)]
    pm_arena = nc.alloc_sbuf_tensor("pm_arena", [128, 32 * TB], F32)
    pm2 = [[Tl(pm_arena[:, (k * 16 + i) * TB:(k * 16 + i + 1) * TB], "pm%d_%d" % (k, i)) for i in range(16)]
           for k in range(2)]
    pm = pm2[0]
    twb2 = [sb("twb%d" % k, [96, TB], BF16) for k in range(2)]
    adb2 = [sb("adb%d" % k, [96, TB], BF16) for k in range(2)]
    tP = sb("tP", [128, TB])
    ssx = sb("ssx", [128, 2])
    rstd_x = sb("rstdx", [128, 2])

    def mk(name, dt=F32, n=2, shape=None):
        return [sb("%s%d" % (name, i), shape or [128, TB], dt) for i in range(n)]

    rh, kh, bh, ah, kt, bt, vb = [mk(n_, BF16) for n_ in ("rh", "kh", "bh", "ah", "kt", "bt", "vb")]
    tokall = mk("tokall", BF16, shape=[128, 3 * CPB, 64])
    ktok = [Tl(tokall[i].t[:, 0:CPB, :]) for i in range(2)]
    btok = [Tl(tokall[i].t[:, CPB:2 * CPB, :]) for i in range(2)]
    vtok = [Tl(tokall[i].t[:, 2 * CPB:3 * CPB, :]) for i in range(2)]
    for i in range(2):
        ktok[i].b = btok[i].b = vtok[i].b = tokall[i].b
    gmC = mk("gmC", shape=[128, CPB])
    bonus = mk("bonus")
    ysb_all = sb("ysb_all", [128, 2, TB])
    ysb = [Tl(ysb_all.t[:, i, :]) for i in range(2)]
    for i in range(2):
        ysb[i].b = ysb_all.b
    Lab, LabT, LakT, LrbT, LrkT = [mk(n_, BF16) for n_ in ("Lab", "LabT", "LakT", "LrbT", "LrkT")]
    Pp = [mk("Pa", BF16), mk("Pb", BF16)]
    PTp = [mk("PTa", BF16), mk("PTb", BF16)]
    TTp = [mk("TTa", BF16), mk("TTb", BF16)]
    tmp_arena = nc.alloc_sbuf_tensor("tmp_arena", [128, 8 * TB], F32)
    tA, tB, tC, tD, tE, tF, tG, tH = [Tl(tmp_arena[:, i * TB:(i + 1) * TB], "tmp%d" % i) for i in range(8)]
    tmps = [tA, tB, tC, tD, tE, tF, tG, tH]
    fgt_ap = tmp_arena[:, 0:D]
    Hf = sb("Hf", [128, 128])
    Hb = sb("Hb", [128, 128], BF16)
    Wb = sb("Wb", [128, 256], BF16)
    Ub = sb("Ub", [128, 256], BF16)
    yout = mk("yout", BF16, n=4)
    qh, gkh, gkd, gvb = [mk(n_, BF16) for n_ in ("qh", "gkh", "gkd", "gvb")]
    gpad = [[sb("gpad%d_%d" % (j_, par_), [128, 2 * NT, 128], BF16) for par_ in range(2)] for j_ in range(2)]
    ggC = mk("ggC", shape=[128, CPB])
    ATb = sb("ATb", [128, TB], BF16)
    osb_all = sb("osb_all", [128, 2, TB])
    osb = [Tl(osb_all.t[:, i, :]) for i in range(2)]
    for i in range(2):
        osb[i].b = osb_all.b
    Sf = sb("Sf", [128, 256])
    Sb_ = sb("Sb", [128, 256], BF16)
    hsb_ap = [pm_arena[:, i * D:(i + 1) * D] for i in range(2)]
    hsb_b = [[pm[k].b for k in range(i * (D // TB), (i + 1) * (D // TB))] for i in range(2)]

    Bp = [ps("Bp%d" % i, [128, 512]) for i in range(2)]
    Bt = ps("Bt", [128, 1024], BF16)
    Bg = [ps("Bg%d" % i, [128, 512]) for i in range(2)]
    B6 = nc.alloc_psum_tensor("B6", [128, 512], F32)
    b_H = Buf("psH")
    b_Y = [Buf("psY%d" % i) for i in range(3)]
    B7 = nc.alloc_psum_tensor("B7", [128, 512], F32)
    b_S = Buf("psS")
    b_O = [Buf("psO%d" % i) for i in range(2)]
    B5 = nc.alloc_psum_tensor("B5", [128, 512], F32)
    b_W = Buf("psW")
    b_U = Buf("psU")
    for grp in ([Bp[0].b], [Bp[1].b], [Bt.b], [Bg[0].b], [Bg[1].b], [b_W, b_U], [b_H] + b_Y, [b_S] + b_O):
        lk_ = Buf("lock")
        for b_ in grp:
            BANK_LOCK[id(b_)] = lk_
    bg_rr = [0]

    def next_bg():
        b = Bg[bg_rr[0] % 2]
        bg_rr[0] += 1
        return b

    DMA(lambda e: e.dma_start(out=prm.t[:, :], in_=prm_d), [], [prm])
    DMA(lambda e: e.dma_start(out=ident.t[:, :], in_=ident_d), [], [ident])
    DMA(lambda e: e.dma_start(out=cst.t[:, :, :], in_=cst_d), [], [cst])
    DMA(lambda e: e.dma_start(out=ones.t[:, :, :], in_=ones_d), [], [ones])
    st0 = next_stage()
    DMA(lambda e: e.dma_start(out=st0.t[0:96, 0:256], in_=w2_d), [], [st0])
    DMA(lambda e: e.dma_start(out=st0.t[0:96, 256:512], in_=a2_d), [], [st0])
    CP("pool", w2b.t[:, :], st0.t[0:96, 0:512], [st0], [w2b])
    OP("pool", lambda e: e.memset(lastcol.t[:, :], 0.0), [], [lastcol])
    OP("pool", lambda e: e.memset(Hf.t[:, :], 0.0), [], [Hf])
    OP("pool", lambda e: e.memset(Hb.t[:, :], 0.0), [], [Hb])
    OP("pool", lambda e: e.memset(Sf.t[:, :], 0.0), [], [Sf])
    OP("pool", lambda e: e.memset(Sb_.t[:, :], 0.0), [], [Sb_])
    for j_ in range(2):
        for par_ in range(2):
            g_ = gpad[j_][par_]
            OP("pool", lambda e, g_=g_: e.memset(g_.t[:, :, :], 0.0), [], [g_])
    OP("pool", lambda e: e.memset(epsn.t[:, 0:1], NORM_EPS), [], [epsn])
    OP("pool", lambda e: e.memset(epsn.t[:, 1:2], LNX_EPS), [], [epsn])
    OP("pool", lambda e: e.memset(epsn.t[:, 2:3], 1.0), [], [epsn])
    TS("dve", drv.t[:, 0:2], prm.t[:, 32:34], -1.0, 1.0, ALU.mult, ALU.add, [prm], [drv])
    TT("dve", drv.t[:, 6:8], prm.t[:, 42:44], prm.t[:, 44:46], ALU.subtract, [prm], [drv])
    SIG(drv.t[:, 2:4], drv.t[:, 6:8], [drv], [drv])
    TS("dve", drv.t[:, 8:12], prm.t[:, 26:30], -1.0, None, ALU.mult, None, [prm], [drv])
    TS("dve", drv.t[:, 4:6], drv.t[:, 2:4], -1.0, 1.0, ALU.mult, ALU.add, [drv], [drv])

    for dc in range(KD):
        st = next_stage()
        DMA(lambda e, st=st, dc=dc: e.dma_start(out=st.t[:, :], in_=w_d[dc * 128:(dc + 1) * 128, :]), [], [st])
        if dc % 2 == 0:
            TS("dve", wbf3[:, dc, :], st.t[:, :], prm.t[:, dc:dc + 1], None, ALU.mult, None, [st, prm], [b_wbf[dc]])
        else:
            ACT(wbf3[:, dc, :], st.t[:, :], AF.Copy, [st, prm], [b_wbf[dc]], scale=prm.t[:, dc:dc + 1])

    NXT = T_SEQ // 128

    def x_dma(t):
        st = stage[t % 2]
        DMA(lambda e: e.dma_start(out=st.t[:, 0:D], in_=x_d[t * 128:(t + 1) * 128, :]), [], [st])

    def stage_X(blk):
        hb = blk % 2
        for tt in range(NT):
            t = blk * NT + tt
            if t == 0:
                x_dma(0)
            if t + 1 < NXT:
                x_dma(t + 1)
            st = stage[t % 2]
            xsb = xs[tt % 2]
            OP("dve", lambda e, st=st, xsb=xsb: e.scalar_tensor_tensor(
                out=xsb.t[:, :], in0=st.t[:, 0:D], scalar=1.0, in1=st.t[:, 0:D], op0=ALU.mult, op1=ALU.mult,
                accum_out=ssx.t[:, 0:1]), [st], [xsb, ssx])
            yield
            ACT(ssx.t[:, 1:2], ssx.t[:, 0:1], AF.Ln, [ssx, epsn], [ssx], bias=epsn.t[:, 0:1], scale=1.0 / D)
            ACT(rstd_x.t[:, 0:1], ssx.t[:, 1:2], AF.Exp, [ssx], [rstd_x], scale=-0.5)
            ACT(xsb.t[:, :], st.t[:, 0:D], AF.Copy, [st, rstd_x], [xsb], scale=rstd_x.t[:, 0:1])
            yield
            for half in range(2):
                for j in range(8):
                    dc = half * 8 + j
                    TR(Bt.t[:, j * 128:(j + 1) * 128], xsb.t[:, dc * 128:(dc + 1) * 128], [xsb], [Bt])
                CP("act" if half == 0 else "dve",
                   hnT[hb][:, half * 8:(half + 1) * 8, tt * 128:(tt + 1) * 128],
                   Bt.t[:, :].rearrange("p (k t) -> p k t", t=128), [Bt], [b_hn[hb]])
                yield

    COLOFF = [i * 128 for i in range(16)] + [2048, 2144]
    CTM = [128] * 16 + [96, 96]

    def stage_P(blk):
        hb = blk % 2
        pmw, twb, adb = pm2[blk % 2], twb2[blk % 2], adb2[blk % 2]
        order = [16, 17, 0, 1, 2, 3, 4, 5, 6, 7, 10, 11, 8, 9, 12, 13, 14, 15]

        def evac(n_, ct):
            M = CTM[ct]
            bp = Bp[n_ % 2]
            if ct < 8 or ct >= 16:
                li = ct if ct < 8 else ct - 8
                mucol = 16 + li
                pr_ = PR[n_ % 3]
                CP("pool", pr_.t[0:M, 0:1], lastcol.t[0:M, li:li + 1], [lastcol], [pr_])
                CP("act" if n_ % 2 else "dve", pr_.t[0:M, 1:TB + 1], bp.t[0:M, 0:TB], [bp], [pr_])
                CP("pool", lastcol.t[0:M, li:li + 1], pr_.t[0:M, TB:TB + 1], [pr_], [lastcol])
                TT("pool", tP.t[0:M, :], pr_.t[0:M, 0:TB], pr_.t[0:M, 1:TB + 1], ALU.subtract, [pr_], [tP])
                if ct < 8:
                    STT(pmw[ct].t[:, :], tP.t[:, :], prm.t[:, mucol:mucol + 1], pr_.t[:, 1:TB + 1], ALU.mult, ALU.add,
                        [tP, prm, pr_], [pmw[ct]])
                elif ct == 16:
                    STT(tP.t[0:96, :], tP.t[0:96, :], prm.t[0:96, mucol:mucol + 1], pr_.t[0:96, 1:TB + 1],
                        ALU.mult, ALU.add, [tP, prm, pr_], [tP])
                    SIG(tP.t[0:96, :], tP.t[0:96, :], [tP], [tP], scale=2.0)
                    TS("dve", twb.t[:, :], tP.t[0:96, :], 2.0, -1.0, ALU.mult, ALU.add, [tP], [twb])
                else:
                    STT(adb.t[:, :], tP.t[0:96, :], prm.t[0:96, mucol:mucol + 1], pr_.t[0:96, 1:TB + 1],
                        ALU.mult, ALU.add, [tP, prm, pr_], [adb])
            else:
                CP("act" if n_ % 2 else "dve", pmw[ct].t[:, :], bp.t[:, 0:TB], [bp], [pmw[ct]])

        pending = None
        for n_, ct in enumerate(order):
            M = CTM[ct]
            bp = Bp[n_ % 2]
            for dc in range(KD):
                MM(bp.t[0:M, 0:TB], wbf3[:, dc, COLOFF[ct]:COLOFF[ct] + M], hnT[hb][:, dc, :], dc == 0, dc == KD - 1,
                   [b_wbf[dc], b_hn[hb]], [bp])
                if dc == KD // 2 - 1:
                    yield
            if pending is not None:
                evac(*pending)
            pending = (n_, ct)
            yield
        evac(*pending)
        yield

    c3 = lambda ap: ap.rearrange("p (c t) -> p c t", t=64)
    tok3 = lambda n: Bt.t[:, n * NT * 128:(n + 1) * NT * 128].rearrange("p (k t) -> p k t", t=128)

    def stage_R(hp, blk):
        pm, twb, adb = pm2[blk % 2], twb2[blk % 2], adb2[blk % 2]
        p_r, p_k, p_v = pm[hp], pm[2 + hp], pm[4 + hp]
        if stop == "R":
            LIM["on"] = True
            LIM["n"] = rlim
            LIM["c"] = 0
        bz = next_bg()
        MM(bz.t[:, 0:TB], w2b.t[:, hp * 128:(hp + 1) * 128], twb.t[:, :], True, True, [w2b, twb], [bz])
        SIG(tA.t[:, :], bz.t[:, 0:TB], [bz, drv], [tA], nbias=drv.t[:, 8 + hp:9 + hp])
        bz2 = next_bg()
        MM(bz2.t[:, 0:TB], w2b.t[:, 256 + hp * 128:256 + (hp + 1) * 128], adb.t[:, :], True, True, [w2b, adb], [bz2])
        SIG(tB.t[:, :], bz2.t[:, 0:TB], [bz2, drv], [tB], nbias=drv.t[:, 10 + hp:11 + hp])
        OP("dve", lambda e: e.tensor_tensor_scan(out=tC.t[:, :], data0=RST, data1=tA.t[:, :], initial=0.0,
                                                  op0=ALU.mult, op1=ALU.add), [cst, tA], [tC])
        ACT(tH.t[:, :], tC.t[:, :], AF.Exp, [tC], [tH], scale=-CEXP)
        CP("pool", gmC[hp].t[:, :], c3(tH.t[:, :])[:, :, 63], [tH], [gmC[hp]])
        ACT(tD.t[:, :], tC.t[:, :], AF.Exp, [tC], [tD], scale=CEXP)
        yield
        TT("pool", tE.t[:, :], tC.t[:, :], tA.t[:, :], ALU.subtract, [tC, tA], [tE])
        ACT(tE.t[:, :], tE.t[:, :], AF.Exp, [tE], [tE], scale=-CEXP)
        TT("dve", c3(tF.t[:, :]), c3(tC.t[:, :])[:, :, 63:64].to_broadcast([128, CPB, 64]), c3(tC.t[:, :]),
           ALU.subtract, [tC], [tF])
        ACT(tF.t[:, :], tF.t[:, :], AF.Exp, [tF], [tF], scale=-CEXP)
        yield
        TS("dve", tG.t[:, :], p_k.t[:, :], prm.t[:, 30 + hp:31 + hp], None, ALU.mult, None, [p_k, prm], [tG])
        ACT(tA.t[:, :], tG.t[:, :], AF.Square, [tG], [tA])
        bz3 = next_bg()
        MM(bz3.t[:, 0:TB], BONES, tA.t[:, :], True, True, [ones, tA], [bz3])
        ACT(tA.t[:, :], bz3.t[:, 0:TB], AF.Ln, [bz3], [tA], scale=64.0)
        TS("dve", tA.t[:, :], tA.t[:, :], 0.5, math.log(1e-12), ALU.mult, ALU.max, [tA], [tA])
        ACT(tA.t[:, :], tA.t[:, :], AF.Exp, [tA], [tA], scale=-1.0)
        TT("dve", tG.t[:, :], tG.t[:, :], tA.t[:, :], ALU.mult, [tG, tA], [tG])
        TS("dve", tA.t[:, :], tB.t[:, :], prm.t[:, 32 + hp:33 + hp], drv.t[:, hp:hp + 1], ALU.mult, ALU.add,
           [tB, prm, drv], [tA])
        TT("dve", tA.t[:, :], tA.t[:, :], p_k.t[:, :], ALU.mult, [tA, p_k], [tA])
        TT("pool", tB.t[:, :], tG.t[:, :], tB.t[:, :], ALU.mult, [tG, tB], [tB])
        yield
        TT("dve", rh[hp].t[:, :], p_r.t[:, :], tH.t[:, :], ALU.mult, [p_r, tH], [rh[hp]])
        TT("dve", kh[hp].t[:, :], tA.t[:, :], tD.t[:, :], ALU.mult, [tA, tD], [kh[hp]])
        TT("pool", bh[hp].t[:, :], tB.t[:, :], tD.t[:, :], ALU.mult, [tB, tD], [bh[hp]])
        STT(ah[hp].t[:, :], tG.t[:, :], -1.0, tE.t[:, :], ALU.mult, ALU.mult, [tG, tE], [ah[hp]])
        TT("dve", kt[hp].t[:, :], tA.t[:, :], tF.t[:, :], ALU.mult, [tA, tF], [kt[hp]])
        TT("pool", bt[hp].t[:, :], tB.t[:, :], tF.t[:, :], ALU.mult, [tB, tF], [bt[hp]])
        CP("act", vb[hp].t[:, :], p_v.t[:, :], [p_v], [vb[hp]])
        STT(tD.t[:, :], p_r.t[:, :], prm.t[:, 34 + hp:35 + hp], tA.t[:, :], ALU.mult, ALU.mult, [p_r, prm, tA], [tD])
        bz4 = next_bg()
        MM(bz4.t[:, 0:TB], BONES, tD.t[:, :], True, True, [ones, tD], [bz4])
        STT(bonus[hp].t[:, :], bz4.t[:, 0:TB], 64.0, p_v.t[:, :], ALU.mult, ALU.mult, [bz4, p_v], [bonus[hp]])
        yield
        for n, src in enumerate((kt[hp], bt[hp], vb[hp])):
            for ck in range(CPB):
                for h in range(2):
                    ph = slice(h * 64, (h + 1) * 64)
                    c0 = (n * CPB + ck) * 64
                    OP("pe", lambda e, ph=ph, c0=c0, src=src, ck=ck: e.transpose(
                        Bt.t[ph, c0:c0 + 64], src.t[ph, ck * 64:(ck + 1) * 64], ident.t[ph, ph]), [src, ident], [Bt])
        CP("act", tokall[hp].t[:, :, :], Bt.t[:, 0:3 * CPB * 64].rearrange("p (k t) -> p k t", t=64), [Bt],
           [tokall[hp]])
        LIM["on"] = False
        yield

    NQ = CPB // 2
    LW = CPB * 64

    def lcol(ck):
        return slice(ck * 64, (ck + 1) * 64)

    def lmat(dst_bg, lhs_t, rhs_t):
        for h in range(2):
            ph = slice(h * 64, (h + 1) * 64)
            for ck in range(CPB):
                cs = slice(ck * 64, (ck + 1) * 64)
                MM(dst_bg.t[ph, lcol(ck)], lhs_t.t[ph, cs], rhs_t.t[ph, cs], True, True, [lhs_t, rhs_t], [dst_bg])

    def lsq(dst_bg, lhs_t, rhs_t):
        for h in range(2):
            ph = slice(h * 64, (h + 1) * 64)
            for ck in range(CPB):
                MM(dst_bg.t[ph, lcol(ck)], lhs_t.t[ph, lcol(ck)], rhs_t.t[ph, lcol(ck)], True, True,
                   [lhs_t, rhs_t], [dst_bg])

    Tfinal = [None, None]

    def stage_L(hp):
        if stop == "L":
            LIM["on"] = True
            LIM["n"] = rlim
            LIM["c"] = 0
        for (dst, lt, rt, msk) in ((LabT, bh, ah, MS), (Lab, ah, bh, MST), (LakT, kh, ah, MS),
                                   (LrbT, bh, rh, MI), (LrkT, kh, rh, MI)):
            bgx = next_bg()
            lmat(bgx, lt[hp], rt[hp])
            TT("dve", dst[hp].t[:, :], bgx.t[:, 0:LW], msk, ALU.mult, [bgx, cst], [dst[hp]])
            yield
        P, PT, Tt = Lab[hp], LabT[hp], TTp[0][hp]
        TT("pool", Tt.t[:, :], PT.t[:, :], EYE, ALU.add, [PT, cst], [Tt])
        for j in range(5):
            Pn = Pp[j % 2][hp]
            PTn = PTp[j % 2][hp]
            b1 = next_bg()
            lsq(b1, PT, P)
            CP("act", Pn.t[:, :], b1.t[:, 0:LW], [b1], [Pn])
            if j < 4:
                b2 = next_bg()
                lsq(b2, P, PT)
                CP("dve", PTn.t[:, :], b2.t[:, 0:LW], [b2], [PTn])
            yield
            b3 = next_bg()
            Tn = TTp[(j + 1) % 2][hp]
            lsq(b3, Pn, Tt)
            TT("dve", Tn.t[:, :], b3.t[:, 0:LW], Tt.t[:, :], ALU.add, [b3, Tt], [Tn])
            P, PT, Tt = Pn, PTn, Tn
            yield
        Tfinal[hp] = Tt
        LIM["on"] = False

    def stage_C():
        for ck in range(CPB):
            cs = slice(ck * 64, (ck + 1) * 64)
            ys = ck % 3
            ycol = 128 + ys * 128
            for hp in range(2):
                hc = slice(hp * 64, (hp + 1) * 64)
                for h in range(2):
                    ph = slice(h * 64, (h + 1) * 64)
                    MM(B5[ph, hc], ah[hp].t[ph, cs], Hb.t[ph, hc], True, False, [ah[hp], Hb], [b_W])
                    MM(B5[ph, hc], LakT[hp].t[ph, lcol(ck)], vtok[hp].t[ph, ck, :], False, True,
                       [LakT[hp], vtok[hp]], [b_W])
            CP("dve", Wb.t[:, 0:128], B5[:, 0:128], [b_W], [Wb])
            for hp in range(2):
                hc = slice(hp * 64, (hp + 1) * 64)
                Tf = Tfinal[hp]
                for h in range(2):
                    ph = slice(h * 64, (h + 1) * 64)
                    MM(B5[ph, 256 + hp * 64:256 + (hp + 1) * 64], Tf.t[ph, lcol(ck)], Wb.t[ph, hc], True, True,
                       [Tf, Wb], [b_U])
            CP("act", Ub.t[:, 0:128], B5[:, 256:384], [b_U], [Ub])
            yield
            for hp in range(2):
                hc = slice(hp * 64, (hp + 1) * 64)
                for h in range(2):
                    ph = slice(h * 64, (h + 1) * 64)
                    MM(B6[ph, hc], btok[hp].t[ph, ck, :], Ub.t[ph, hc], True, False, [btok[hp], Ub], [b_H])
                    MM(B6[ph, hc], ktok[hp].t[ph, ck, :], vtok[hp].t[ph, ck, :], False, True,
                       [ktok[hp], vtok[hp]], [b_H])
            for hp in range(2):
                hc = slice(hp * 64, (hp + 1) * 64)
                for h in range(2):
                    ph = slice(h * 64, (h + 1) * 64)
                    yo = B6[ph, ycol + hp * 64:ycol + (hp + 1) * 64]
                    MM(yo, Hb.t[ph, hc], rh[hp].t[ph, cs], True, False, [Hb, rh[hp]], [b_Y[ys]])
                    MM(yo, Ub.t[ph, hc], LrbT[hp].t[ph, lcol(ck)], False, False, [Ub, LrbT[hp]], [b_Y[ys]])
                    MM(yo, vtok[hp].t[ph, ck, :], LrkT[hp].t[ph, lcol(ck)], False, True, [vtok[hp], LrkT[hp]],
                       [b_Y[ys]])
            for hp in range(2):
                STT(Hf.t[:, hp * 64:(hp + 1) * 64], Hf.t[:, hp * 64:(hp + 1) * 64],
                    gmC[hp].t[:, ck:ck + 1], B6[:, hp * 64:(hp + 1) * 64], ALU.mult, ALU.add,
                    [Hf, gmC[hp], b_H], [Hf])
            CP("act", Hb.t[:, :], Hf.t[:, :], [Hf], [Hb])
            CP("act", ysb_all.t[:, :, cs], B6[:, ycol:ycol + 128].rearrange("p (h t) -> p h t", t=64), [b_Y[ys]],
               [ysb_all])
            yield

    def ystore(m, blk, src):
        tok0 = blk * TB
        j_ = tok0 // 1024
        r0 = m * 128
        c0 = tok0 % 1024
        DMA(lambda e: e.dma_start(out=ybuf_d[j_][r0:r0 + 128, c0:c0 + TB], in_=src.t[:, :]), [src], [b_ybuf[j_]],
            eng="act")

    def stage_Y(hp, blk, tA=None, tB=None):
        tA = tA or tmps[0]
        tB = tB or tmps[1]
        p_g = pm2[blk % 2][6 + hp]
        bm = next_bg()
        MM(bm.t[:, 0:TB], BONES, ysb[hp].t[:, :], True, True, [ones, ysb[hp]], [bm])
        TT("dve", tA.t[:, :], ysb[hp].t[:, :], bm.t[:, 0:TB], ALU.subtract, [ysb[hp], bm], [tA])
        ACT(tB.t[:, :], tA.t[:, :], AF.Square, [tA], [tB])
        yield
        bv = next_bg()
        MM(bv.t[:, 0:TB], BONES, tB.t[:, :], True, True, [ones, tB], [bv])
        ACT(tB.t[:, :], bv.t[:, 0:TB], AF.Ln, [bv, epsn], [tB], bias=epsn.t[:, 1:2])
        ACT(tB.t[:, :], tB.t[:, :], AF.Exp, [tB], [tB], scale=-0.5)
        TT("dve", tA.t[:, :], tA.t[:, :], tB.t[:, :], ALU.mult, [tA, tB], [tA])
        yield
        TS("dve", tA.t[:, :], tA.t[:, :], prm.t[:, 36 + hp:37 + hp], prm.t[:, 38 + hp:39 + hp], ALU.mult, ALU.add,
           [tA, prm], [tA])
        TT("pool", tA.t[:, :], tA.t[:, :], bonus[hp].t[:, :], ALU.add, [tA, bonus[hp]], [tA])
        SIG(tB.t[:, :], p_g.t[:, :], [p_g], [tB])
        TT("pool", tB.t[:, :], tB.t[:, :], p_g.t[:, :], ALU.mult, [tB, p_g], [tB])
        TT("dve", yout[hp].t[:, :], tA.t[:, :], tB.t[:, :], ALU.mult, [tA, tB], [yout[hp]])
        ystore(hp, blk, yout[hp])
        yield

    def stage_G(j, blk):
        pm = pm2[blk % 2]
        p_q, p_f, p_i = pm[8 + j], pm[10 + j], pm[12 + j]
        SIG(tA.t[:, :], p_f.t[:, :], [p_f], [tA])
        TS("dve", tA.t[:, :], tA.t[:, :], drv.t[:, 4 + j:5 + j], drv.t[:, 2 + j:3 + j], ALU.mult, ALU.add,
           [tA, drv], [tA])
        TS("pool", tB.t[:, :], tA.t[:, :], -1.0, 1.0, ALU.mult, ALU.add, [tA], [tB])
        ACT(tA.t[:, :], tA.t[:, :], AF.Ln, [tA], [tA])
        yield
        OP("dve", lambda e: e.tensor_tensor_scan(out=tC.t[:, :], data0=RST, data1=tA.t[:, :], initial=0.0,
                                                  op0=ALU.mult, op1=ALU.add), [cst, tA], [tC])
        ACT(tH.t[:, :], tC.t[:, :], AF.Exp, [tC], [tH])
        CP("pool", ggC[j].t[:, :], c3(tH.t[:, :])[:, :, 63], [tH], [ggC[j]])
        ACT(tD.t[:, :], tC.t[:, :], AF.Exp, [tC], [tD], scale=-1.0)
        TT("dve", c3(tF.t[:, :]), c3(tC.t[:, :])[:, :, 63:64].to_broadcast([128, CPB, 64]), c3(tC.t[:, :]),
           ALU.subtract, [tC], [tF])
        ACT(tF.t[:, :], tF.t[:, :], AF.Exp, [tF], [tF])
        yield
        TT("dve", qh[j].t[:, :], p_q.t[:, :], tH.t[:, :], ALU.mult, [p_q, tH], [qh[j]])
        TT("pool", gkh[j].t[:, :], tB.t[:, :], tD.t[:, :], ALU.mult, [tB, tD], [gkh[j]])
        TT("dve", gkd[j].t[:, :], tB.t[:, :], tF.t[:, :], ALU.mult, [tB, tF], [gkd[j]])
        CP("act", gvb[j].t[:, :], p_i.t[:, :], [p_i], [gvb[j]])
        yield
        for n, src in enumerate((gkd[j], gvb[j])):
            for jj in range(NT):
                TR(Bt.t[:, (n * NT + jj) * 128:(n * NT + jj + 1) * 128], src.t[:, jj * 128:(jj + 1) * 128], [src], [Bt])
        for par in range(2):
            pp = slice(par * 64, (par + 1) * 64)
            CP("act", gpad[j][par].t[pp, :, :], Bt.t[pp, 0:2 * NT * 128].rearrange("p (k t) -> p k t", t=128), [Bt],
               [gpad[j][par]])
        yield

    def gcol(j, ck):
        return slice((j * NQ + ck // 2) * 64, (j * NQ + ck // 2 + 1) * 64)

    def stage_GA():
        bga = next_bg()
        for j in range(2):
            for ck in range(CPB):
                par = ck % 2
                cs = slice(ck * 64, (ck + 1) * 64)
                MM(bga.t[par * 64:(par + 1) * 64, gcol(j, ck)], gkh[j].t[:, cs], qh[j].t[:, cs], True, True,
                   [gkh[j], qh[j]], [bga])
        TT("dve", ATb.t[:, 0:2 * NQ * 64], bga.t[:, 0:2 * NQ * 64], MI[:, 0:2 * NQ * 64], ALU.mult, [bga, cst], [ATb])
        yield

    def stage_H():
        if stop == "H":
            LIM["on"] = True
            LIM["n"] = rlim
            LIM["c"] = 0
        for ck in range(CPB):
            par, q = ck % 2, ck // 2
            cs = slice(ck * 64, (ck + 1) * 64)
            osl = ck % 2
            ocol = 256 + osl * 128
            for j in range(2):
                gp = gpad[j][par]
                MM(B7[:, ocol + j * 64:ocol + (j + 1) * 64], Sb_.t[:, j * 128:(j + 1) * 128], qh[j].t[:, cs], True, False,
                   [Sb_, qh[j]], [b_O[osl]])
                MM(B7[:, ocol + j * 64:ocol + (j + 1) * 64], gp.t[:, NT + q, :], ATb.t[:, gcol(j, ck)], False, True,
                   [gp, ATb], [b_O[osl]])
            for j in range(2):
                gp = gpad[j][par]
                MM(B7[:, j * 128:(j + 1) * 128], gp.t[:, q, :], gp.t[:, NT + q, :], True, True, [gp], [b_S])
            for j in range(2):
                STT(Sf.t[:, j * 128:(j + 1) * 128], Sf.t[:, j * 128:(j + 1) * 128],
                    ggC[j].t[:, ck:ck + 1], B7[:, j * 128:(j + 1) * 128], ALU.mult, ALU.add,
                    [Sf, ggC[j], b_S], [Sf])
            CP("act", Sb_.t[:, :], Sf.t[:, :], [Sf], [Sb_])
            CP("dve", osb_all.t[:, :, cs], B7[:, ocol:ocol + 128].rearrange("p (h t) -> p h t", t=64), [b_O[osl]],
               [osb_all])
            yield

    def stage_O(j, blk, tA=None, tB=None):
        tA = tA or tmps[0]
        tB = tB or tmps[1]
        p_g = pm2[blk % 2][14 + j]
        ACT(tA.t[:, :], osb[j].t[:, :], AF.Square, [osb[j]], [tA])
        bm = next_bg()
        MM(bm.t[:, 0:TB], AONES, tA.t[:, :], True, True, [ones, tA], [bm])
        ACT(tA.t[:, :], bm.t[:, 0:TB], AF.Ln, [bm, epsn], [tA], bias=epsn.t[:, 0:1])
        ACT(tA.t[:, :], tA.t[:, :], AF.Exp, [tA], [tA], scale=-0.5)
        TT("dve", tA.t[:, :], tA.t[:, :], osb[j].t[:, :], ALU.mult, [tA, osb[j]], [tA])
        yield
        SIG(tB.t[:, :], p_g.t[:, :], [p_g], [tB])
        TT("pool", tB.t[:, :], tB.t[:, :], p_g.t[:, :], ALU.mult, [tB, p_g], [tB])
        STT(yout[2 + j].t[:, :], tA.t[:, :], prm.t[:, 40 + j:41 + j], tB.t[:, :], ALU.mult, ALU.mult,
            [tA, prm, tB], [yout[2 + j]])
        ystore(2 + j, blk, yout[2 + j])
        yield

    def run(gen):
        for _ in gen:
            pass

    def gather(j_):
        S.wait_all("pool", [b_ybuf[j_]])
        DMA(lambda e: e.collective_compute("AllGather", ALU.bypass, replica_groups=[[0, 1, 2, 3], [4, 5, 6, 7]],
                                           ins=[ybuf_d[j_].ap().opt()],
                                           outs=[yall_d[j_ * 2048:(j_ + 1) * 2048, :].opt()]),
            [b_ybuf[j_]], [b_yall], eng="pool", inc=1)

    dbg_n = [0]

    if dbg:
        dbgt = sb("dbgt", [128, TB])

    def dbg_dump(ap, tl, np_=128, w=TB):
        if dbg:
            slot = dbg_n[0]
            dbg_n[0] += 1
            if ap.dtype != F32:
                CP("dve", dbgt.t[0:np_, 0:w], ap, [tl], [dbgt])
                DMA(lambda e: e.dma_start(out=dbg_d[slot, 0:np_, 0:w], in_=dbgt.t[0:np_, 0:w]), [dbgt], [b_dbg])
            else:
                DMA(lambda e: e.dma_start(out=dbg_d[slot, 0:np_, 0:w], in_=ap), [tl], [b_dbg])

    nblk = NB if not dbg else dbg

    def phaseA():
        run(stage_X(0))
        if stop == "X":
            return
        for blk in range(nblk):
            run(stage_P(blk))
            if blk + 1 < nblk:
                run(stage_X(blk + 1))
            if dbg and blk == nblk - 1:
                for i_ in (0, 2, 4, 6, 8, 10):
                    dbg_dump(pm2[blk % 2][i_].t[:, :], pm2[blk % 2][i_])
            if stop == "P":
                continue
            for hp in range(2):
                run(stage_R(hp, blk))
                if stop == "R":
                    continue
                run(stage_L(hp))
            if dbg and blk == nblk - 1:
                dbg_dump(rh[0].t[:, :], rh[0]); dbg_dump(kh[0].t[:, :], kh[0]); dbg_dump(ah[0].t[:, :], ah[0])
                dbg_dump(bh[0].t[:, :], bh[0])
            if stop in ("R", "L"):
                continue
            for j in range(2):
                run(stage_G(j, blk))
            run(stage_GA())
            if stop == "G":
                continue
            run(stage_C())
            if dbg and blk == nblk - 1 and stop == "C":
                dbg_dump(ysb[0].t[:, :], ysb[0])
            if stop == "C":
                continue
            run(stage_H())
            if dbg and blk == nblk - 1:
                dbg_dump(ysb[0].t[:, :], ysb[0])
                dbg_dump(osb[0].t[:, :], osb[0])
            if stop == "H":
                continue
            for hp in range(2):
                run(stage_Y(hp, blk))
            for j in range(2):
                run(stage_O(j, blk))
            if dbg and blk == nblk - 1:
                dbg_dump(yout[0].t[:, :], yout[0])
                dbg_dump(yout[2].t[:, :], yout[2])
            if phaseB and ((blk + 1) * TB) % 1024 == 0:
                gather(((blk + 1) * TB) // 1024 - 1)
    def chain_(*gens):
        for g_ in gens:
            for tag in g_:
                yield tag

    def nofill(gen):
        for _ in gen:
            yield "nofill"

    def rr_(*gens):
        gens = list(gens)
        while gens:
            for g_ in list(gens):
                try:
                    yield next(g_)
                except StopIteration:
                    gens.remove(g_)

    W_DONE = [False]

    def stage_W():
        W_DONE[0] = True
        for kt_ in range(KD):
            r_, m_ = kt_ // 4, kt_ % 4
            row0 = (r_ * 256 + m_ * 128) if m_ < 2 else (1024 + r_ * 256 + (m_ - 2) * 128)
            st = next_stage()
            DMA(lambda e, st=st, row0=row0: e.dma_start(out=st.t[:, 0:D], in_=wout_d[row0:row0 + 128, :]), [], [st])
            yield
            CP("dve" if kt_ % 2 == 0 else "act", wob3[:, kt_, :], st.t[:, 0:D], [st], b_wbf)
            yield

    def n_main_ops():
        return S.count["act"] + S.count["dve"] + S.count["pool"]

    def interleave(main, filler, ops_per_fill):
        fill_alive = True
        credit = 0.0
        last = n_main_ops()
        for tag in main:
            now = n_main_ops()
            if tag != "nofill":
                credit += (now - last) / ops_per_fill
            while fill_alive and credit >= 1.0:
                credit -= 1.0
                try:
                    next(filler)
                except StopIteration:
                    fill_alive = False
                now = n_main_ops()
            last = now
        if fill_alive:
            for _ in filler:
                pass

    def phaseA_pipelined():
        run(stage_X(0))
        run(stage_P(0))
        if NB > 1:
            run(stage_X(1))
        for blk in range(NB):
            fl = []
            if blk + 1 < NB:
                fl.append(stage_P(blk + 1))
            if blk + 2 < NB:
                fl.append(stage_X(blk + 2))
            if blk == NB - 1 and phaseB:
                fl.append(stage_W())
            main = chain_(stage_R(0, blk),
                          rr_(stage_L(0), stage_R(1, blk)),
                          rr_(stage_L(1), chain_(stage_G(0, blk), stage_G(1, blk), stage_GA())),
                          rr_(stage_C(),
                              chain_(stage_H(), stage_O(0, blk, tmps[4], tmps[5]), stage_O(1, blk, tmps[6], tmps[7]))),
                          rr_(stage_Y(0, blk, tmps[0], tmps[1]), stage_Y(1, blk, tmps[2], tmps[3])))
            interleave(main, chain_(*fl), FILL_OPS)
            if phaseB and ((blk + 1) * TB) % 1024 == 0:
                gather(((blk + 1) * TB) // 1024 - 1)

    if dbg or stop:
        phaseA()
    else:
        phaseA_pipelined()

    if not phaseB:
        fin = b_ybuf + ([b_dbg] if dbg else [])
        for e_ in ("sp", "pool", "act"):
            S.wait_all(e_, fin)
        S.emit()
        S.close()
        return nc
    if not W_DONE[0]:
        run(stage_W())
    tmp_b = [t.b for t in tmps]
    pid_cache = {}
    DMA(lambda e: e.dma_start(out=fgt_ap, in_=fg_d), [], tmp_b)
    yTq = [hn_raw[:, k_ * KD * 256:(k_ + 1) * KD * 256].rearrange("p (k t) -> p k t", t=256) for k_ in range(2)]
    for qt in range(4):
        kb = qt % 2
        for hh_ in range(2):
            def ld(e, qt=qt, hh_=hh_, kb=kb):
                if "g" not in pid_cache:
                    pid_cache["g"] = e.partition_id() % 4
                g_ = pid_cache["g"]
                src = yall_d[bass.ds(g_ * 2048 + hh_ * 1024, 1024), qt * 256:(qt + 1) * 256]
                return e.dma_start(out=yTq[kb][:, hh_ * 8:(hh_ + 1) * 8, :], in_=src.rearrange("(m p) c -> p m c", p=128))
            DMA(ld, [b_yall], [b_hn[kb]], eng="pool")
        for t2 in range(2):
            tt = qt * 2 + t2
            st = next_stage()
            DMA(lambda e, st=st, tt=tt: e.dma_start(out=st.t[:, 0:D], in_=xres_d[tt * 128:(tt + 1) * 128, :]), [], [st])
            h_ap = hsb_ap[tt % 2]
            h_b = hsb_b[tt % 2]
            for n_ in range(4):
                bp = Bp[n_ % 2]
                for kt_ in range(KD):
                    MM(bp.t[:, :], yTq[kb][:, kt_, t2 * 128:(t2 + 1) * 128], wob3[:, kt_, n_ * 512:(n_ + 1) * 512],
                       kt_ == 0, kt_ == KD - 1, [b_hn[kb]] + b_wbf, [bp])
                TT("dve", h_ap[:, n_ * 512:(n_ + 1) * 512], bp.t[:, :], st.t[:, n_ * 512:(n_ + 1) * 512], ALU.add,
                   [bp, st], h_b)
            ACT(xs[0].t[:, :], h_ap, AF.Square, h_b, [xs[0], ssx], accum=ssx.t[:, 0:1])
            ACT(ssx.t[:, 1:2], ssx.t[:, 0:1], AF.Ln, [ssx, epsn], [ssx], bias=epsn.t[:, 0:1], scale=1.0 / D)
            ACT(rstd_x.t[:, 0:1], ssx.t[:, 1:2], AF.Exp, [ssx], [rstd_x], scale=-0.5)
            STT(h_ap, h_ap, rstd_x.t[:, 0:1], fgt_ap, ALU.mult, ALU.mult, h_b + [rstd_x] + tmp_b, h_b)
            DMA(lambda e, h_ap=h_ap, tt=tt: e.dma_start(out=out_d[tt * 128:(tt + 1) * 128, :], in_=h_ap),
                h_b, [b_out])
    fin = [b_out] + ([b_dbg] if dbg else [])
    for e_ in ("sp", "pool", "act"):
        S.wait_all(e_, fin)
    S.emit()
    S.close()
    return nc


def _host_inputs(x, norm_g, w_in, mu, w0, w2, a0, a2, k_k, k_a, r_k, lnx_w, lnx_b, hgrn_norm_g, lb_param, w_out,
                 final_g):
    f = lambda a: np.ascontiguousarray(np.asarray(a), dtype=np.float32)
    x, norm_g, w_in, mu, w0, w2, a0, a2, k_k, k_a, r_k, lnx_w, lnx_b, hgrn_norm_g, lb_param, w_out, final_g = map(
        f, (x, norm_g, w_in, mu, w0, w2, a0, a2, k_k, k_a, r_k, lnx_w, lnx_b, hgrn_norm_g, lb_param, w_out, final_g))
    p = np.arange(128)
    j = np.arange(TB)
    cst = np.zeros((128, 5, TB), np.float32)
    pj, jj = (p % 64)[:, None], (j % 64)[None, :]
    cst[:, 0] = (jj > pj)
    cst[:, 1] = (jj >= pj)
    cst[:, 2] = (jj < pj)
    cst[:, 3] = (jj != 0)
    cst[:, 4] = (jj == pj)
    ones = np.zeros((128, 2, 128), np.float32)
    ones[:, 0] = ((p[:, None] // 64) == (p[None, :] // 64)) / 64.0
    ones[:, 1] = 1.0 / 128.0
    ident = np.eye(128, dtype=np.float32).astype(ml_dtypes.bfloat16)
    fg = np.ascontiguousarray(np.broadcast_to(final_g[None, :], (128, D)))
    wout = w_out[0]
    maps = []
    rk_flat = r_k[0].reshape(-1)
    for c in range(8):
        b, g = c // 4, c % 4
        cols = []
        for base in (0, 1024, 2048, 3072):
            cols.append(np.arange(base + g * 256, base + (g + 1) * 256))
        for base in (0, 1024, 2048, 3072):
            cols.append(np.arange(4288 + base + g * 256, 4288 + base + (g + 1) * 256))
        cols.append(np.arange(4096, 4288))
        cols = np.concatenate(cols)
        w = np.ascontiguousarray(w_in[0][:, cols])
        prm = np.zeros((128, NPRM), np.float32)
        prm[:, 0:16] = norm_g[0].reshape(16, 128).T
        for i, base in enumerate((0, 1024, 2048, 3072)):
            for hp in range(2):
                prm[:, 16 + 2 * i + hp] = mu[0][base + g * 256 + hp * 128 + p]
        prm[0:96, 24] = mu[0][4096:4192]
        prm[0:96, 25] = mu[0][4192:4288]
        for hp in range(2):
            ch = g * 256 + hp * 128 + p
            prm[:, 26 + hp] = w0[0][ch]
            prm[:, 28 + hp] = a0[0][ch]
            prm[:, 30 + hp] = k_k[0][ch]
            prm[:, 32 + hp] = k_a[0][ch]
            prm[:, 34 + hp] = rk_flat[ch]
            prm[:, 36 + hp] = lnx_w[0][ch]
            prm[:, 38 + hp] = lnx_b[0][ch]
            prm[:, 40 + hp] = hgrn_norm_g[0][ch]
            prm[:, 42 + hp] = lb_param[0][ch]
            prm[:, 44 + hp] = lb_param[1][ch]
        maps.append({
            "x": x[b], "xres": np.ascontiguousarray(x[b, g * 1024:(g + 1) * 1024]), "w": w, "prm": prm,
            "w2s": np.ascontiguousarray(w2[0][:, g * 256:(g + 1) * 256]),
            "a2s": np.ascontiguousarray(a2[0][:, g * 256:(g + 1) * 256]),
            "wout": wout, "fg": fg, "ident": ident, "cst": cst, "ones": ones,
        })
    return maps


_CACHE = {}


def kernel(**inputs):
    maps = _host_inputs(**inputs)
    nc = _get_program()
    res = run_bass_kernel_spmd(nc, maps, core_ids=list(range(8)))
    out = np.zeros((2, T_SEQ, D), np.float32)
    for c in range(8):
        b, g = c // 4, c % 4
        out[b, g * 1024:(g + 1) * 1024, :] = res.results[c]["out"]
    return out


def _get_program():
    if "nc" not in _CACHE:
        _CACHE["nc"] = build_program()
    return _CACHE["nc"]
```

```python
import math
import numpy as np
import ml_dtypes
import concourse.bass as bass
import concourse.mybir as mybir
from concourse.bass_utils import run_bass_kernel_spmd

F32 = mybir.dt.float32
BF16 = mybir.dt.bfloat16
ALU = mybir.AluOpType
AF = mybir.ActivationFunctionType

T_SEQ = 4096
D = 2048
KD = 16
TB = 256
FILL_OPS = 5.0
CPB = TB // 64
NT = TB // 128
NB = T_SEQ // TB
NCOL = 2240
NPRM = 46
CEXP = math.exp(-0.5)
NORM_EPS = 1e-6
LNX_EPS = 64e-5


class Buf:
    __slots__ = ("name", "last_write", "reads")

    def __init__(self, name=""):
        self.name = name
        self.last_write = None
        self.reads = []


class Sched:
    ENG = ("pe", "act", "dve", "pool", "sp")

    def __init__(self, nc, n_dma_sems=32, same_eng_sync=True):
        self.nc = nc
        self.same_eng_sync = same_eng_sync
        self.ops = {e: [] for e in self.ENG}
        self.count = {e: 0 for e in self.ENG}
        self.waited = {e: {} for e in self.ENG}
        self.sems = {}
        self._stack = []
        self.dma_sems = []
        for i in range(n_dma_sems):
            cm = nc.semaphore("dma_%d" % i)
            self.dma_sems.append([cm.__enter__(), 0])
            self._stack.append(cm)
        self.dma_rr = 0
        self.n_rot = n_dma_sems

    def close(self):
        for cm in reversed(self._stack):
            cm.__exit__(None, None, None)

    def _collect(self, eng, reads, writes):
        deps = []
        for b in reads:
            if b.last_write is not None:
                deps.append(b.last_write)
        for b in writes:
            if b.last_write is not None:
                deps.append(b.last_write)
            deps.extend(b.reads)
        wd = self.waited[eng]
        best = {}
        for (key, sem, val, src) in deps:
            if src == eng and (eng in ("pe", "sp") or not self.same_eng_sync):
                continue
            if src == eng and key == "e_%s_%d" % (eng, self.count[eng] // self.EPOCH) \
                    and (self.count[eng] % self.EPOCH) - val >= self.SAME_ENG_GAP:
                continue
            if wd.get(key, 0) >= val:
                continue
            wd[key] = val
            best[key] = (sem, val)
        return list(best.values())

    EPOCH = 2000
    SAME_ENG_GAP = 3

    def _esem(self, eng, epoch):
        key = (eng, epoch)
        if key not in self.sems:
            cm = self.nc.semaphore("prog_%s_%d" % (eng, epoch))
            self.sems[key] = cm.__enter__()
            self._stack.append(cm)
        return self.sems[key]

    def op(self, eng, fn, reads=(), writes=()):
        waits = self._collect(eng, reads, writes)
        epoch, pos = divmod(self.count[eng], self.EPOCH)
        self.count[eng] += 1
        sem = self._esem(eng, epoch)
        tok = ("e_%s_%d" % (eng, epoch), sem, pos + 1, eng)
        self.ops[eng].append((waits, fn, sem, 1))
        for b in reads:
            b.reads.append(tok)
        for b in writes:
            b.last_write = tok
            b.reads = []
        return tok

    def dma(self, eng, fn, reads=(), writes=(), inc=16):
        waits = self._collect(eng, reads, writes)
        if eng == "pool":
            cm = self.nc.semaphore("swdma_%d" % len(self.dma_sems))
            self.dma_sems.append([cm.__enter__(), 0])
            self._stack.append(cm)
            idx = len(self.dma_sems) - 1
        else:
            idx = self.dma_rr % self.n_rot
            self.dma_rr += 1
        slot = self.dma_sems[idx]
        key = "d_%d" % idx
        sem, cur = slot
        wd = self.waited[eng]
        if cur > 0 and wd.get(key, 0) < cur:
            wd[key] = cur
            waits.append((sem, cur))
        slot[1] = cur + inc
        tok = (key, sem, slot[1], None)
        self.ops[eng].append((waits, fn, sem, inc))
        for b in reads:
            b.reads.append(tok)
        for b in writes:
            b.last_write = tok
            b.reads = []
        return tok

    def wait_all(self, eng, bufs):
        waits = self._collect(eng, bufs, ())
        if waits:
            self.ops[eng].append((waits, None, None, 0))

    def emit(self):
        nc = self.nc
        ops = self.ops

        def run(e, lst):
            for waits, fn, sem, inc in lst:
                for s, v in waits:
                    e.wait_ge(s, v)
                if fn is not None:
                    fn(e).then_inc(sem, inc)

        with nc.Block() as block:
            @block.tensor
            def _(e):
                run(e, ops["pe"])

            @block.scalar
            def _(e):
                run(e, ops["act"])

            @block.vector
            def _(e):
                run(e, ops["dve"])

            @block.gpsimd
            def _(e):
                run(e, ops["pool"])

            @block.sync
            def _(e):
                run(e, ops["sp"])


class Tl:
    def __init__(self, t, name=""):
        self.t = t
        self.b = Buf(name)


def build_program(dbg=False, stop=None, phaseB=True, rlim=None):
    nc = bass.Bass("TRN2", target_bir_lowering=False)
    S = Sched(nc)

    def din(name, shape, dt=F32):
        return nc.dram_tensor(name, list(shape), dt, kind="ExternalInput").ap()

    x_d = din("x", [T_SEQ, D])
    w_d = din("w", [D, NCOL])
    prm_d = din("prm", [128, NPRM])
    w2_d = din("w2s", [96, 256])
    a2_d = din("a2s", [96, 256])
    wout_d = din("wout", [D, D])
    fg_d = din("fg", [128, D])
    ident_d = din("ident", [128, 128], BF16)
    cst_d = din("cst", [128, 5, TB])
    ones_d = din("ones", [128, 2, 128])
    out_d = nc.dram_tensor("out", [1024, D], F32, kind="ExternalOutput").ap()
    xres_d = din("xres", [1024, D])
    ybuf_d = [nc.dram_tensor("ybuf%d" % j_, [512, 1024], BF16) for j_ in range(4)]
    yall_d = nc.dram_tensor("yall", [8192, 1024], BF16)
    b_ybuf = [Buf("ybuf%d" % j_) for j_ in range(4)]
    b_yall = Buf("yall")
    b_out = Buf("out")
    if dbg:
        dbg_d = nc.dram_tensor("dbg", [16, 128, TB], F32, kind="ExternalOutput").ap()
        b_dbg = Buf("dbg")

    def sb(name, shape, dt=F32):
        return Tl(nc.alloc_sbuf_tensor("s_" + name, list(shape), dt), name)

    def ps(name, shape, dt=F32):
        return Tl(nc.alloc_psum_tensor("p_" + name, list(shape), dt), name)

    LIM = {"on": False, "n": None, "c": 0}
    BANK_LOCK = {}

    def OP(eng, fn, r, w):
        if LIM["on"] and LIM["n"] is not None:
            LIM["c"] += 1
            if LIM["c"] > LIM["n"]:
                return None
        rr = [t.b if isinstance(t, Tl) else t for t in r]
        ww = [t.b if isinstance(t, Tl) else t for t in w]
        if True:
            for b_ in rr + ww:
                lk = BANK_LOCK.get(id(b_))
                if lk is not None and lk not in ww:
                    ww.append(lk)
        return S.op(eng, fn, reads=rr, writes=ww)

    def DMA(fn, r, w, eng="sp", inc=16):
        return S.dma(eng, fn, reads=[t.b if isinstance(t, Tl) else t for t in r],
                     writes=[t.b if isinstance(t, Tl) else t for t in w], inc=inc)

    def TT(eng, out, in0, in1, op, r, w):
        OP(eng, lambda e: e.tensor_tensor(out=out, in0=in0, in1=in1, op=op), r, w)

    def TS(eng, out, in0, s1, s2, op0, op1, r, w):
        if op1 is None:
            OP(eng, lambda e: e.tensor_scalar(out=out, in0=in0, scalar1=s1, scalar2=None, op0=op0), r, w)
        else:
            OP(eng, lambda e: e.tensor_scalar(out=out, in0=in0, scalar1=s1, scalar2=s2, op0=op0, op1=op1), r, w)

    def STT(out, in0, scalar, in1, op0, op1, r, w):
        OP("dve", lambda e: e.scalar_tensor_tensor(out=out, in0=in0, scalar=scalar, in1=in1, op0=op0, op1=op1), r, w)

    def ACT(out, in_, func, r, w, bias=None, scale=None, accum=None, eng="act"):
        kw = {}
        if bias is not None:
            kw["bias"] = bias
        if scale is not None:
            kw["scale"] = scale
        if accum is not None:
            kw["accum_out"] = accum
        OP(eng, lambda e: e.activation(out=out, in_=in_, func=func, **kw), r, w)

    def SIG(out, in_, r, w, nbias=None, scale=1.0):
        ACT(out, in_, AF.Exp, r, w, bias=nbias, scale=-scale)
        ACT(out, out, AF.Ln, w + [epsn], w, bias=epsn.t[0:out.shape[0], 2:3])
        ACT(out, out, AF.Exp, w, w, scale=-1.0)

    def CP(eng, out, in_, r, w):
        if eng == "act":
            OP(eng, lambda e: e.activation(out=out, in_=in_, func=AF.Copy), r, w)
        else:
            OP(eng, lambda e: e.tensor_copy(out=out, in_=in_), r, w)

    def MM(out, lhsT, rhs, start, stop, r, w):
        OP("pe", lambda e: e.matmul(out, lhsT=lhsT, rhs=rhs, start=start, stop=stop), r, w)

    def TR(out, in_, r, w):
        OP("pe", lambda e: e.transpose(out, in_, ident.t[:, :]), list(r) + [ident], w)

    wbf = nc.alloc_sbuf_tensor("wbf", [128, KD * NCOL], BF16)
    b_wbf = [Buf("wbf%d" % i) for i in range(KD)]
    wbf3 = wbf[:, :].rearrange("p (k c) -> p k c", c=NCOL)
    wob3 = wbf[:, 0:KD * D].rearrange("p (k c) -> p k c", c=D)
    stage = [sb("stage%d" % i, [128, NCOL]) for i in range(2)]
    stage_rr = [0]

    def next_stage():
        s = stage[stage_rr[0] % 2]
        stage_rr[0] += 1
        return s

    xs = [sb("xs%d" % i, [128, D], BF16) for i in range(2)]
    hn_raw = nc.alloc_sbuf_tensor("hnT", [128, 2 * KD * TB], BF16)
    b_hn = [Buf("hn0"), Buf("hn1")]
    hnT = [hn_raw[:, i * KD * TB:(i + 1) * KD * TB].rearrange("p (k t) -> p k t", t=TB) for i in range(2)]
    yT3 = hn_raw[:, :].rearrange("p (k t) -> p k t", t=512)
    prm = sb("prm", [128, NPRM])
    drv = sb("drv", [128, 12])
    epsn = sb("epsn", [128, 3])
    ident = sb("ident", [128, 128], BF16)
    cst = sb("cst", [128, 5, TB])
    ones = sb("ones", [128, 2, 128])
    w2b = sb("w2b", [96, 512], BF16)
    MS = cst.t[:, 0, :]
    MI = cst.t[:, 1, :]
    MST = cst.t[:, 2, :]
    RST = cst.t[:, 3, :]
    EYE = cst.t[:, 4, :]
    BONES = ones.t[:, 0, :]
    AONES = ones.t[:, 1, :]

    lastcol = sb("lastcol", [128, 10])
    PR = [sb("PR%d" % i, [128, TB + 1]) for i in range(3)]
    pm_arena = nc.alloc_sbuf_tensor("pm_arena", [128, 32 * TB], F32)
    pm2 = [[Tl(pm_arena[:, (k * 16 + i) * TB:(k * 16 + i + 1) * TB], "pm%d_%d" % (k, i)) for i in range(16)]
           for k in range(2)]
    pm = pm2[0]
    twb2 = [sb("twb%d" % k, [96, TB], BF16) for k in range(2)]
    adb2 = [sb("adb%d" % k, [96, TB], BF16) for k in range(2)]
    tP = sb("tP", [128, TB])
    ssx = sb("ssx", [128, 2])
    rstd_x = sb("rstdx", [128, 2])

    def mk(name, dt=F32, n=2, shape=None):
        return [sb("%s%d" % (name, i), shape or [128, TB], dt) for i in range(n)]

    rh, kh, bh, ah, kt, bt, vb = [mk(n_, BF16) for n_ in ("rh", "kh", "bh", "ah", "kt", "bt", "vb")]
    tokall = mk("tokall", BF16, shape=[128, 3 * CPB, 64])
    ktok = [Tl(tokall[i].t[:, 0:CPB, :]) for i in range(2)]
    btok = [Tl(tokall[i].t[:, CPB:2 * CPB, :]) for i in range(2)]
    vtok = [Tl(tokall[i].t[:, 2 * CPB:3 * CPB, :]) for i in range(2)]
    for i in range(2):
        ktok[i].b = btok[i].b = vtok[i].b = tokall[i].b
    gmC = mk("gmC", shape=[128, CPB])
    bonus = mk("bonus")
    ysb_all = sb("ysb_all", [128, 2, TB])
    ysb = [Tl(ysb_all.t[:, i, :]) for i in range(2)]
    for i in range(2):
        ysb[i].b = ysb_all.b
    Lab, LabT, LakT, LrbT, LrkT = [mk(n_, BF16) for n_ in ("Lab", "LabT", "LakT", "LrbT", "LrkT")]
    Pp = [mk("Pa", BF16), mk("Pb", BF16)]
    PTp = [mk("PTa", BF16), mk("PTb", BF16)]
    TTp = [mk("TTa", BF16), mk("TTb", BF16)]
    tmp_arena = nc.alloc_sbuf_tensor("tmp_arena", [128, 8 * TB], F32)
    tA, tB, tC, tD, tE, tF, tG, tH = [Tl(tmp_arena[:, i * TB:(i + 1) * TB], "tmp%d" % i) for i in range(8)]
    tmps = [tA, tB, tC, tD, tE, tF, tG, tH]
    fgt_ap = tmp_arena[:, 0:D]
    Hf = sb("Hf", [128, 128])
    Hb = sb("Hb", [128, 128], BF16)
    Wb = sb("Wb", [128, 256], BF16)
    Ub = sb("Ub", [128, 256], BF16)
    yout = mk("yout", BF16, n=4)
    qh, gkh, gkd, gvb = [mk(n_, BF16) for n_ in ("qh", "gkh", "gkd", "gvb")]
    gpad = [[sb("gpad%d_%d" % (j_, par_), [128, 2 * NT, 128], BF16) for par_ in range(2)] for j_ in range(2)]
    ggC = mk("ggC", shape=[128, CPB])
    ATb = sb("ATb", [128, TB], BF16)
    osb_all = sb("osb_all", [128, 2, TB])
    osb = [Tl(osb_all.t[:, i, :]) for i in range(2)]
    for i in range(2):
        osb[i].b = osb_all.b
    Sf = sb("Sf", [128, 256])
    Sb_ = sb("Sb", [128, 256], BF16)
    hsb_ap = [pm_arena[:, i * D:(i + 1) * D] for i in range(2)]
    hsb_b = [[pm[k].b for k in range(i * (D // TB), (i + 1) * (D // TB))] for i in range(2)]

    Bp = [ps("Bp%d" % i, [128, 512]) for i in range(2)]
    Bt = ps("Bt", [128, 1024], BF16)
    Bg = [ps("Bg%d" % i, [128, 512]) for i in range(2)]
    B6 = nc.alloc_psum_tensor("B6", [128, 512], F32)
    b_H = Buf("psH")
    b_Y = [Buf("psY%d" % i) for i in range(3)]
    B7 = nc.alloc_psum_tensor("B7", [128, 512], F32)
    b_S = Buf("psS")
    b_O = [Buf("psO%d" % i) for i in range(2)]
    B5 = nc.alloc_psum_tensor("B5", [128, 512], F32)
    b_W = Buf("psW")
    b_U = Buf("psU")
    for grp in ([Bp[0].b], [Bp[1].b], [Bt.b], [Bg[0].b], [Bg[1].b], [b_W, b_U], [b_H] + b_Y, [b_S] + b_O):
        lk_ = Buf("lock")
        for b_ in grp:
            BANK_LOCK[id(b_)] = lk_
    bg_rr = [0]

    def next_bg():
        b = Bg[bg_rr[0] % 2]
        bg_rr[0] += 1
        return b

    DMA(lambda e: e.dma_start(out=prm.t[:, :], in_=prm_d), [], [prm])
    DMA(lambda e: e.dma_start(out=ident.t[:, :], in_=ident_d), [], [ident])
    DMA(lambda e: e.dma_start(out=cst.t[:, :, :], in_=cst_d), [], [cst])
    DMA(lambda e: e.dma_start(out=ones.t[:, :, :], in_=ones_d), [], [ones])
    st0 = next_stage()
    DMA(lambda e: e.dma_start(out=st0.t[0:96, 0:256], in_=w2_d), [], [st0])
    DMA(lambda e: e.dma_start(out=st0.t[0:96, 256:512], in_=a2_d), [], [st0])
    CP("pool", w2b.t[:, :], st0.t[0:96, 0:512], [st0], [w2b])
    OP("pool", lambda e: e.memset(lastcol.t[:, :], 0.0), [], [lastcol])
    OP("pool", lambda e: e.memset(Hf.t[:, :], 0.0), [], [Hf])
    OP("pool", lambda e: e.memset(Hb.t[:, :], 0.0), [], [Hb])
    OP("pool", lambda e: e.memset(Sf.t[:, :], 0.0), [], [Sf])
    OP("pool", lambda e: e.memset(Sb_.t[:, :], 0.0), [], [Sb_])
    for j_ in range(2):
        for par_ in range(2):
            g_ = gpad[j_][par_]
            OP("pool", lambda e, g_=g_: e.memset(g_.t[:, :, :], 0.0), [], [g_])
    OP("pool", lambda e: e.memset(epsn.t[:, 0:1], NORM_EPS), [], [epsn])
    OP("pool", lambda e: e.memset(epsn.t[:, 1:2], LNX_EPS), [], [epsn])
    OP("pool", lambda e: e.memset(epsn.t[:, 2:3], 1.0), [], [epsn])
    TS("dve", drv.t[:, 0:2], prm.t[:, 32:34], -1.0, 1.0, ALU.mult, ALU.add, [prm], [drv])
    TT("dve", drv.t[:, 6:8], prm.t[:, 42:44], prm.t[:, 44:46], ALU.subtract, [prm], [drv])
    SIG(drv.t[:, 2:4], drv.t[:, 6:8], [drv], [drv])
    TS("dve", drv.t[:, 8:12], prm.t[:, 26:30], -1.0, None, ALU.mult, None, [prm], [drv])
    TS("dve", drv.t[:, 4:6], drv.t[:, 2:4], -1.0, 1.0, ALU.mult, ALU.add, [drv], [drv])

    for dc in range(KD):
        st = next_stage()
        DMA(lambda e, st=st, dc=dc: e.dma_start(out=st.t[:, :], in_=w_d[dc * 128:(dc + 1) * 128, :]), [], [st])
        if dc % 2 == 0:
            TS("dve", wbf3[:, dc, :], st.t[:, :], prm.t[:, dc:dc + 1], None, ALU.mult, None, [st, prm], [b_wbf[dc]])
        else:
            ACT(wbf3[:, dc, :], st.t[:, :], AF.Copy, [st, prm], [b_wbf[dc]], scale=prm.t[:, dc:dc + 1])

    NXT = T_SEQ // 128

    def x_dma(t):
        st = stage[t % 2]
        DMA(lambda e: e.dma_start(out=st.t[:, 0:D], in_=x_d[t * 128:(t + 1) * 128, :]), [], [st])

    def stage_X(blk):
        hb = blk % 2
        for tt in range(NT):
            t = blk * NT + tt
            if t == 0:
                x_dma(0)
            if t + 1 < NXT:
                x_dma(t + 1)
            st = stage[t % 2]
            xsb = xs[tt % 2]
            OP("dve", lambda e, st=st, xsb=xsb: e.scalar_tensor_tensor(
                out=xsb.t[:, :], in0=st.t[:, 0:D], scalar=1.0, in1=st.t[:, 0:D], op0=ALU.mult, op1=ALU.mult,
                accum_out=ssx.t[:, 0:1]), [st], [xsb, ssx])
            yield
            ACT(ssx.t[:, 1:2], ssx.t[:, 0:1], AF.Ln, [ssx, epsn], [ssx], bias=epsn.t[:, 0:1], scale=1.0 / D)
            ACT(rstd_x.t[:, 0:1], ssx.t[:, 1:2], AF.Exp, [ssx], [rstd_x], scale=-0.5)
            ACT(xsb.t[:, :], st.t[:, 0:D], AF.Copy, [st, rstd_x], [xsb], scale=rstd_x.t[:, 0:1])
            yield
            for half in range(2):
                for j in range(8):
                    dc = half * 8 + j
                    TR(Bt.t[:, j * 128:(j + 1) * 128], xsb.t[:, dc * 128:(dc + 1) * 128], [xsb], [Bt])
                CP("act" if half == 0 else "dve",
                   hnT[hb][:, half * 8:(half + 1) * 8, tt * 128:(tt + 1) * 128],
                   Bt.t[:, :].rearrange("p (k t) -> p k t", t=128), [Bt], [b_hn[hb]])
                yield

    COLOFF = [i * 128 for i in range(16)] + [2048, 2144]
    CTM = [128] * 16 + [96, 96]

    def stage_P(blk):
        hb = blk % 2
        pmw, twb, adb = pm2[blk % 2], twb2[blk % 2], adb2[blk % 2]
        order = [16, 17, 0, 1, 2, 3, 4, 5, 6, 7, 10, 11, 8, 9, 12, 13, 14, 15]

        def evac(n_, ct):
            M = CTM[ct]
            bp = Bp[n_ % 2]
            if ct < 8 or ct >= 16:
                li = ct if ct < 8 else ct - 8
                mucol = 16 + li
                pr_ = PR[n_ % 3]
                CP("pool", pr_.t[0:M, 0:1], lastcol.t[0:M, li:li + 1], [lastcol], [pr_])
                CP("act" if n_ % 2 else "dve", pr_.t[0:M, 1:TB + 1], bp.t[0:M, 0:TB], [bp], [pr_])
                CP("pool", lastcol.t[0:M, li:li + 1], pr_.t[0:M, TB:TB + 1], [pr_], [lastcol])
                TT("pool", tP.t[0:M, :], pr_.t[0:M, 0:TB], pr_.t[0:M, 1:TB + 1], ALU.subtract, [pr_], [tP])
                if ct < 8:
                    STT(pmw[ct].t[:, :], tP.t[:, :], prm.t[:, mucol:mucol + 1], pr_.t[:, 1:TB + 1], ALU.mult, ALU.add,
                        [tP, prm, pr_], [pmw[ct]])
                elif ct == 16:
                    STT(tP.t[0:96, :], tP.t[0:96, :], prm.t[0:96, mucol:mucol + 1], pr_.t[0:96, 1:TB + 1],
                        ALU.mult, ALU.add, [tP, prm, pr_], [tP])
                    SIG(tP.t[0:96, :], tP.t[0:96, :], [tP], [tP], scale=2.0)
                    TS("dve", twb.t[:, :], tP.t[0:96, :], 2.0, -1.0, ALU.mult, ALU.add, [tP], [twb])
                else:
                    STT(adb.t[:, :], tP.t[0:96, :], prm.t[0:96, mucol:mucol + 1], pr_.t[0:96, 1:TB + 1],
                        ALU.mult, ALU.add, [tP, prm, pr_], [adb])
            else:
                CP("act" if n_ % 2 else "dve", pmw[ct].t[:, :], bp.t[:, 0:TB], [bp], [pmw[ct]])

        pending = None
        for n_, ct in enumerate(order):
            M = CTM[ct]
            bp = Bp[n_ % 2]
            for dc in range(KD):
                MM(bp.t[0:M, 0:TB], wbf3[:, dc, COLOFF[ct]:COLOFF[ct] + M], hnT[hb][:, dc, :], dc == 0, dc == KD - 1,
                   [b_wbf[dc], b_hn[hb]], [bp])
                if dc == KD // 2 - 1:
                    yield
            if pending is not None:
                evac(*pending)
            pending = (n_, ct)
            yield
        evac(*pending)
        yield

    c3 = lambda ap: ap.rearrange("p (c t) -> p c t", t=64)
    tok3 = lambda n: Bt.t[:, n * NT * 128:(n + 1) * NT * 128].rearrange("p (k t) -> p k t", t=128)

    def stage_R(hp, blk):
        pm, twb, adb = pm2[blk % 2], twb2[blk % 2], adb2[blk % 2]
        p_r, p_k, p_v = pm[hp], pm[2 + hp], pm[4 + hp]
        if stop == "R":
            LIM["on"] = True
            LIM["n"] = rlim
            LIM["c"] = 0
        bz = next_bg()
        MM(bz.t[:, 0:TB], w2b.t[:, hp * 128:(hp + 1) * 128], twb.t[:, :], True, True, [w2b, twb], [bz])
        SIG(tA.t[:, :], bz.t[:, 0:TB], [bz, drv], [tA], nbias=drv.t[:, 8 + hp:9 + hp])
        bz2 = next_bg()
        MM(bz2.t[:, 0:TB], w2b.t[:, 256 + hp * 128:256 + (hp + 1) * 128], adb.t[:, :], True, True, [w2b, adb], [bz2])
        SIG(tB.t[:, :], bz2.t[:, 0:TB], [bz2, drv], [tB], nbias=drv.t[:, 10 + hp:11 + hp])
        OP("dve", lambda e: e.tensor_tensor_scan(out=tC.t[:, :], data0=RST, data1=tA.t[:, :], initial=0.0,
                                                  op0=ALU.mult, op1=ALU.add), [cst, tA], [tC])
        ACT(tH.t[:, :], tC.t[:, :], AF.Exp, [tC], [tH], scale=-CEXP)
        CP("pool", gmC[hp].t[:, :], c3(tH.t[:, :])[:, :, 63], [tH], [gmC[hp]])
        ACT(tD.t[:, :], tC.t[:, :], AF.Exp, [tC], [tD], scale=CEXP)
        yield
        TT("pool", tE.t[:, :], tC.t[:, :], tA.t[:, :], ALU.subtract, [tC, tA], [tE])
        ACT(tE.t[:, :], tE.t[:, :], AF.Exp, [tE], [tE], scale=-CEXP)
        TT("dve", c3(tF.t[:, :]), c3(tC.t[:, :])[:, :, 63:64].to_broadcast([128, CPB, 64]), c3(tC.t[:, :]),
           ALU.subtract, [tC], [tF])
        ACT(tF.t[:, :], tF.t[:, :], AF.Exp, [tF], [tF], scale=-CEXP)
        yield
        TS("dve", tG.t[:, :], p_k.t[:, :], prm.t[:, 30 + hp:31 + hp], None, ALU.mult, None, [p_k, prm], [tG])
        ACT(tA.t[:, :], tG.t[:, :], AF.Square, [tG], [tA])
        bz3 = next_bg()
        MM(bz3.t[:, 0:TB], BONES, tA.t[:, :], True, True, [ones, tA], [bz3])
        ACT(tA.t[:, :], bz3.t[:, 0:TB], AF.Ln, [bz3], [tA], scale=64.0)
        TS("dve", tA.t[:, :], tA.t[:, :], 0.5, math.log(1e-12), ALU.mult, ALU.max, [tA], [tA])
        ACT(tA.t[:, :], tA.t[:, :], AF.Exp, [tA], [tA], scale=-1.0)
        TT("dve", tG.t[:, :], tG.t[:, :], tA.t[:, :], ALU.mult, [tG, tA], [tG])
        TS("dve", tA.t[:, :], tB.t[:, :], prm.t[:, 32 + hp:33 + hp], drv.t[:, hp:hp + 1], ALU.mult, ALU.add,
           [tB, prm, drv], [tA])
        TT("dve", tA.t[:, :], tA.t[:, :], p_k.t[:, :], ALU.mult, [tA, p_k], [tA])
        TT("pool", tB.t[:, :], tG.t[:, :], tB.t[:, :], ALU.mult, [tG, tB], [tB])
        yield
        TT("dve", rh[hp].t[:, :], p_r.t[:, :], tH.t[:, :], ALU.mult, [p_r, tH], [rh[hp]])
        TT("dve", kh[hp].t[:, :], tA.t[:, :], tD.t[:, :], ALU.mult, [tA, tD], [kh[hp]])
        TT("pool", bh[hp].t[:, :], tB.t[:, :], tD.t[:, :], ALU.mult, [tB, tD], [bh[hp]])
        STT(ah[hp].t[:, :], tG.t[:, :], -1.0, tE.t[:, :], ALU.mult, ALU.mult, [tG, tE], [ah[hp]])
        TT("dve", kt[hp].t[:, :], tA.t[:, :], tF.t[:, :], ALU.mult, [tA, tF], [kt[hp]])
        TT("pool", bt[hp].t[:, :], tB.t[:, :], tF.t[:, :], ALU.mult, [tB, tF], [bt[hp]])
        CP("act", vb[hp].t[:, :], p_v.t[:, :], [p_v], [vb[hp]])
        STT(tD.t[:, :], p_r.t[:, :], prm.t[:, 34 + hp:35 + hp], tA.t[:, :], ALU.mult, ALU.mult, [p_r, prm, tA], [tD])
        bz4 = next_bg()
        MM(bz4.t[:, 0:TB], BONES, tD.t[:, :], True, True, [ones, tD], [bz4])
        STT(bonus[hp].t[:, :], bz4.t[:, 0:TB], 64.0, p_v.t[:, :], ALU.mult, ALU.mult, [bz4, p_v], [bonus[hp]])
        yield
        for n, src in enumerate((kt[hp], bt[hp], vb[hp])):
            for ck in range(CPB):
                for h in range(2):
                    ph = slice(h * 64, (h + 1) * 64)
                    c0 = (n * CPB + ck) * 64
                    OP("pe", lambda e, ph=ph, c0=c0, src=src, ck=ck: e.transpose(
                        Bt.t[ph, c0:c0 + 64], src.t[ph, ck * 64:(ck + 1) * 64], ident.t[ph, ph]), [src, ident], [Bt])
        CP("act", tokall[hp].t[:, :, :], Bt.t[:, 0:3 * CPB * 64].rearrange("p (k t) -> p k t", t=64), [Bt],
           [tokall[hp]])
        LIM["on"] = False
        yield

    NQ = CPB // 2
    LW = CPB * 64

    def lcol(ck):
        return slice(ck * 64, (ck + 1) * 64)

    def lmat(dst_bg, lhs_t, rhs_t):
        for h in range(2):
            ph = slice(h * 64, (h + 1) * 64)
            for ck in range(CPB):
                cs = slice(ck * 64, (ck + 1) * 64)
                MM(dst_bg.t[ph, lcol(ck)], lhs_t.t[ph, cs], rhs_t.t[ph, cs], True, True, [lhs_t, rhs_t], [dst_bg])

    def lsq(dst_bg, lhs_t, rhs_t):
        for h in range(2):
            ph = slice(h * 64, (h + 1) * 64)
            for ck in range(CPB):
                MM(dst_bg.t[ph, lcol(ck)], lhs_t.t[ph, lcol(ck)], rhs_t.t[ph, lcol(ck)], True, True,
                   [lhs_t, rhs_t], [dst_bg])

    Tfinal = [None, None]

    def stage_L(hp):
        if stop == "L":
            LIM["on"] = True
            LIM["n"] = rlim
            LIM["c"] = 0
        for (dst, lt, rt, msk) in ((LabT, bh, ah, MS), (Lab, ah, bh, MST), (LakT, kh, ah, MS),
                                   (LrbT, bh, rh, MI), (LrkT, kh, rh, MI)):
            bgx = next_bg()
            lmat(bgx, lt[hp], rt[hp])
            TT("dve", dst[hp].t[:, :], bgx.t[:, 0:LW], msk, ALU.mult, [bgx, cst], [dst[hp]])
            yield
        P, PT, Tt = Lab[hp], LabT[hp], TTp[0][hp]
        TT("pool", Tt.t[:, :], PT.t[:, :], EYE, ALU.add, [PT, cst], [Tt])
        for j in range(5):
            Pn = Pp[j % 2][hp]
            PTn = PTp[j % 2][hp]
            b1 = next_bg()
            lsq(b1, PT, P)
            CP("act", Pn.t[:, :], b1.t[:, 0:LW], [b1], [Pn])
            if j < 4:
                b2 = next_bg()
                lsq(b2, P, PT)
                CP("dve", PTn.t[:, :], b2.t[:, 0:LW], [b2], [PTn])
            yield
            b3 = next_bg()
            Tn = TTp[(j + 1) % 2][hp]
            lsq(b3, Pn, Tt)
            TT("dve", Tn.t[:, :], b3.t[:, 0:LW], Tt.t[:, :], ALU.add, [b3, Tt], [Tn])
            P, PT, Tt = Pn, PTn, Tn
            yield
        Tfinal[hp] = Tt
        LIM["on"] = False

    def stage_C():
        for ck in range(CPB):
            cs = slice(ck * 64, (ck + 1) * 64)
            ys = ck % 3
            ycol = 128 + ys * 128
            for hp in range(2):
                hc = slice(hp * 64, (hp + 1) * 64)
                for h in range(2):
                    ph = slice(h * 64, (h + 1) * 64)
                    MM(B5[ph, hc], ah[hp].t[ph, cs], Hb.t[ph, hc], True, False, [ah[hp], Hb], [b_W])
                    MM(B5[ph, hc], LakT[hp].t[ph, lcol(ck)], vtok[hp].t[ph, ck, :], False, True,
                       [LakT[hp], vtok[hp]], [b_W])
            CP("dve", Wb.t[:, 0:128], B5[:, 0:128], [b_W], [Wb])
            for hp in range(2):
                hc = slice(hp * 64, (hp + 1) * 64)
                Tf = Tfinal[hp]
                for h in range(2):
                    ph = slice(h * 64, (h + 1) * 64)
                    MM(B5[ph, 256 + hp * 64:256 + (hp + 1) * 64], Tf.t[ph, lcol(ck)], Wb.t[ph, hc], True, True,
                       [Tf, Wb], [b_U])
            CP("act", Ub.t[:, 0:128], B5[:, 256:384], [b_U], [Ub])
            yield
            for hp in range(2):
                hc = slice(hp * 64, (hp + 1) * 64)
                for h in range(2):
                    ph = slice(h * 64, (h + 1) * 64)
                    MM(B6[ph, hc], btok[hp].t[ph, ck, :], Ub.t[ph, hc], True, False, [btok[hp], Ub], [b_H])
                    MM(B6[ph, hc], ktok[hp].t[ph, ck, :], vtok[hp].t[ph, ck, :], False, True,
                       [ktok[hp], vtok[hp]], [b_H])
            for hp in range(2):
                hc = slice(hp * 64, (hp + 1) * 64)
                for h in range(2):
                    ph = slice(h * 64, (h + 1) * 64)
                    yo = B6[ph, ycol + hp * 64:ycol + (hp + 1) * 64]
                    MM(yo, Hb.t[ph, hc], rh[hp].t[ph, cs], True, False, [Hb, rh[hp]], [b_Y[ys]])
                    MM(yo, Ub.t[ph, hc], LrbT[hp].t[ph, lcol(ck)], False, False, [Ub, LrbT[hp]], [b_Y[ys]])
                    MM(yo, vtok[hp].t[ph, ck, :], LrkT[hp].t[ph, lcol(ck)], False, True, [vtok[hp], LrkT[hp]],
                       [b_Y[ys]])
            for hp in range(2):
                STT(Hf.t[:, hp * 64:(hp + 1) * 64], Hf.t[:, hp * 64:(hp + 1) * 64],
                    gmC[hp].t[:, ck:ck + 1], B6[:, hp * 64:(hp + 1) * 64], ALU.mult, ALU.add,
                    [Hf, gmC[hp], b_H], [Hf])
            CP("act", Hb.t[:, :], Hf.t[:, :], [Hf], [Hb])
            CP("act", ysb_all.t[:, :, cs], B6[:, ycol:ycol + 128].rearrange("p (h t) -> p h t", t=64), [b_Y[ys]],
               [ysb_all])
            yield

    def ystore(m, blk, src):
        tok0 = blk * TB
        j_ = tok0 // 1024
        r0 = m * 128
        c0 = tok0 % 1024
        DMA(lambda e: e.dma_start(out=ybuf_d[j_][r0:r0 + 128, c0:c0 + TB], in_=src.t[:, :]), [src], [b_ybuf[j_]],
            eng="act")

    def stage_Y(hp, blk, tA=None, tB=None):
        tA = tA or tmps[0]
        tB = tB or tmps[1]
        p_g = pm2[blk % 2][6 + hp]
        bm = next_bg()
        MM(bm.t[:, 0:TB], BONES, ysb[hp].t[:, :], True, True, [ones, ysb[hp]], [bm])
        TT("dve", tA.t[:, :], ysb[hp].t[:, :], bm.t[:, 0:TB], ALU.subtract, [ysb[hp], bm], [tA])
        ACT(tB.t[:, :], tA.t[:, :], AF.Square, [tA], [tB])
        yield
        bv = next_bg()
        MM(bv.t[:, 0:TB], BONES, tB.t[:, :], True, True, [ones, tB], [bv])
        ACT(tB.t[:, :], bv.t[:, 0:TB], AF.Ln, [bv, epsn], [tB], bias=epsn.t[:, 1:2])
        ACT(tB.t[:, :], tB.t[:, :], AF.Exp, [tB], [tB], scale=-0.5)
        TT("dve", tA.t[:, :], tA.t[:, :], tB.t[:, :], ALU.mult, [tA, tB], [tA])
        yield
        TS("dve", tA.t[:, :], tA.t[:, :], prm.t[:, 36 + hp:37 + hp], prm.t[:, 38 + hp:39 + hp], ALU.mult, ALU.add,
           [tA, prm], [tA])
        TT("pool", tA.t[:, :], tA.t[:, :], bonus[hp].t[:, :], ALU.add, [tA, bonus[hp]], [tA])
        SIG(tB.t[:, :], p_g.t[:, :], [p_g], [tB])
        TT("pool", tB.t[:, :], tB.t[:, :], p_g.t[:, :], ALU.mult, [tB, p_g], [tB])
        TT("dve", yout[hp].t[:, :], tA.t[:, :], tB.t[:, :], ALU.mult, [tA, tB], [yout[hp]])
        ystore(hp, blk, yout[hp])
        yield

    def stage_G(j, blk):
        pm = pm2[blk % 2]
        p_q, p_f, p_i = pm[8 + j], pm[10 + j], pm[12 + j]
        SIG(tA.t[:, :], p_f.t[:, :], [p_f], [tA])
        TS("dve", tA.t[:, :], tA.t[:, :], drv.t[:, 4 + j:5 + j], drv.t[:, 2 + j:3 + j], ALU.mult, ALU.add,
           [tA, drv], [tA])
        TS("pool", tB.t[:, :], tA.t[:, :], -1.0, 1.0, ALU.mult, ALU.add, [tA], [tB])
        ACT(tA.t[:, :], tA.t[:, :], AF.Ln, [tA], [tA])
        yield
        OP("dve", lambda e: e.tensor_tensor_scan(out=tC.t[:, :], data0=RST, data1=tA.t[:, :], initial=0.0,
                                                  op0=ALU.mult, op1=ALU.add), [cst, tA], [tC])
        ACT(tH.t[:, :], tC.t[:, :], AF.Exp, [tC], [tH])
        CP("pool", ggC[j].t[:, :], c3(tH.t[:, :])[:, :, 63], [tH], [ggC[j]])
        ACT(tD.t[:, :], tC.t[:, :], AF.Exp, [tC], [tD], scale=-1.0)
        TT("dve", c3(tF.t[:, :]), c3(tC.t[:, :])[:, :, 63:64].to_broadcast([128, CPB, 64]), c3(tC.t[:, :]),
           ALU.subtract, [tC], [tF])
        ACT(tF.t[:, :], tF.t[:, :], AF.Exp, [tF], [tF])
        yield
        TT("dve", qh[j].t[:, :], p_q.t[:, :], tH.t[:, :], ALU.mult, [p_q, tH], [qh[j]])
        TT("pool", gkh[j].t[:, :], tB.t[:, :], tD.t[:, :], ALU.mult, [tB, tD], [gkh[j]])
        TT("dve", gkd[j].t[:, :], tB.t[:, :], tF.t[:, :], ALU.mult, [tB, tF], [gkd[j]])
        CP("act", gvb[j].t[:, :], p_i.t[:, :], [p_i], [gvb[j]])
        yield
        for n, src in enumerate((gkd[j], gvb[j])):
            for jj in range(NT):
                TR(Bt.t[:, (n * NT + jj) * 128:(n * NT + jj + 1) * 128], src.t[:, jj * 128:(jj + 1) * 128], [src], [Bt])
        for par in range(2):
            pp = slice(par * 64, (par + 1) * 64)
            CP("act", gpad[j][par].t[pp, :, :], Bt.t[pp, 0:2 * NT * 128].rearrange("p (k t) -> p k t", t=128), [Bt],
               [gpad[j][par]])
        yield

    def gcol(j, ck):
        return slice((j * NQ + ck // 2) * 64, (j * NQ + ck // 2 + 1) * 64)

    def stage_GA():
        bga = next_bg()
        for j in range(2):
            for ck in range(CPB):
                par = ck % 2
                cs = slice(ck * 64, (ck + 1) * 64)
                MM(bga.t[par * 64:(par + 1) * 64, gcol(j, ck)], gkh[j].t[:, cs], qh[j].t[:, cs], True, True,
                   [gkh[j], qh[j]], [bga])
        TT("dve", ATb.t[:, 0:2 * NQ * 64], bga.t[:, 0:2 * NQ * 64], MI[:, 0:2 * NQ * 64], ALU.mult, [bga, cst], [ATb])
        yield

    def stage_H():
        if stop == "H":
            LIM["on"] = True
            LIM["n"] = rlim
            LIM["c"] = 0
        for ck in range(CPB):
            par, q = ck % 2, ck // 2
            cs = slice(ck * 64, (ck + 1) * 64)
            osl = ck % 2
            ocol = 256 + osl * 128
            for j in range(2):
                gp = gpad[j][par]
                MM(B7[:, ocol + j * 64:ocol + (j + 1) * 64], Sb_.t[:, j * 128:(j + 1) * 128], qh[j].t[:, cs], True, False,
                   [Sb_, qh[j]], [b_O[osl]])
                MM(B7[:, ocol + j * 64:ocol + (j + 1) * 64], gp.t[:, NT + q, :], ATb.t[:, gcol(j, ck)], False, True,
                   [gp, ATb], [b_O[osl]])
            for j in range(2):
                gp = gpad[j][par]
                MM(B7[:, j * 128:(j + 1) * 128], gp.t[:, q, :], gp.t[:, NT + q, :], True, True, [gp], [b_S])
            for j in range(2):
                STT(Sf.t[:, j * 128:(j + 1) * 128], Sf.t[:, j * 128:(j + 1) * 128],
                    ggC[j].t[:, ck:ck + 1], B7[:, j * 128:(j + 1) * 128], ALU.mult, ALU.add,
                    [Sf, ggC[j], b_S], [Sf])
            CP("act", Sb_.t[:, :], Sf.t[:, :], [Sf], [Sb_])
            CP("dve", osb_all.t[:, :, cs], B7[:, ocol:ocol + 128].rearrange("p (h t) -> p h t", t=64), [b_O[osl]],
               [osb_all])
            yield

    def stage_O(j, blk, tA=None, tB=None):
        tA = tA or tmps[0]
        tB = tB or tmps[1]
        p_g = pm2[blk % 2][14 + j]
        ACT(tA.t[:, :], osb[j].t[:, :], AF.Square, [osb[j]], [tA])
        bm = next_bg()
        MM(bm.t[:, 0:TB], AONES, tA.t[:, :], True, True, [ones, tA], [bm])
        ACT(tA.t[:, :], bm.t[:, 0:TB], AF.Ln, [bm, epsn], [tA], bias=epsn.t[:, 0:1])
        ACT(tA.t[:, :], tA.t[:, :], AF.Exp, [tA], [tA], scale=-0.5)
        TT("dve", tA.t[:, :], tA.t[:, :], osb[j].t[:, :], ALU.mult, [tA, osb[j]], [tA])
        yield
        SIG(tB.t[:, :], p_g.t[:, :], [p_g], [tB])
        TT("pool", tB.t[:, :], tB.t[:, :], p_g.t[:, :], ALU.mult, [tB, p_g], [tB])
        STT(yout[2 + j].t[:, :], tA.t[:, :], prm.t[:, 40 + j:41 + j], tB.t[:, :], ALU.mult, ALU.mult,
            [tA, prm, tB], [yout[2 + j]])
        ystore(2 + j, blk, yout[2 + j])
        yield

    def run(gen):
        for _ in gen:
            pass

    def gather(j_):
        S.wait_all("pool", [b_ybuf[j_]])
        DMA(lambda e: e.collective_compute("AllGather", ALU.bypass, replica_groups=[[0, 1, 2, 3], [4, 5, 6, 7]],
                                           ins=[ybuf_d[j_].ap().opt()],
                                           outs=[yall_d[j_ * 2048:(j_ + 1) * 2048, :].opt()]),
            [b_ybuf[j_]], [b_yall], eng="pool", inc=1)

    dbg_n = [0]

    if dbg:
        dbgt = sb("dbgt", [128, TB])

    def dbg_dump(ap, tl, np_=128, w=TB):
        if dbg:
            slot = dbg_n[0]
            dbg_n[0] += 1
            if ap.dtype != F32:
                CP("dve", dbgt.t[0:np_, 0:w], ap, [tl], [dbgt])
                DMA(lambda e: e.dma_start(out=dbg_d[slot, 0:np_, 0:w], in_=dbgt.t[0:np_, 0:w]), [dbgt], [b_dbg])
            else:
                DMA(lambda e: e.dma_start(out=dbg_d[slot, 0:np_, 0:w], in_=ap), [tl], [b_dbg])

    nblk = NB if not dbg else dbg

    def phaseA():
        run(stage_X(0))
        if stop == "X":
            return
        for blk in range(nblk):
            run(stage_P(blk))
            if blk + 1 < nblk:
                run(stage_X(blk + 1))
            if dbg and blk == nblk - 1:
                for i_ in (0, 2, 4, 6, 8, 10):
                    dbg_dump(pm2[blk % 2][i_].t[:, :], pm2[blk % 2][i_])
            if stop == "P":
                continue
            for hp in range(2):
                run(stage_R(hp, blk))
                if stop == "R":
                    continue
                run(stage_L(hp))
            if dbg and blk == nblk - 1:
                dbg_dump(rh[0].t[:, :], rh[0]); dbg_dump(kh[0].t[:, :], kh[0]); dbg_dump(ah[0].t[:, :], ah[0])
                dbg_dump(bh[0].t[:, :], bh[0])
            if stop in ("R", "L"):
                continue
            for j in range(2):
                run(stage_G(j, blk))
            run(stage_GA())
            if stop == "G":
                continue
            run(stage_C())
            if dbg and blk == nblk - 1 and stop == "C":
                dbg_dump(ysb[0].t[:, :], ysb[0])
            if stop == "C":
                continue
            run(stage_H())
            if dbg and blk == nblk - 1:
                dbg_dump(ysb[0].t[:, :], ysb[0])
                dbg_dump(osb[0].t[:, :], osb[0])
            if stop == "H":
                continue
            for hp in range(2):
                run(stage_Y(hp, blk))
            for j in range(2):
                run(stage_O(j, blk))
            if dbg and blk == nblk - 1:
                dbg_dump(yout[0].t[:, :], yout[0])
                dbg_dump(yout[2].t[:, :], yout[2])
            if phaseB and ((blk + 1) * TB) % 1024 == 0:
                gather(((blk + 1) * TB) // 1024 - 1)
    def chain_(*gens):
        for g_ in gens:
            for tag in g_:
                yield tag

    def nofill(gen):
        for _ in gen:
            yield "nofill"

    def rr_(*gens):
        gens = list(gens)
        while gens:
            for g_ in list(gens):
                try:
                    yield next(g_)
                except StopIteration:
                    gens.remove(g_)

    W_DONE = [False]

    def stage_W():
        W_DONE[0] = True
        for kt_ in range(KD):
            r_, m_ = kt_ // 4, kt_ % 4
            row0 = (r_ * 256 + m_ * 128) if m_ < 2 else (1024 + r_ * 256 + (m_ - 2) * 128)
            st = next_stage()
            DMA(lambda e, st=st, row0=row0: e.dma_start(out=st.t[:, 0:D], in_=wout_d[row0:row0 + 128, :]), [], [st])
            yield
            CP("dve" if kt_ % 2 == 0 else "act", wob3[:, kt_, :], st.t[:, 0:D], [st], b_wbf)
            yield

    def n_main_ops():
        return S.count["act"] + S.count["dve"] + S.count["pool"]

    def interleave(main, filler, ops_per_fill):
        fill_alive = True
        credit = 0.0
        last = n_main_ops()
        for tag in main:
            now = n_main_ops()
            if tag != "nofill":
                credit += (now - last) / ops_per_fill
            while fill_alive and credit >= 1.0:
                credit -= 1.0
                try:
                    next(filler)
                except StopIteration:
                    fill_alive = False
                now = n_main_ops()
            last = now
        if fill_alive:
            for _ in filler:
                pass

    def phaseA_pipelined():
        run(stage_X(0))
        run(stage_P(0))
        if NB > 1:
            run(stage_X(1))
        for blk in range(NB):
            fl = []
            if blk + 1 < NB:
                fl.append(stage_P(blk + 1))
            if blk + 2 < NB:
                fl.append(stage_X(blk + 2))
            if blk == NB - 1 and phaseB:
                fl.append(stage_W())
            main = chain_(stage_R(0, blk),
                          rr_(stage_L(0), stage_R(1, blk)),
                          rr_(stage_L(1), chain_(stage_G(0, blk), stage_G(1, blk), stage_GA())),
                          rr_(stage_C(),
                              chain_(stage_H(), stage_O(0, blk, tmps[4], tmps[5]), stage_O(1, blk, tmps[6], tmps[7]))),
                          rr_(stage_Y(0, blk, tmps[0], tmps[1]), stage_Y(1, blk, tmps[2], tmps[3])))
            interleave(main, chain_(*fl), FILL_OPS)
            if phaseB and ((blk + 1) * TB) % 1024 == 0:
                gather(((blk + 1) * TB) // 1024 - 1)

    if dbg or stop:
        phaseA()
    else:
        phaseA_pipelined()

    if not phaseB:
        fin = b_ybuf + ([b_dbg] if dbg else [])
        for e_ in ("sp", "pool", "act"):
            S.wait_all(e_, fin)
        S.emit()
        S.close()
        return nc
    if not W_DONE[0]:
        run(stage_W())
    tmp_b = [t.b for t in tmps]
    pid_cache = {}
    DMA(lambda e: e.dma_start(out=fgt_ap, in_=fg_d), [], tmp_b)
    def xr_dma(tt):
        st = stage[tt % 2]
        DMA(lambda e: e.dma_start(out=st.t[:, 0:D], in_=xres_d[tt * 128:(tt + 1) * 128, :]), [], [st])

    yTq = [hn_raw[:, k_ * KD * 256:(k_ + 1) * KD * 256].rearrange("p (k t) -> p k t", t=256) for k_ in range(2)]
    for qt in range(4):
        kb = qt % 2
        for hh_ in range(2):
            def ld(e, qt=qt, hh_=hh_, kb=kb):
                if "g" not in pid_cache:
                    pid_cache["g"] = e.partition_id() % 4
                g_ = pid_cache["g"]
                src = yall_d[bass.ds(g_ * 2048 + hh_ * 1024, 1024), qt * 256:(qt + 1) * 256]
                return e.dma_start(out=yTq[kb][:, hh_ * 8:(hh_ + 1) * 8, :], in_=src.rearrange("(m p) c -> p m c", p=128))
            DMA(ld, [b_yall], [b_hn[kb]], eng="pool")
        for t2 in range(2):
            tt = qt * 2 + t2
            if tt == 0:
                xr_dma(0)
            if tt + 1 < 8:
                xr_dma(tt + 1)
            st = stage[tt % 2]
            h_ap = hsb_ap[tt % 2]
            h_b = hsb_b[tt % 2]
            for n_ in range(4):
                bp = Bp[n_ % 2]
                for kt_ in range(KD):
                    MM(bp.t[:, :], yTq[kb][:, kt_, t2 * 128:(t2 + 1) * 128], wob3[:, kt_, n_ * 512:(n_ + 1) * 512],
                       kt_ == 0, kt_ == KD - 1, [b_hn[kb]] + b_wbf, [bp])
                TT("dve", h_ap[:, n_ * 512:(n_ + 1) * 512], bp.t[:, :], st.t[:, n_ * 512:(n_ + 1) * 512], ALU.add,
                   [bp, st], h_b)
            ACT(xs[0].t[:, :], h_ap, AF.Square, h_b, [xs[0], ssx], accum=ssx.t[:, 0:1])
            ACT(ssx.t[:, 1:2], ssx.t[:, 0:1], AF.Ln, [ssx, epsn], [ssx], bias=epsn.t[:, 0:1], scale=1.0 / D)
            ACT(rstd_x.t[:, 0:1], ssx.t[:, 1:2], AF.Exp, [ssx], [rstd_x], scale=-0.5)
            STT(h_ap, h_ap, rstd_x.t[:, 0:1], fgt_ap, ALU.mult, ALU.mult, h_b + [rstd_x] + tmp_b, h_b)
            DMA(lambda e, h_ap=h_ap, tt=tt: e.dma_start(out=out_d[tt * 128:(tt + 1) * 128, :], in_=h_ap),
                h_b, [b_out], eng="act")
    fin = [b_out] + ([b_dbg] if dbg else [])
    for e_ in ("sp", "pool", "act"):
        S.wait_all(e_, fin)
    S.emit()
    S.close()
    return nc


def _host_inputs(x, norm_g, w_in, mu, w0, w2, a0, a2, k_k, k_a, r_k, lnx_w, lnx_b, hgrn_norm_g, lb_param, w_out,
                 final_g):
    f = lambda a: np.ascontiguousarray(np.asarray(a), dtype=np.float32)
    x, norm_g, w_in, mu, w0, w2, a0, a2, k_k, k_a, r_k, lnx_w, lnx_b, hgrn_norm_g, lb_param, w_out, final_g = map(
        f, (x, norm_g, w_in, mu, w0, w2, a0, a2, k_k, k_a, r_k, lnx_w, lnx_b, hgrn_norm_g, lb_param, w_out, final_g))
    p = np.arange(128)
    j = np.arange(TB)
    cst = np.zeros((128, 5, TB), np.float32)
    pj, jj = (p % 64)[:, None], (j % 64)[None, :]
    cst[:, 0] = (jj > pj)
    cst[:, 1] = (jj >= pj)
    cst[:, 2] = (jj < pj)
    cst[:, 3] = (jj != 0)
    cst[:, 4] = (jj == pj)
    ones = np.zeros((128, 2, 128), np.float32)
    ones[:, 0] = ((p[:, None] // 64) == (p[None, :] // 64)) / 64.0
    ones[:, 1] = 1.0 / 128.0
    ident = np.eye(128, dtype=np.float32).astype(ml_dtypes.bfloat16)
    fg = np.ascontiguousarray(np.broadcast_to(final_g[None, :], (128, D)))
    wout = w_out[0]
    maps = []
    rk_flat = r_k[0].reshape(-1)
    for c in range(8):
        b, g = c // 4, c % 4
        cols = []
        for base in (0, 1024, 2048, 3072):
            cols.append(np.arange(base + g * 256, base + (g + 1) * 256))
        for base in (0, 1024, 2048, 3072):
            cols.append(np.arange(4288 + base + g * 256, 4288 + base + (g + 1) * 256))
        cols.append(np.arange(4096, 4288))
        cols = np.concatenate(cols)
        w = np.ascontiguousarray(w_in[0][:, cols])
        prm = np.zeros((128, NPRM), np.float32)
        prm[:, 0:16] = norm_g[0].reshape(16, 128).T
        for i, base in enumerate((0, 1024, 2048, 3072)):
            for hp in range(2):
                prm[:, 16 + 2 * i + hp] = mu[0][base + g * 256 + hp * 128 + p]
        prm[0:96, 24] = mu[0][4096:4192]
        prm[0:96, 25] = mu[0][4192:4288]
        for hp in range(2):
            ch = g * 256 + hp * 128 + p
            prm[:, 26 + hp] = w0[0][ch]
            prm[:, 28 + hp] = a0[0][ch]
            prm[:, 30 + hp] = k_k[0][ch]
            prm[:, 32 + hp] = k_a[0][ch]
            prm[:, 34 + hp] = rk_flat[ch]
            prm[:, 36 + hp] = lnx_w[0][ch]
            prm[:, 38 + hp] = lnx_b[0][ch]
            prm[:, 40 + hp] = hgrn_norm_g[0][ch]
            prm[:, 42 + hp] = lb_param[0][ch]
            prm[:, 44 + hp] = lb_param[1][ch]
        maps.append({
            "x": x[b], "xres": np.ascontiguousarray(x[b, g * 1024:(g + 1) * 1024]), "w": w, "prm": prm,
            "w2s": np.ascontiguousarray(w2[0][:, g * 256:(g + 1) * 256]),
            "a2s": np.ascontiguousarray(a2[0][:, g * 256:(g + 1) * 256]),
            "wout": wout, "fg": fg, "ident": ident, "cst": cst, "ones": ones,
        })
    return maps


_CACHE = {}


def kernel(**inputs):
    maps = _host_inputs(**inputs)
    nc = _get_program()
    res = run_bass_kernel_spmd(nc, maps, core_ids=list(range(8)))
    out = np.zeros((2, T_SEQ, D), np.float32)
    for c in range(8):
        b, g = c // 4, c % 4
        out[b, g * 1024:(g + 1) * 1024, :] = res.results[c]["out"]
    return out


def _get_program():
    if "nc" not in _CACHE:
        _CACHE["nc"] = build_program()
    return _CACHE["nc"]
```

```python
import math
import numpy as np
import ml_dtypes
import concourse.bass as bass
import concourse.mybir as mybir
from concourse.bass_utils import run_bass_kernel_spmd

F32 = mybir.dt.float32
BF16 = mybir.dt.bfloat16
ALU = mybir.AluOpType
AF = mybir.ActivationFunctionType

T_SEQ = 4096
D = 2048
KD = 16
TB = 256
FILL_OPS = 5.0
CPB = TB // 64
NT = TB // 128
NB = T_SEQ // TB
NCOL = 2240
NPRM = 46
CEXP = math.exp(-0.5)
NORM_EPS = 1e-6
LNX_EPS = 64e-5


class Buf:
    __slots__ = ("name", "last_write", "reads")

    def __init__(self, name=""):
        self.name = name
        self.last_write = None
        self.reads = []


class Sched:
    ENG = ("pe", "act", "dve", "pool", "sp")

    def __init__(self, nc, n_dma_sems=32, same_eng_sync=True):
        self.nc = nc
        self.same_eng_sync = same_eng_sync
        self.ops = {e: [] for e in self.ENG}
        self.count = {e: 0 for e in self.ENG}
        self.waited = {e: {} for e in self.ENG}
        self.sems = {}
        self._stack = []
        self.dma_sems = []
        for i in range(n_dma_sems):
            cm = nc.semaphore("dma_%d" % i)
            self.dma_sems.append([cm.__enter__(), 0])
            self._stack.append(cm)
        self.dma_rr = 0
        self.n_rot = n_dma_sems

    def close(self):
        for cm in reversed(self._stack):
            cm.__exit__(None, None, None)

    def _collect(self, eng, reads, writes):
        deps = []
        for b in reads:
            if b.last_write is not None:
                deps.append(b.last_write)
        for b in writes:
            if b.last_write is not None:
                deps.append(b.last_write)
            deps.extend(b.reads)
        wd = self.waited[eng]
        best = {}
        for (key, sem, val, src) in deps:
            if src == eng and (eng in ("pe", "sp") or not self.same_eng_sync):
                continue
            if src == eng and key == "e_%s_%d" % (eng, self.count[eng] // self.EPOCH) \
                    and (self.count[eng] % self.EPOCH) - val >= self.SAME_ENG_GAP:
                continue
            if wd.get(key, 0) >= val:
                continue
            wd[key] = val
            best[key] = (sem, val)
        return list(best.values())

    EPOCH = 2000
    SAME_ENG_GAP = 3

    def _esem(self, eng, epoch):
        key = (eng, epoch)
        if key not in self.sems:
            cm = self.nc.semaphore("prog_%s_%d" % (eng, epoch))
            self.sems[key] = cm.__enter__()
            self._stack.append(cm)
        return self.sems[key]

    def op(self, eng, fn, reads=(), writes=()):
        waits = self._collect(eng, reads, writes)
        epoch, pos = divmod(self.count[eng], self.EPOCH)
        self.count[eng] += 1
        sem = self._esem(eng, epoch)
        tok = ("e_%s_%d" % (eng, epoch), sem, pos + 1, eng)
        self.ops[eng].append((waits, fn, sem, 1))
        for b in reads:
            b.reads.append(tok)
        for b in writes:
            b.last_write = tok
            b.reads = []
        return tok

    def dma(self, eng, fn, reads=(), writes=(), inc=16):
        waits = self._collect(eng, reads, writes)
        if eng == "pool":
            cm = self.nc.semaphore("swdma_%d" % len(self.dma_sems))
            self.dma_sems.append([cm.__enter__(), 0])
            self._stack.append(cm)
            idx = len(self.dma_sems) - 1
        else:
            idx = self.dma_rr % self.n_rot
            self.dma_rr += 1
        slot = self.dma_sems[idx]
        key = "d_%d" % idx
        sem, cur = slot
        wd = self.waited[eng]
        if cur > 0 and wd.get(key, 0) < cur:
            wd[key] = cur
            waits.append((sem, cur))
        slot[1] = cur + inc
        tok = (key, sem, slot[1], None)
        self.ops[eng].append((waits, fn, sem, inc))
        for b in reads:
            b.reads.append(tok)
        for b in writes:
            b.last_write = tok
            b.reads = []
        return tok

    def wait_all(self, eng, bufs):
        waits = self._collect(eng, bufs, ())
        if waits:
            self.ops[eng].append((waits, None, None, 0))

    def emit(self):
        nc = self.nc
        ops = self.ops

        def run(e, lst):
            for waits, fn, sem, inc in lst:
                for s, v in waits:
                    e.wait_ge(s, v)
                if fn is not None:
                    fn(e).then_inc(sem, inc)

        with nc.Block() as block:
            @block.tensor
            def _(e):
                run(e, ops["pe"])

            @block.scalar
            def _(e):
                run(e, ops["act"])

            @block.vector
            def _(e):
                run(e, ops["dve"])

            @block.gpsimd
            def _(e):
                run(e, ops["pool"])

            @block.sync
            def _(e):
                run(e, ops["sp"])


class Tl:
    def __init__(self, t, name=""):
        self.t = t
        self.b = Buf(name)


def build_program(dbg=False, stop=None, phaseB=True, rlim=None):
    nc = bass.Bass("TRN2", target_bir_lowering=False)
    S = Sched(nc)

    def din(name, shape, dt=F32):
        return nc.dram_tensor(name, list(shape), dt, kind="ExternalInput").ap()

    x_d = din("x", [T_SEQ, D])
    w_d = din("w", [D, NCOL])
    prm_d = din("prm", [128, NPRM])
    w2_d = din("w2s", [96, 256])
    a2_d = din("a2s", [96, 256])
    wout_d = din("wout", [D, D])
    fg_d = din("fg", [128, D])
    ident_d = din("ident", [128, 128], BF16)
    cst_d = din("cst", [128, 5, TB])
    ones_d = din("ones", [128, 2, 128])
    out_d = nc.dram_tensor("out", [1024, D], F32, kind="ExternalOutput").ap()
    xres_d = din("xres", [1024, D])
    ybuf_d = [nc.dram_tensor("ybuf%d" % j_, [512, 1024], BF16) for j_ in range(4)]
    yall_d = nc.dram_tensor("yall", [8192, 1024], BF16)
    b_ybuf = [Buf("ybuf%d" % j_) for j_ in range(4)]
    b_yall = Buf("yall")
    b_out = Buf("out")
    if dbg:
        dbg_d = nc.dram_tensor("dbg", [16, 128, TB], F32, kind="ExternalOutput").ap()
        b_dbg = Buf("dbg")

    def sb(name, shape, dt=F32):
        return Tl(nc.alloc_sbuf_tensor("s_" + name, list(shape), dt), name)

    def ps(name, shape, dt=F32):
        return Tl(nc.alloc_psum_tensor("p_" + name, list(shape), dt), name)

    LIM = {"on": False, "n": None, "c": 0}
    BANK_LOCK = {}

    def OP(eng, fn, r, w):
        if LIM["on"] and LIM["n"] is not None:
            LIM["c"] += 1
            if LIM["c"] > LIM["n"]:
                return None
        rr = [t.b if isinstance(t, Tl) else t for t in r]
        ww = [t.b if isinstance(t, Tl) else t for t in w]
        if True:
            for b_ in rr + ww:
                lk = BANK_LOCK.get(id(b_))
                if lk is not None and lk not in ww:
                    ww.append(lk)
        return S.op(eng, fn, reads=rr, writes=ww)

    def DMA(fn, r, w, eng="sp", inc=16):
        return S.dma(eng, fn, reads=[t.b if isinstance(t, Tl) else t for t in r],
                     writes=[t.b if isinstance(t, Tl) else t for t in w], inc=inc)

    def TT(eng, out, in0, in1, op, r, w):
        OP(eng, lambda e: e.tensor_tensor(out=out, in0=in0, in1=in1, op=op), r, w)

    def TS(eng, out, in0, s1, s2, op0, op1, r, w):
        if op1 is None:
            OP(eng, lambda e: e.tensor_scalar(out=out, in0=in0, scalar1=s1, scalar2=None, op0=op0), r, w)
        else:
            OP(eng, lambda e: e.tensor_scalar(out=out, in0=in0, scalar1=s1, scalar2=s2, op0=op0, op1=op1), r, w)

    def STT(out, in0, scalar, in1, op0, op1, r, w):
        OP("dve", lambda e: e.scalar_tensor_tensor(out=out, in0=in0, scalar=scalar, in1=in1, op0=op0, op1=op1), r, w)

    def ACT(out, in_, func, r, w, bias=None, scale=None, accum=None, eng="act"):
        kw = {}
        if bias is not None:
            kw["bias"] = bias
        if scale is not None:
            kw["scale"] = scale
        if accum is not None:
            kw["accum_out"] = accum
        OP(eng, lambda e: e.activation(out=out, in_=in_, func=func, **kw), r, w)

    def SIG(out, in_, r, w, nbias=None, scale=1.0):
        ACT(out, in_, AF.Exp, r, w, bias=nbias, scale=-scale)
        ACT(out, out, AF.Ln, w + [epsn], w, bias=epsn.t[0:out.shape[0], 2:3])
        ACT(out, out, AF.Exp, w, w, scale=-1.0)

    def CP(eng, out, in_, r, w):
        if eng == "act":
            OP(eng, lambda e: e.activation(out=out, in_=in_, func=AF.Copy), r, w)
        else:
            OP(eng, lambda e: e.tensor_copy(out=out, in_=in_), r, w)

    def MM(out, lhsT, rhs, start, stop, r, w):
        OP("pe", lambda e: e.matmul(out, lhsT=lhsT, rhs=rhs, start=start, stop=stop), r, w)

    def TR(out, in_, r, w):
        OP("pe", lambda e: e.transpose(out, in_, ident.t[:, :]), list(r) + [ident], w)

    wbf = nc.alloc_sbuf_tensor("wbf", [128, KD * NCOL], BF16)
    b_wbf = [Buf("wbf%d" % i) for i in range(KD)]
    wbf3 = wbf[:, :].rearrange("p (k c) -> p k c", c=NCOL)
    wob3 = wbf[:, 0:KD * D].rearrange("p (k c) -> p k c", c=D)
    stage = [sb("stage%d" % i, [128, NCOL]) for i in range(2)]
    stage_rr = [0]

    def next_stage():
        s = stage[stage_rr[0] % 2]
        stage_rr[0] += 1
        return s

    xs = [sb("xs0", [128, D], BF16)] * 2
    uT = [sb("uT%d" % i, [128, TB]) for i in range(4)]
    hn_raw = nc.alloc_sbuf_tensor("hnT", [128, 2 * KD * TB], BF16)
    b_hn = [Buf("hn0"), Buf("hn1")]
    hnT = [hn_raw[:, i * KD * TB:(i + 1) * KD * TB].rearrange("p (k t) -> p k t", t=TB) for i in range(2)]
    yT3 = hn_raw[:, :].rearrange("p (k t) -> p k t", t=512)
    prm = sb("prm", [128, NPRM])
    drv = sb("drv", [128, 12])
    epsn = sb("epsn", [128, 3])
    ident = sb("ident", [128, 128], BF16)
    cst = sb("cst", [128, 5, TB])
    ones = sb("ones", [128, 2, 128])
    w2b = sb("w2b", [96, 512], BF16)
    MS = cst.t[:, 0, :]
    MI = cst.t[:, 1, :]
    MST = cst.t[:, 2, :]
    RST = cst.t[:, 3, :]
    EYE = cst.t[:, 4, :]
    BONES = ones.t[:, 0, :]
    AONES = ones.t[:, 1, :]

    lastcol = sb("lastcol", [128, 10])
    PR = [sb("PR%d" % i, [128, TB + 1]) for i in range(3)]
    pm_arena = nc.alloc_sbuf_tensor("pm_arena", [128, 32 * TB], F32)
    pm2 = [[Tl(pm_arena[:, (k * 16 + i) * TB:(k * 16 + i + 1) * TB], "pm%d_%d" % (k, i)) for i in range(16)]
           for k in range(2)]
    pm = pm2[0]
    twb2 = [sb("twb%d" % k, [96, TB], BF16) for k in range(2)]
    adb2 = [sb("adb%d" % k, [96, TB], BF16) for k in range(2)]
    tP = sb("tP", [128, TB])
    ssx = sb("ssx", [128, 2])
    rstd_x = sb("rstdx", [128, 2])

    def mk(name, dt=F32, n=2, shape=None):
        return [sb("%s%d" % (name, i), shape or [128, TB], dt) for i in range(n)]

    rh, kh, bh, ah, kt, bt, vb = [mk(n_, BF16) for n_ in ("rh", "kh", "bh", "ah", "kt", "bt", "vb")]
    tokall = mk("tokall", BF16, shape=[128, 3 * CPB, 64])
    ktok = [Tl(tokall[i].t[:, 0:CPB, :]) for i in range(2)]
    btok = [Tl(tokall[i].t[:, CPB:2 * CPB, :]) for i in range(2)]
    vtok = [Tl(tokall[i].t[:, 2 * CPB:3 * CPB, :]) for i in range(2)]
    for i in range(2):
        ktok[i].b = btok[i].b = vtok[i].b = tokall[i].b
    gmC = mk("gmC", shape=[128, CPB])
    bonus = mk("bonus")
    ysb_all = sb("ysb_all", [128, 2, TB])
    ysb = [Tl(ysb_all.t[:, i, :]) for i in range(2)]
    for i in range(2):
        ysb[i].b = ysb_all.b
    Lab, LabT, LakT, LrbT, LrkT = [mk(n_, BF16) for n_ in ("Lab", "LabT", "LakT", "LrbT", "LrkT")]
    Pp = [mk("Pa", BF16), mk("Pb", BF16)]
    PTp = [mk("PTa", BF16), mk("PTb", BF16)]
    TTp = [mk("TTa", BF16), mk("TTb", BF16)]
    tmp_arena = nc.alloc_sbuf_tensor("tmp_arena", [128, 8 * TB], F32)
    tA, tB, tC, tD, tE, tF, tG, tH = [Tl(tmp_arena[:, i * TB:(i + 1) * TB], "tmp%d" % i) for i in range(8)]
    tmps = [tA, tB, tC, tD, tE, tF, tG, tH]
    fgt_ap = tmp_arena[:, 0:D]
    Hf = sb("Hf", [128, 128])
    Hb = sb("Hb", [128, 128], BF16)
    Wb = sb("Wb", [128, 256], BF16)
    Ub = sb("Ub", [128, 256], BF16)
    yout = mk("yout", BF16, n=4)
    qh, gkh, gkd, gvb = [mk(n_, BF16) for n_ in ("qh", "gkh", "gkd", "gvb")]
    gpad = [[sb("gpad%d_%d" % (j_, par_), [128, 2 * NT, 128], BF16) for par_ in range(2)] for j_ in range(2)]
    ggC = mk("ggC", shape=[128, CPB])
    ATb = sb("ATb", [128, TB], BF16)
    osb_all = sb("osb_all", [128, 2, TB])
    osb = [Tl(osb_all.t[:, i, :]) for i in range(2)]
    for i in range(2):
        osb[i].b = osb_all.b
    Sf = sb("Sf", [128, 256])
    Sb_ = sb("Sb", [128, 256], BF16)
    hsb_ap = [pm_arena[:, i * D:(i + 1) * D] for i in range(2)]
    hsb_b = [[pm[k].b for k in range(i * (D // TB), (i + 1) * (D // TB))] for i in range(2)]

    Bp = [ps("Bp%d" % i, [128, 512]) for i in range(2)]
    Bt = ps("Bt", [128, 1024], BF16)
    Bg = [ps("Bg%d" % i, [128, 512]) for i in range(2)]
    B6 = nc.alloc_psum_tensor("B6", [128, 512], F32)
    b_H = Buf("psH")
    b_Y = [Buf("psY%d" % i) for i in range(3)]
    B7 = nc.alloc_psum_tensor("B7", [128, 512], F32)
    b_S = Buf("psS")
    b_O = [Buf("psO%d" % i) for i in range(2)]
    B5 = nc.alloc_psum_tensor("B5", [128, 512], F32)
    b_W = Buf("psW")
    b_U = Buf("psU")
    for grp in ([Bp[0].b], [Bp[1].b], [Bt.b], [Bg[0].b], [Bg[1].b], [b_W, b_U], [b_H] + b_Y, [b_S] + b_O):
        lk_ = Buf("lock")
        for b_ in grp:
            BANK_LOCK[id(b_)] = lk_
    bg_rr = [0]

    def next_bg():
        b = Bg[bg_rr[0] % 2]
        bg_rr[0] += 1
        return b

    DMA(lambda e: e.dma_start(out=prm.t[:, :], in_=prm_d), [], [prm])
    DMA(lambda e: e.dma_start(out=ident.t[:, :], in_=ident_d), [], [ident])
    DMA(lambda e: e.dma_start(out=cst.t[:, :, :], in_=cst_d), [], [cst])
    DMA(lambda e: e.dma_start(out=ones.t[:, :, :], in_=ones_d), [], [ones])
    st0 = next_stage()
    DMA(lambda e: e.dma_start(out=st0.t[0:96, 0:256], in_=w2_d), [], [st0])
    DMA(lambda e: e.dma_start(out=st0.t[0:96, 256:512], in_=a2_d), [], [st0])
    CP("pool", w2b.t[:, :], st0.t[0:96, 0:512], [st0], [w2b])
    OP("pool", lambda e: e.memset(lastcol.t[:, :], 0.0), [], [lastcol])
    OP("pool", lambda e: e.memset(Hf.t[:, :], 0.0), [], [Hf])
    OP("pool", lambda e: e.memset(Hb.t[:, :], 0.0), [], [Hb])
    OP("pool", lambda e: e.memset(Sf.t[:, :], 0.0), [], [Sf])
    OP("pool", lambda e: e.memset(Sb_.t[:, :], 0.0), [], [Sb_])
    for j_ in range(2):
        for par_ in range(2):
            g_ = gpad[j_][par_]
            OP("pool", lambda e, g_=g_: e.memset(g_.t[:, :, :], 0.0), [], [g_])
    OP("pool", lambda e: e.memset(epsn.t[:, 0:1], NORM_EPS), [], [epsn])
    OP("pool", lambda e: e.memset(epsn.t[:, 1:2], LNX_EPS), [], [epsn])
    OP("pool", lambda e: e.memset(epsn.t[:, 2:3], 1.0), [], [epsn])
    TS("dve", drv.t[:, 0:2], prm.t[:, 32:34], -1.0, 1.0, ALU.mult, ALU.add, [prm], [drv])
    TT("dve", drv.t[:, 6:8], prm.t[:, 42:44], prm.t[:, 44:46], ALU.subtract, [prm], [drv])
    SIG(drv.t[:, 2:4], drv.t[:, 6:8], [drv], [drv])
    TS("dve", drv.t[:, 8:12], prm.t[:, 26:30], -1.0, None, ALU.mult, None, [prm], [drv])
    TS("dve", drv.t[:, 4:6], drv.t[:, 2:4], -1.0, 1.0, ALU.mult, ALU.add, [drv], [drv])

    for dc in range(KD):
        st = next_stage()
        DMA(lambda e, st=st, dc=dc: e.dma_start(out=st.t[:, :], in_=w_d[dc * 128:(dc + 1) * 128, :]), [], [st])
        if dc % 2 == 0:
            TS("dve", wbf3[:, dc, :], st.t[:, :], prm.t[:, dc:dc + 1], None, ALU.mult, None, [st, prm], [b_wbf[dc]])
        else:
            ACT(wbf3[:, dc, :], st.t[:, :], AF.Copy, [st, prm], [b_wbf[dc]], scale=prm.t[:, dc:dc + 1])

    NXT = T_SEQ // 128

    def x_dma(t):
        st = stage[t % 2]
        DMA(lambda e: e.dma_start(out=st.t[:, 0:D], in_=x_d[t * 128:(t + 1) * 128, :]), [], [st])

    def stage_X(blk):
        hb = blk % 2
        for tt in range(NT):
            t = blk * NT + tt
            if t == 0:
                x_dma(0)
            if t + 1 < NXT:
                x_dma(t + 1)
            st = stage[t % 2]
            xsb = xs[tt % 2]
            OP("dve", lambda e, st=st, xsb=xsb: e.scalar_tensor_tensor(
                out=xsb.t[:, :], in0=st.t[:, 0:D], scalar=1.0, in1=st.t[:, 0:D], op0=ALU.mult, op1=ALU.mult,
                accum_out=ssx.t[:, 0:1]), [st], [xsb, ssx])
            yield
            ACT(ssx.t[:, 1:2], ssx.t[:, 0:1], AF.Ln, [ssx, epsn], [ssx], bias=epsn.t[:, 0:1], scale=1.0 / D)
            ACT(rstd_x.t[:, 0:1], ssx.t[:, 1:2], AF.Exp, [ssx], [rstd_x], scale=-0.5)
            ACT(xsb.t[:, :], st.t[:, 0:D], AF.Copy, [st, rstd_x], [xsb], scale=rstd_x.t[:, 0:1])
            yield
            for half in range(2):
                for j in range(8):
                    dc = half * 8 + j
                    TR(Bt.t[:, j * 128:(j + 1) * 128], xsb.t[:, dc * 128:(dc + 1) * 128], [xsb], [Bt])
                CP("act" if half == 0 else "dve",
                   hnT[hb][:, half * 8:(half + 1) * 8, tt * 128:(tt + 1) * 128],
                   Bt.t[:, :].rearrange("p (k t) -> p k t", t=128), [Bt], [b_hn[hb]])
                yield

    COLOFF = [i * 128 for i in range(16)] + [2048, 2144]
    CTM = [128] * 16 + [96, 96]

    def stage_P(blk):
        hb = blk % 2
        pmw, twb, adb = pm2[blk % 2], twb2[blk % 2], adb2[blk % 2]
        order = [16, 17, 0, 1, 2, 3, 4, 5, 6, 7, 10, 11, 8, 9, 12, 13, 14, 15]

        def evac(n_, ct):
            M = CTM[ct]
            bp = Bp[n_ % 2]
            if ct < 8 or ct >= 16:
                li = ct if ct < 8 else ct - 8
                mucol = 16 + li
                pr_ = PR[n_ % 3]
                CP("pool", pr_.t[0:M, 0:1], lastcol.t[0:M, li:li + 1], [lastcol], [pr_])
                CP("act" if n_ % 2 else "dve", pr_.t[0:M, 1:TB + 1], bp.t[0:M, 0:TB], [bp], [pr_])
                CP("pool", lastcol.t[0:M, li:li + 1], pr_.t[0:M, TB:TB + 1], [pr_], [lastcol])
                TT("pool", tP.t[0:M, :], pr_.t[0:M, 0:TB], pr_.t[0:M, 1:TB + 1], ALU.subtract, [pr_], [tP])
                if ct < 8:
                    STT(pmw[ct].t[:, :], tP.t[:, :], prm.t[:, mucol:mucol + 1], pr_.t[:, 1:TB + 1], ALU.mult, ALU.add,
                        [tP, prm, pr_], [pmw[ct]])
                elif ct == 16:
                    STT(tP.t[0:96, :], tP.t[0:96, :], prm.t[0:96, mucol:mucol + 1], pr_.t[0:96, 1:TB + 1],
                        ALU.mult, ALU.add, [tP, prm, pr_], [tP])
                    SIG(tP.t[0:96, :], tP.t[0:96, :], [tP], [tP], scale=2.0)
                    TS("dve", twb.t[:, :], tP.t[0:96, :], 2.0, -1.0, ALU.mult, ALU.add, [tP], [twb])
                else:
                    STT(adb.t[:, :], tP.t[0:96, :], prm.t[0:96, mucol:mucol + 1], pr_.t[0:96, 1:TB + 1],
                        ALU.mult, ALU.add, [tP, prm, pr_], [adb])
            else:
                CP("act" if n_ % 2 else "dve", pmw[ct].t[:, :], bp.t[:, 0:TB], [bp], [pmw[ct]])

        pending = None
        for n_, ct in enumerate(order):
            M = CTM[ct]
            bp = Bp[n_ % 2]
            for dc in range(KD):
                MM(bp.t[0:M, 0:TB], wbf3[:, dc, COLOFF[ct]:COLOFF[ct] + M], hnT[hb][:, dc, :], dc == 0, dc == KD - 1,
                   [b_wbf[dc], b_hn[hb]], [bp])
                if dc == KD // 2 - 1:
                    yield
            if pending is not None:
                evac(*pending)
            pending = (n_, ct)
            yield
        evac(*pending)
        yield

    c3 = lambda ap: ap.rearrange("p (c t) -> p c t", t=64)
    tok3 = lambda n: Bt.t[:, n * NT * 128:(n + 1) * NT * 128].rearrange("p (k t) -> p k t", t=128)

    def stage_R(hp, blk):
        pm, twb, adb = pm2[blk % 2], twb2[blk % 2], adb2[blk % 2]
        p_r, p_k, p_v = pm[hp], pm[2 + hp], pm[4 + hp]
        if stop == "R":
            LIM["on"] = True
            LIM["n"] = rlim
            LIM["c"] = 0
        bz = next_bg()
        MM(bz.t[:, 0:TB], w2b.t[:, hp * 128:(hp + 1) * 128], twb.t[:, :], True, True, [w2b, twb], [bz])
        SIG(tA.t[:, :], bz.t[:, 0:TB], [bz, drv], [tA], nbias=drv.t[:, 8 + hp:9 + hp])
        bz2 = next_bg()
        MM(bz2.t[:, 0:TB], w2b.t[:, 256 + hp * 128:256 + (hp + 1) * 128], adb.t[:, :], True, True, [w2b, adb], [bz2])
        SIG(tB.t[:, :], bz2.t[:, 0:TB], [bz2, drv], [tB], nbias=drv.t[:, 10 + hp:11 + hp])
        OP("dve", lambda e: e.tensor_tensor_scan(out=tC.t[:, :], data0=RST, data1=tA.t[:, :], initial=0.0,
                                                  op0=ALU.mult, op1=ALU.add), [cst, tA], [tC])
        ACT(tH.t[:, :], tC.t[:, :], AF.Exp, [tC], [tH], scale=-CEXP)
        CP("pool", gmC[hp].t[:, :], c3(tH.t[:, :])[:, :, 63], [tH], [gmC[hp]])
        ACT(tD.t[:, :], tC.t[:, :], AF.Exp, [tC], [tD], scale=CEXP)
        yield
        TT("pool", tE.t[:, :], tC.t[:, :], tA.t[:, :], ALU.subtract, [tC, tA], [tE])
        ACT(tE.t[:, :], tE.t[:, :], AF.Exp, [tE], [tE], scale=-CEXP)
        TT("dve", c3(tF.t[:, :]), c3(tC.t[:, :])[:, :, 63:64].to_broadcast([128, CPB, 64]), c3(tC.t[:, :]),
           ALU.subtract, [tC], [tF])
        ACT(tF.t[:, :], tF.t[:, :], AF.Exp, [tF], [tF], scale=-CEXP)
        yield
        TS("dve", tG.t[:, :], p_k.t[:, :], prm.t[:, 30 + hp:31 + hp], None, ALU.mult, None, [p_k, prm], [tG])
        ACT(tA.t[:, :], tG.t[:, :], AF.Square, [tG], [tA])
        bz3 = next_bg()
        MM(bz3.t[:, 0:TB], BONES, tA.t[:, :], True, True, [ones, tA], [bz3])
        ACT(tA.t[:, :], bz3.t[:, 0:TB], AF.Ln, [bz3], [tA], scale=64.0)
        TS("dve", tA.t[:, :], tA.t[:, :], 0.5, math.log(1e-12), ALU.mult, ALU.max, [tA], [tA])
        ACT(tA.t[:, :], tA.t[:, :], AF.Exp, [tA], [tA], scale=-1.0)
        TT("dve", tG.t[:, :], tG.t[:, :], tA.t[:, :], ALU.mult, [tG, tA], [tG])
        TS("dve", tA.t[:, :], tB.t[:, :], prm.t[:, 32 + hp:33 + hp], drv.t[:, hp:hp + 1], ALU.mult, ALU.add,
           [tB, prm, drv], [tA])
        TT("dve", tA.t[:, :], tA.t[:, :], p_k.t[:, :], ALU.mult, [tA, p_k], [tA])
        TT("pool", tB.t[:, :], tG.t[:, :], tB.t[:, :], ALU.mult, [tG, tB], [tB])
        yield
        TT("dve", rh[hp].t[:, :], p_r.t[:, :], tH.t[:, :], ALU.mult, [p_r, tH], [rh[hp]])
        TT("dve", kh[hp].t[:, :], tA.t[:, :], tD.t[:, :], ALU.mult, [tA, tD], [kh[hp]])
        TT("pool", bh[hp].t[:, :], tB.t[:, :], tD.t[:, :], ALU.mult, [tB, tD], [bh[hp]])
        STT(ah[hp].t[:, :], tG.t[:, :], -1.0, tE.t[:, :], ALU.mult, ALU.mult, [tG, tE], [ah[hp]])
        TT("dve", kt[hp].t[:, :], tA.t[:, :], tF.t[:, :], ALU.mult, [tA, tF], [kt[hp]])
        TT("pool", bt[hp].t[:, :], tB.t[:, :], tF.t[:, :], ALU.mult, [tB, tF], [bt[hp]])
        CP("act", vb[hp].t[:, :], p_v.t[:, :], [p_v], [vb[hp]])
        STT(tD.t[:, :], p_r.t[:, :], prm.t[:, 34 + hp:35 + hp], tA.t[:, :], ALU.mult, ALU.mult, [p_r, prm, tA], [tD])
        bz4 = next_bg()
        MM(bz4.t[:, 0:TB], BONES, tD.t[:, :], True, True, [ones, tD], [bz4])
        STT(bonus[hp].t[:, :], bz4.t[:, 0:TB], 64.0, p_v.t[:, :], ALU.mult, ALU.mult, [bz4, p_v], [bonus[hp]])
        yield
        for n, src in enumerate((kt[hp], bt[hp], vb[hp])):
            for ck in range(CPB):
                for h in range(2):
                    ph = slice(h * 64, (h + 1) * 64)
                    c0 = (n * CPB + ck) * 64
                    OP("pe", lambda e, ph=ph, c0=c0, src=src, ck=ck: e.transpose(
                        Bt.t[ph, c0:c0 + 64], src.t[ph, ck * 64:(ck + 1) * 64], ident.t[ph, ph]), [src, ident], [Bt])
        CP("act", tokall[hp].t[:, :, :], Bt.t[:, 0:3 * CPB * 64].rearrange("p (k t) -> p k t", t=64), [Bt],
           [tokall[hp]])
        LIM["on"] = False
        yield

    NQ = CPB // 2
    LW = CPB * 64

    def lcol(ck):
        return slice(ck * 64, (ck + 1) * 64)

    def lmat(dst_bg, lhs_t, rhs_t):
        for h in range(2):
            ph = slice(h * 64, (h + 1) * 64)
            for ck in range(CPB):
                cs = slice(ck * 64, (ck + 1) * 64)
                MM(dst_bg.t[ph, lcol(ck)], lhs_t.t[ph, cs], rhs_t.t[ph, cs], True, True, [lhs_t, rhs_t], [dst_bg])

    def lsq(dst_bg, lhs_t, rhs_t):
        for h in range(2):
            ph = slice(h * 64, (h + 1) * 64)
            for ck in range(CPB):
                MM(dst_bg.t[ph, lcol(ck)], lhs_t.t[ph, lcol(ck)], rhs_t.t[ph, lcol(ck)], True, True,
                   [lhs_t, rhs_t], [dst_bg])

    Tfinal = [None, None]

    def stage_L(hp):
        if stop == "L":
            LIM["on"] = True
            LIM["n"] = rlim
            LIM["c"] = 0
        for (dst, lt, rt, msk) in ((LabT, bh, ah, MS), (Lab, ah, bh, MST), (LakT, kh, ah, MS),
                                   (LrbT, bh, rh, MI), (LrkT, kh, rh, MI)):
            bgx = next_bg()
            lmat(bgx, lt[hp], rt[hp])
            TT("dve", dst[hp].t[:, :], bgx.t[:, 0:LW], msk, ALU.mult, [bgx, cst], [dst[hp]])
            yield
        P, PT, Tt = Lab[hp], LabT[hp], TTp[0][hp]
        TT("pool", Tt.t[:, :], PT.t[:, :], EYE, ALU.add, [PT, cst], [Tt])
        for j in range(5):
            Pn = Pp[j % 2][hp]
            PTn = PTp[j % 2][hp]
            b1 = next_bg()
            lsq(b1, PT, P)
            CP("act", Pn.t[:, :], b1.t[:, 0:LW], [b1], [Pn])
            if j < 4:
                b2 = next_bg()
                lsq(b2, P, PT)
                CP("dve", PTn.t[:, :], b2.t[:, 0:LW], [b2], [PTn])
            yield
            b3 = next_bg()
            Tn = TTp[(j + 1) % 2][hp]
            lsq(b3, Pn, Tt)
            TT("dve", Tn.t[:, :], b3.t[:, 0:LW], Tt.t[:, :], ALU.add, [b3, Tt], [Tn])
            P, PT, Tt = Pn, PTn, Tn
            yield
        Tfinal[hp] = Tt
        LIM["on"] = False

    def stage_C():
        for ck in range(CPB):
            cs = slice(ck * 64, (ck + 1) * 64)
            ys = ck % 3
            ycol = 128 + ys * 128
            for hp in range(2):
                hc = slice(hp * 64, (hp + 1) * 64)
                for h in range(2):
                    ph = slice(h * 64, (h + 1) * 64)
                    MM(B5[ph, hc], ah[hp].t[ph, cs], Hb.t[ph, hc], True, False, [ah[hp], Hb], [b_W])
                    MM(B5[ph, hc], LakT[hp].t[ph, lcol(ck)], vtok[hp].t[ph, ck, :], False, True,
                       [LakT[hp], vtok[hp]], [b_W])
            CP("dve", Wb.t[:, 0:128], B5[:, 0:128], [b_W], [Wb])
            for hp in range(2):
                hc = slice(hp * 64, (hp + 1) * 64)
                Tf = Tfinal[hp]
                for h in range(2):
                    ph = slice(h * 64, (h + 1) * 64)
                    MM(B5[ph, 256 + hp * 64:256 + (hp + 1) * 64], Tf.t[ph, lcol(ck)], Wb.t[ph, hc], True, True,
                       [Tf, Wb], [b_U])
            CP("act", Ub.t[:, 0:128], B5[:, 256:384], [b_U], [Ub])
            yield
            for hp in range(2):
                hc = slice(hp * 64, (hp + 1) * 64)
                for h in range(2):
                    ph = slice(h * 64, (h + 1) * 64)
                    MM(B6[ph, hc], btok[hp].t[ph, ck, :], Ub.t[ph, hc], True, False, [btok[hp], Ub], [b_H])
                    MM(B6[ph, hc], ktok[hp].t[ph, ck, :], vtok[hp].t[ph, ck, :], False, True,
                       [ktok[hp], vtok[hp]], [b_H])
            for hp in range(2):
                hc = slice(hp * 64, (hp + 1) * 64)
                for h in range(2):
                    ph = slice(h * 64, (h + 1) * 64)
                    yo = B6[ph, ycol + hp * 64:ycol + (hp + 1) * 64]
                    MM(yo, Hb.t[ph, hc], rh[hp].t[ph, cs], True, False, [Hb, rh[hp]], [b_Y[ys]])
                    MM(yo, Ub.t[ph, hc], LrbT[hp].t[ph, lcol(ck)], False, False, [Ub, LrbT[hp]], [b_Y[ys]])
                    MM(yo, vtok[hp].t[ph, ck, :], LrkT[hp].t[ph, lcol(ck)], False, True, [vtok[hp], LrkT[hp]],
                       [b_Y[ys]])
            for hp in range(2):
                STT(Hf.t[:, hp * 64:(hp + 1) * 64], Hf.t[:, hp * 64:(hp + 1) * 64],
                    gmC[hp].t[:, ck:ck + 1], B6[:, hp * 64:(hp + 1) * 64], ALU.mult, ALU.add,
                    [Hf, gmC[hp], b_H], [Hf])
            CP("act", Hb.t[:, :], Hf.t[:, :], [Hf], [Hb])
            CP("act", ysb_all.t[:, :, cs], B6[:, ycol:ycol + 128].rearrange("p (h t) -> p h t", t=64), [b_Y[ys]],
               [ysb_all])
            yield

    def ystore(m, blk, src):
        tok0 = blk * TB
        j_ = tok0 // 1024
        r0 = m * 128
        c0 = tok0 % 1024
        DMA(lambda e: e.dma_start(out=ybuf_d[j_][r0:r0 + 128, c0:c0 + TB], in_=src.t[:, :]), [src], [b_ybuf[j_]],
            eng="act")

    def stage_Y(hp, blk, tA=None, tB=None):
        tA = tA or tmps[0]
        tB = tB or tmps[1]
        p_g = pm2[blk % 2][6 + hp]
        bm = next_bg()
        MM(bm.t[:, 0:TB], BONES, ysb[hp].t[:, :], True, True, [ones, ysb[hp]], [bm])
        TT("dve", tA.t[:, :], ysb[hp].t[:, :], bm.t[:, 0:TB], ALU.subtract, [ysb[hp], bm], [tA])
        ACT(tB.t[:, :], tA.t[:, :], AF.Square, [tA], [tB])
        yield
        bv = next_bg()
        MM(bv.t[:, 0:TB], BONES, tB.t[:, :], True, True, [ones, tB], [bv])
        ACT(tB.t[:, :], bv.t[:, 0:TB], AF.Ln, [bv, epsn], [tB], bias=epsn.t[:, 1:2])
        ACT(tB.t[:, :], tB.t[:, :], AF.Exp, [tB], [tB], scale=-0.5)
        TT("dve", tA.t[:, :], tA.t[:, :], tB.t[:, :], ALU.mult, [tA, tB], [tA])
        yield
        TS("dve", tA.t[:, :], tA.t[:, :], prm.t[:, 36 + hp:37 + hp], prm.t[:, 38 + hp:39 + hp], ALU.mult, ALU.add,
           [tA, prm], [tA])
        TT("pool", tA.t[:, :], tA.t[:, :], bonus[hp].t[:, :], ALU.add, [tA, bonus[hp]], [tA])
        SIG(tB.t[:, :], p_g.t[:, :], [p_g], [tB])
        TT("pool", tB.t[:, :], tB.t[:, :], p_g.t[:, :], ALU.mult, [tB, p_g], [tB])
        TT("dve", yout[hp].t[:, :], tA.t[:, :], tB.t[:, :], ALU.mult, [tA, tB], [yout[hp]])
        ystore(hp, blk, yout[hp])
        yield

    def stage_G(j, blk):
        pm = pm2[blk % 2]
        p_q, p_f, p_i = pm[8 + j], pm[10 + j], pm[12 + j]
        SIG(tA.t[:, :], p_f.t[:, :], [p_f], [tA])
        TS("dve", tA.t[:, :], tA.t[:, :], drv.t[:, 4 + j:5 + j], drv.t[:, 2 + j:3 + j], ALU.mult, ALU.add,
           [tA, drv], [tA])
        TS("pool", tB.t[:, :], tA.t[:, :], -1.0, 1.0, ALU.mult, ALU.add, [tA], [tB])
        ACT(tA.t[:, :], tA.t[:, :], AF.Ln, [tA], [tA])
        yield
        OP("dve", lambda e: e.tensor_tensor_scan(out=tC.t[:, :], data0=RST, data1=tA.t[:, :], initial=0.0,
                                                  op0=ALU.mult, op1=ALU.add), [cst, tA], [tC])
        ACT(tH.t[:, :], tC.t[:, :], AF.Exp, [tC], [tH])
        CP("pool", ggC[j].t[:, :], c3(tH.t[:, :])[:, :, 63], [tH], [ggC[j]])
        ACT(tD.t[:, :], tC.t[:, :], AF.Exp, [tC], [tD], scale=-1.0)
        TT("dve", c3(tF.t[:, :]), c3(tC.t[:, :])[:, :, 63:64].to_broadcast([128, CPB, 64]), c3(tC.t[:, :]),
           ALU.subtract, [tC], [tF])
        ACT(tF.t[:, :], tF.t[:, :], AF.Exp, [tF], [tF])
        yield
        TT("dve", qh[j].t[:, :], p_q.t[:, :], tH.t[:, :], ALU.mult, [p_q, tH], [qh[j]])
        TT("pool", gkh[j].t[:, :], tB.t[:, :], tD.t[:, :], ALU.mult, [tB, tD], [gkh[j]])
        TT("dve", gkd[j].t[:, :], tB.t[:, :], tF.t[:, :], ALU.mult, [tB, tF], [gkd[j]])
        CP("act", gvb[j].t[:, :], p_i.t[:, :], [p_i], [gvb[j]])
        yield
        for n, src in enumerate((gkd[j], gvb[j])):
            for jj in range(NT):
                TR(Bt.t[:, (n * NT + jj) * 128:(n * NT + jj + 1) * 128], src.t[:, jj * 128:(jj + 1) * 128], [src], [Bt])
        for par in range(2):
            pp = slice(par * 64, (par + 1) * 64)
            CP("act", gpad[j][par].t[pp, :, :], Bt.t[pp, 0:2 * NT * 128].rearrange("p (k t) -> p k t", t=128), [Bt],
               [gpad[j][par]])
        yield

    def gcol(j, ck):
        return slice((j * NQ + ck // 2) * 64, (j * NQ + ck // 2 + 1) * 64)

    def stage_GA():
        bga = next_bg()
        for j in range(2):
            for ck in range(CPB):
                par = ck % 2
                cs = slice(ck * 64, (ck + 1) * 64)
                MM(bga.t[par * 64:(par + 1) * 64, gcol(j, ck)], gkh[j].t[:, cs], qh[j].t[:, cs], True, True,
                   [gkh[j], qh[j]], [bga])
        TT("dve", ATb.t[:, 0:2 * NQ * 64], bga.t[:, 0:2 * NQ * 64], MI[:, 0:2 * NQ * 64], ALU.mult, [bga, cst], [ATb])
        yield

    def stage_H():
        if stop == "H":
            LIM["on"] = True
            LIM["n"] = rlim
            LIM["c"] = 0
        for ck in range(CPB):
            par, q = ck % 2, ck // 2
            cs = slice(ck * 64, (ck + 1) * 64)
            osl = ck % 2
            ocol = 256 + osl * 128
            for j in range(2):
                gp = gpad[j][par]
                MM(B7[:, ocol + j * 64:ocol + (j + 1) * 64], Sb_.t[:, j * 128:(j + 1) * 128], qh[j].t[:, cs], True, False,
                   [Sb_, qh[j]], [b_O[osl]])
                MM(B7[:, ocol + j * 64:ocol + (j + 1) * 64], gp.t[:, NT + q, :], ATb.t[:, gcol(j, ck)], False, True,
                   [gp, ATb], [b_O[osl]])
            for j in range(2):
                gp = gpad[j][par]
                MM(B7[:, j * 128:(j + 1) * 128], gp.t[:, q, :], gp.t[:, NT + q, :], True, True, [gp], [b_S])
            for j in range(2):
                STT(Sf.t[:, j * 128:(j + 1) * 128], Sf.t[:, j * 128:(j + 1) * 128],
                    ggC[j].t[:, ck:ck + 1], B7[:, j * 128:(j + 1) * 128], ALU.mult, ALU.add,
                    [Sf, ggC[j], b_S], [Sf])
            CP("act", Sb_.t[:, :], Sf.t[:, :], [Sf], [Sb_])
            CP("dve", osb_all.t[:, :, cs], B7[:, ocol:ocol + 128].rearrange("p (h t) -> p h t", t=64), [b_O[osl]],
               [osb_all])
            yield

    def stage_O(j, blk, tA=None, tB=None):
        tA = tA or tmps[0]
        tB = tB or tmps[1]
        p_g = pm2[blk % 2][14 + j]
        ACT(tA.t[:, :], osb[j].t[:, :], AF.Square, [osb[j]], [tA])
        bm = next_bg()
        MM(bm.t[:, 0:TB], AONES, tA.t[:, :], True, True, [ones, tA], [bm])
        ACT(tA.t[:, :], bm.t[:, 0:TB], AF.Ln, [bm, epsn], [tA], bias=epsn.t[:, 0:1])
        ACT(tA.t[:, :], tA.t[:, :], AF.Exp, [tA], [tA], scale=-0.5)
        TT("dve", tA.t[:, :], tA.t[:, :], osb[j].t[:, :], ALU.mult, [tA, osb[j]], [tA])
        yield
        SIG(tB.t[:, :], p_g.t[:, :], [p_g], [tB])
        TT("pool", tB.t[:, :], tB.t[:, :], p_g.t[:, :], ALU.mult, [tB, p_g], [tB])
        STT(yout[2 + j].t[:, :], tA.t[:, :], prm.t[:, 40 + j:41 + j], tB.t[:, :], ALU.mult, ALU.mult,
            [tA, prm, tB], [yout[2 + j]])
        ystore(2 + j, blk, yout[2 + j])
        yield

    def run(gen):
        for _ in gen:
            pass

    def gather(j_):
        S.wait_all("pool", [b_ybuf[j_]])
        DMA(lambda e: e.collective_compute("AllGather", ALU.bypass, replica_groups=[[0, 1, 2, 3], [4, 5, 6, 7]],
                                           ins=[ybuf_d[j_].ap().opt()],
                                           outs=[yall_d[j_ * 2048:(j_ + 1) * 2048, :].opt()]),
            [b_ybuf[j_]], [b_yall], eng="pool", inc=1)

    dbg_n = [0]

    if dbg:
        dbgt = sb("dbgt", [128, TB])

    def dbg_dump(ap, tl, np_=128, w=TB):
        if dbg:
            slot = dbg_n[0]
            dbg_n[0] += 1
            if ap.dtype != F32:
                CP("dve", dbgt.t[0:np_, 0:w], ap, [tl], [dbgt])
                DMA(lambda e: e.dma_start(out=dbg_d[slot, 0:np_, 0:w], in_=dbgt.t[0:np_, 0:w]), [dbgt], [b_dbg])
            else:
                DMA(lambda e: e.dma_start(out=dbg_d[slot, 0:np_, 0:w], in_=ap), [tl], [b_dbg])

    nblk = NB if not dbg else dbg

    def phaseA():
        run(stage_X(0))
        if stop == "X":
            return
        for blk in range(nblk):
            run(stage_P(blk))
            if blk + 1 < nblk:
                run(stage_X(blk + 1))
            if dbg and blk == nblk - 1:
                for i_ in (0, 2, 4, 6, 8, 10):
                    dbg_dump(pm2[blk % 2][i_].t[:, :], pm2[blk % 2][i_])
            if stop == "P":
                continue
            for hp in range(2):
                run(stage_R(hp, blk))
                if stop == "R":
                    continue
                run(stage_L(hp))
            if dbg and blk == nblk - 1:
                dbg_dump(rh[0].t[:, :], rh[0]); dbg_dump(kh[0].t[:, :], kh[0]); dbg_dump(ah[0].t[:, :], ah[0])
                dbg_dump(bh[0].t[:, :], bh[0])
            if stop in ("R", "L"):
                continue
            for j in range(2):
                run(stage_G(j, blk))
            run(stage_GA())
            if stop == "G":
                continue
            run(stage_C())
            if dbg and blk == nblk - 1 and stop == "C":
                dbg_dump(ysb[0].t[:, :], ysb[0])
            if stop == "C":
                continue
            run(stage_H())
            if dbg and blk == nblk - 1:
                dbg_dump(ysb[0].t[:, :], ysb[0])
                dbg_dump(osb[0].t[:, :], osb[0])
            if stop == "H":
                continue
            for hp in range(2):
                run(stage_Y(hp, blk))
            for j in range(2):
                run(stage_O(j, blk))
            if dbg and blk == nblk - 1:
                dbg_dump(yout[0].t[:, :], yout[0])
                dbg_dump(yout[2].t[:, :], yout[2])
            if phaseB and ((blk + 1) * TB) % 1024 == 0:
                gather(((blk + 1) * TB) // 1024 - 1)
    def chain_(*gens):
        for g_ in gens:
            for tag in g_:
                yield tag

    def nofill(gen):
        for _ in gen:
            yield "nofill"

    def rr_(*gens):
        gens = list(gens)
        while gens:
            for g_ in list(gens):
                try:
                    yield next(g_)
                except StopIteration:
                    gens.remove(g_)

    W_DONE = [False]

    def stage_W():
        W_DONE[0] = True
        for kt_ in range(KD):
            r_, m_ = kt_ // 4, kt_ % 4
            row0 = (r_ * 256 + m_ * 128) if m_ < 2 else (1024 + r_ * 256 + (m_ - 2) * 128)
            st = next_stage()
            DMA(lambda e, st=st, row0=row0: e.dma_start(out=st.t[:, 0:D], in_=wout_d[row0:row0 + 128, :]), [], [st])
            yield
            CP("dve" if kt_ % 2 == 0 else "act", wob3[:, kt_, :], st.t[:, 0:D], [st], b_wbf)
            yield

    def n_main_ops():
        return S.count["act"] + S.count["dve"] + S.count["pool"]

    def interleave(main, filler, ops_per_fill):
        fill_alive = True
        credit = 0.0
        last = n_main_ops()
        for tag in main:
            now = n_main_ops()
            if tag != "nofill":
                credit += (now - last) / ops_per_fill
            while fill_alive and credit >= 1.0:
                credit -= 1.0
                try:
                    next(filler)
                except StopIteration:
                    fill_alive = False
                now = n_main_ops()
            last = now
        if fill_alive:
            for _ in filler:
                pass

    def phaseA_pipelined():
        run(stage_X(0))
        run(stage_P(0))
        if NB > 1:
            run(stage_X(1))
        for blk in range(NB):
            fl = []
            if blk + 1 < NB:
                fl.append(stage_P(blk + 1))
            if blk + 2 < NB:
                fl.append(stage_X(blk + 2))
            if blk == NB - 1 and phaseB:
                fl.append(stage_W())
            tail = [stage_Y(0, blk, uT[0], uT[1]), stage_Y(1, blk, uT[2], uT[3])]
            if blk + 1 < NB:
                tail.append(stage_R(0, blk + 1))
            head = [stage_R(0, blk)] if blk == 0 else []
            main = chain_(*head,
                          rr_(stage_L(0), stage_R(1, blk)),
                          rr_(stage_L(1), chain_(stage_G(0, blk), stage_G(1, blk), stage_GA())),
                          rr_(stage_C(),
                              chain_(stage_H(), stage_O(0, blk, tmps[4], tmps[5]), stage_O(1, blk, tmps[6], tmps[7]))),
                          rr_(*tail))
            interleave(main, chain_(*fl), FILL_OPS)
            if phaseB and ((blk + 1) * TB) % 1024 == 0:
                gather(((blk + 1) * TB) // 1024 - 1)

    if dbg or stop:
        phaseA()
    else:
        phaseA_pipelined()

    if not phaseB:
        fin = b_ybuf + ([b_dbg] if dbg else [])
        for e_ in ("sp", "pool", "act"):
            S.wait_all(e_, fin)
        S.emit()
        S.close()
        return nc
    if not W_DONE[0]:
        run(stage_W())
    tmp_b = [t.b for t in tmps]
    pid_cache = {}
    DMA(lambda e: e.dma_start(out=fgt_ap, in_=fg_d), [], tmp_b)
    def xr_dma(tt):
        st = stage[tt % 2]
        DMA(lambda e: e.dma_start(out=st.t[:, 0:D], in_=xres_d[tt * 128:(tt + 1) * 128, :]), [], [st])

    yTq = [hn_raw[:, k_ * KD * 256:(k_ + 1) * KD * 256].rearrange("p (k t) -> p k t", t=256) for k_ in range(2)]
    for qt in range(4):
        kb = qt % 2
        for hh_ in range(2):
            def ld(e, qt=qt, hh_=hh_, kb=kb):
                if "g" not in pid_cache:
                    pid_cache["g"] = e.partition_id() % 4
                g_ = pid_cache["g"]
                src = yall_d[bass.ds(g_ * 2048 + hh_ * 1024, 1024), qt * 256:(qt + 1) * 256]
                return e.dma_start(out=yTq[kb][:, hh_ * 8:(hh_ + 1) * 8, :], in_=src.rearrange("(m p) c -> p m c", p=128))
            DMA(ld, [b_yall], [b_hn[kb]], eng="pool")
        for t2 in range(2):
            tt = qt * 2 + t2
            if tt == 0:
                xr_dma(0)
            if tt + 1 < 8:
                xr_dma(tt + 1)
            st = stage[tt % 2]
            h_ap = hsb_ap[tt % 2]
            h_b = hsb_b[tt % 2]
            for n_ in range(4):
                bp = Bp[n_ % 2]
                for kt_ in range(KD):
                    MM(bp.t[:, :], yTq[kb][:, kt_, t2 * 128:(t2 + 1) * 128], wob3[:, kt_, n_ * 512:(n_ + 1) * 512],
                       kt_ == 0, kt_ == KD - 1, [b_hn[kb]] + b_wbf, [bp])
                TT("dve", h_ap[:, n_ * 512:(n_ + 1) * 512], bp.t[:, :], st.t[:, n_ * 512:(n_ + 1) * 512], ALU.add,
                   [bp, st], h_b)
            ACT(xs[0].t[:, :], h_ap, AF.Square, h_b, [xs[0], ssx], accum=ssx.t[:, 0:1])
            ACT(ssx.t[:, 1:2], ssx.t[:, 0:1], AF.Ln, [ssx, epsn], [ssx], bias=epsn.t[:, 0:1], scale=1.0 / D)
            ACT(rstd_x.t[:, 0:1], ssx.t[:, 1:2], AF.Exp, [ssx], [rstd_x], scale=-0.5)
            STT(h_ap, h_ap, rstd_x.t[:, 0:1], fgt_ap, ALU.mult, ALU.mult, h_b + [rstd_x] + tmp_b, h_b)
            DMA(lambda e, h_ap=h_ap, tt=tt: e.dma_start(out=out_d[tt * 128:(tt + 1) * 128, :], in_=h_ap),
                h_b, [b_out], eng="act")
    fin = [b_out] + ([b_dbg] if dbg else [])
    for e_ in ("sp", "pool", "act"):
        S.wait_all(e_, fin)
    S.emit()
    S.close()
    return nc


def _host_inputs(x, norm_g, w_in, mu, w0, w2, a0, a2, k_k, k_a, r_k, lnx_w, lnx_b, hgrn_norm_g, lb_param, w_out,
                 final_g):
    f = lambda a: np.ascontiguousarray(np.asarray(a), dtype=np.float32)
    x, norm_g, w_in, mu, w0, w2, a0, a2, k_k, k_a, r_k, lnx_w, lnx_b, hgrn_norm_g, lb_param, w_out, final_g = map(
        f, (x, norm_g, w_in, mu, w0, w2, a0, a2, k_k, k_a, r_k, lnx_w, lnx_b, hgrn_norm_g, lb_param, w_out, final_g))
    p = np.arange(128)
    j = np.arange(TB)
    cst = np.zeros((128, 5, TB), np.float32)
    pj, jj = (p % 64)[:, None], (j % 64)[None, :]
    cst[:, 0] = (jj > pj)
    cst[:, 1] = (jj >= pj)
    cst[:, 2] = (jj < pj)
    cst[:, 3] = (jj != 0)
    cst[:, 4] = (jj == pj)
    ones = np.zeros((128, 2, 128), np.float32)
    ones[:, 0] = ((p[:, None] // 64) == (p[None, :] // 64)) / 64.0
    ones[:, 1] = 1.0 / 128.0
    ident = np.eye(128, dtype=np.float32).astype(ml_dtypes.bfloat16)
    fg = np.ascontiguousarray(np.broadcast_to(final_g[None, :], (128, D)))
    wout = w_out[0]
    maps = []
    rk_flat = r_k[0].reshape(-1)
    for c in range(8):
        b, g = c // 4, c % 4
        cols = []
        for base in (0, 1024, 2048, 3072):
            cols.append(np.arange(base + g * 256, base + (g + 1) * 256))
        for base in (0, 1024, 2048, 3072):
            cols.append(np.arange(4288 + base + g * 256, 4288 + base + (g + 1) * 256))
        cols.append(np.arange(4096, 4288))
        cols = np.concatenate(cols)
        w = np.ascontiguousarray(w_in[0][:, cols])
        prm = np.zeros((128, NPRM), np.float32)
        prm[:, 0:16] = norm_g[0].reshape(16, 128).T
        for i, base in enumerate((0, 1024, 2048, 3072)):
            for hp in range(2):
                prm[:, 16 + 2 * i + hp] = mu[0][base + g * 256 + hp * 128 + p]
        prm[0:96, 24] = mu[0][4096:4192]
        prm[0:96, 25] = mu[0][4192:4288]
        for hp in range(2):
            ch = g * 256 + hp * 128 + p
            prm[:, 26 + hp] = w0[0][ch]
            prm[:, 28 + hp] = a0[0][ch]
            prm[:, 30 + hp] = k_k[0][ch]
            prm[:, 32 + hp] = k_a[0][ch]
            prm[:, 34 + hp] = rk_flat[ch]
            prm[:, 36 + hp] = lnx_w[0][ch]
            prm[:, 38 + hp] = lnx_b[0][ch]
            prm[:, 40 + hp] = hgrn_norm_g[0][ch]
            prm[:, 42 + hp] = lb_param[0][ch]
            prm[:, 44 + hp] = lb_param[1][ch]
        maps.append({
            "x": x[b], "xres": np.ascontiguousarray(x[b, g * 1024:(g + 1) * 1024]), "w": w, "prm": prm,
            "w2s": np.ascontiguousarray(w2[0][:, g * 256:(g + 1) * 256]),
            "a2s": np.ascontiguousarray(a2[0][:, g * 256:(g + 1) * 256]),
            "wout": wout, "fg": fg, "ident": ident, "cst": cst, "ones": ones,
        })
    return maps


_CACHE = {}


def kernel(**inputs):
    maps = _host_inputs(**inputs)
    nc = _get_program()
    res = run_bass_kernel_spmd(nc, maps, core_ids=list(range(8)))
    out = np.zeros((2, T_SEQ, D), np.float32)
    for c in range(8):
        b, g = c // 4, c % 4
        out[b, g * 1024:(g + 1) * 1024, :] = res.results[c]["out"]
    return out


def _get_program():
    if "nc" not in _CACHE:
        _CACHE["nc"] = build_program()
    return _CACHE["nc"]
```
